# Optimizing a Trainium2 kernel written in Bass

```python
import jax, jax.numpy as jnp
from jax import lax
import numpy as np

D_MODEL = 1024
BATCH = 16
SEQ = 2048
DEPTH = 4
DEC_BATCH = 8
DEC_SEQ = 4096
PAST_LEN = 128

GRID_W = 64
RMS_EPS = 1e-6
F_FLOOR = 1e-30
MASK_VALUE = -1e30
D_FF = 2816
HG_HEADS = 4
HG_DK = 128
HG_DV = 128
HG_KWIDTH = HG_HEADS * HG_DK
HG_WIDTH = HG_HEADS * HG_DV
HG_CHUNK = 64
CV_GROUPS = 4
CV_GROUP_DIM = 128
CV_WIDTH = CV_GROUPS * CV_GROUP_DIM
CONV_K = 3
NA_HEADS = 8
NA_DH = 64
NA_WIDTH = NA_HEADS * NA_DH
NA_KH = 8
NA_KW = 16
NA_QB = 16
NA_NQB = GRID_W // NA_QB
NA_KBW = NA_QB + NA_KW
NA_SCALE = NA_DH ** -0.5
IN_WIDTHS = (HG_KWIDTH, HG_WIDTH, HG_KWIDTH, HG_KWIDTH, HG_WIDTH,
             CV_WIDTH, CV_WIDTH, CV_WIDTH,
             NA_WIDTH, NA_WIDTH, NA_WIDTH,
             D_MODEL, D_MODEL, D_MODEL)
N_IN = sum(IN_WIDTHS)

kernel_name = 'hybrid_bidir_hgrn2_conv_natten_encoder'


def rmsnorm(x, gain):
    x32 = x.astype(jnp.float32)
    y = x32 * lax.rsqrt(jnp.mean(x32 * x32, axis=-1, keepdims=True) + RMS_EPS)
    return (y * gain.astype(jnp.float32)).astype(x.dtype)


def swiglu_ffn(x, norm_g, w_gu, w_down):
    h = rmsnorm(x, norm_g) @ w_gu
    a, b = jnp.split(h, 2, axis=-1)
    return (jax.nn.silu(a) * b) @ w_down


def hgrn2_lower_bounds(lb_logits):
    p = jax.nn.softmax(lb_logits.astype(jnp.float32), axis=1)
    return jnp.cumsum(p, axis=1) - p[:, :1]


def hgrn2_gates(z, lb):
    z32 = z.astype(jnp.float32)
    f = lb + (1.0 - lb) * jax.nn.sigmoid(z32)
    logf = jnp.log(jnp.maximum(f, F_FLOOR))
    k = (1.0 - lb) * jax.nn.sigmoid(-z32)
    return k, logf


def hgrn2_chunk_scan(q, k, v, logf):
    B, T, H, DK = q.shape
    DV = v.shape[-1]
    C = HG_CHUNK
    N = T // C

    def to_chunks(a):
        return a.reshape(B, N, C, H, a.shape[-1]).transpose(1, 0, 3, 2, 4)

    tri = jnp.asarray(np.tril(np.ones((C, C), dtype=bool)))[:, :, None]

    def step(S, inp):
        qc, kc, vc, lc = inp
        b = jnp.cumsum(lc, axis=2)
        o_inter = jnp.einsum('bhtk,bhkv->bhtv', qc * jnp.exp(b), S)
        diff = b[:, :, :, None, :] - b[:, :, None, :, :]
        decay = jnp.where(tri, jnp.exp(jnp.where(tri, diff, 0.0)), 0.0)
        A = jnp.einsum('bhtk,bhtsk->bhts', qc, decay * kc[:, :, None, :, :])
        o_intra = jnp.einsum('bhts,bhsv->bhtv', A, vc)
        b_end = b[:, :, -1, :]
        S = jnp.exp(b_end)[..., None] * S + jnp.einsum(
            'bhsk,bhsv->bhkv', kc * jnp.exp(b_end[:, :, None, :] - b), vc)
        return S, o_inter + o_intra

    S0 = jnp.zeros((B, H, DK, DV), jnp.float32)
    _, o = lax.scan(step, S0, (to_chunks(q), to_chunks(k), to_chunks(v), to_chunks(logf)))
    return o.transpose(1, 0, 3, 2, 4).reshape(B, T, H, DV)


def neighbourhood_attention(q, k, v, rpb):
    B, T, H, dh = q.shape
    rows = T // GRID_W
    kh = min(NA_KH, rows)
    qg = q.reshape(B, rows, NA_NQB, NA_QB, H, dh).transpose(1, 0, 2, 3, 4, 5)
    kg = k.reshape(B, rows, GRID_W, H, dh)
    vg = v.reshape(B, rows, GRID_W, H, dh)
    qcol = np.arange(GRID_W).reshape(NA_NQB, NA_QB)
    kstart = np.clip(np.arange(NA_NQB) * NA_QB - NA_KW // 2, 0, GRID_W - NA_KBW)
    kcol = kstart[:, None] + np.arange(NA_KBW)
    cs = np.clip(qcol - NA_KW // 2, 0, GRID_W - NA_KW)
    col_ok = jnp.asarray((kcol[:, None, :] >= cs[..., None]) & (kcol[:, None, :] < cs[..., None] + NA_KW))
    dc = np.clip(kcol[:, None, :] - qcol[..., None] + NA_KW - 1, 0, 2 * NA_KW - 2)
    rpb_col = rpb.astype(jnp.float32)[:, :, dc]

    def one_row(args):
        r, q_row = args
        rs = jnp.clip(r - kh // 2, 0, rows - kh)
        k_win = lax.dynamic_slice_in_dim(kg, rs, kh, axis=1)[:, :, kcol]
        v_win = lax.dynamic_slice_in_dim(vg, rs, kh, axis=1)[:, :, kcol]
        s = jnp.einsum('bjqhd,bajkhd->bhjqak', q_row, k_win).astype(jnp.float32) * NA_SCALE
        dr = rs + jnp.arange(kh) - r + NA_KH - 1
        bias = jnp.take(rpb_col, dr, axis=1).transpose(0, 2, 3, 1, 4)
        s = jnp.where(col_ok[:, :, None, :], s + bias, MASK_VALUE)
        p = jax.nn.softmax(s, axis=(-2, -1))
        o = jnp.einsum('bhjqak,bajkhd->bjqhd', p.astype(v_win.dtype), v_win)
        return o.reshape(B, GRID_W, H * dh)

    out = lax.map(one_row, (jnp.arange(rows), qg))
    return out.transpose(1, 0, 2, 3).reshape(B, T, H * dh)


def hybrid_mixer(u, layer_lb, w_in, hg_out_norm, w_hg_out, conv_w, conv_b, w_cv_out,
                 na_rpb, w_na_out, w_out):
    B, T, _ = u.shape
    proj = u @ w_in
    splits = np.cumsum(IN_WIDTHS)[:-1].tolist()
    (hq, hi, hzf, hzb, hg, ca, cb, cc, nq, nk, nv, g_hg, g_cv, g_na) = jnp.split(proj, splits, axis=-1)

    q = hq.reshape(B, T, HG_HEADS, HG_DK).astype(jnp.float32)
    v = hi.reshape(B, T, HG_HEADS, HG_DV).astype(jnp.float32)
    k_f, logf_f = hgrn2_gates(hzf.reshape(B, T, HG_HEADS, HG_DK), layer_lb[0].reshape(HG_HEADS, HG_DK))
    k_b, logf_b = hgrn2_gates(hzb.reshape(B, T, HG_HEADS, HG_DK), layer_lb[1].reshape(HG_HEADS, HG_DK))
    flip = lambda a: jnp.flip(a, axis=1)
    o = hgrn2_chunk_scan(q, k_f, v, logf_f) + flip(hgrn2_chunk_scan(flip(q), flip(k_b), flip(v), flip(logf_b)))
    o = o * lax.rsqrt(jnp.mean(o * o, axis=-1, keepdims=True) + RMS_EPS) \
        * hg_out_norm.astype(jnp.float32).reshape(HG_HEADS, HG_DV)
    o = o.reshape(B, T, HG_WIDTH) * jax.nn.silu(hg.astype(jnp.float32))
    y_hg = o.astype(u.dtype) @ w_hg_out

    z = cc * ca
    zc = lax.conv_general_dilated(z, conv_w[:, None, :].astype(z.dtype), window_strides=(1,),
                                  padding=((CONV_K // 2, CONV_K // 2),),
                                  dimension_numbers=('NWC', 'WIO', 'NWC'),
                                  feature_group_count=CV_WIDTH)
    y_cv = (cb * (zc + conv_b)) @ w_cv_out

    o_na = neighbourhood_attention(nq.reshape(B, T, NA_HEADS, NA_DH), nk.reshape(B, T, NA_HEADS, NA_DH),
                                   nv.reshape(B, T, NA_HEADS, NA_DH), na_rpb)
    y_na = o_na @ w_na_out

    m = jax.nn.sigmoid(g_hg) * y_hg + jax.nn.sigmoid(g_cv) * y_cv + jax.nn.sigmoid(g_na) * y_na
    return m @ w_out


def run_trunk(x, ffn1_norm, ffn1_w_gu, ffn1_w_down, mix_norm, w_in, hg_lb_logits, hg_out_norm,
              w_hg_out, conv_w, conv_b, w_cv_out, na_rpb, w_na_out, w_out,
              ffn2_norm, ffn2_w_gu, ffn2_w_down, final_norm):
    lb_all = hgrn2_lower_bounds(hg_lb_logits)
    for l in range(DEPTH):
        x = x + 0.5 * swiglu_ffn(x, ffn1_norm[l], ffn1_w_gu[l], ffn1_w_down[l])
        x = x + hybrid_mixer(rmsnorm(x, mix_norm[l]), lb_all[:, l], w_in[l], hg_out_norm[l], w_hg_out[l],
                             conv_w[l], conv_b[l], w_cv_out[l], na_rpb[l], w_na_out[l], w_out[l])
        x = x + 0.5 * swiglu_ffn(x, ffn2_norm[l], ffn2_w_gu[l], ffn2_w_down[l])
    return rmsnorm(x, final_norm)


def setup_inputs(seed: int = 0) -> dict:
    key = jax.random.key(seed)
    ks = jax.random.split(key, 24)
    nrm = lambda k, shape, scale: jax.random.normal(k, shape, jnp.float32) * scale
    L, D = DEPTH, D_MODEL
    return {
        'x_prompt': nrm(ks[0], (BATCH, SEQ, D), 1.0),
        'x_sample': nrm(ks[1], (DEC_BATCH, DEC_SEQ, D), 1.0),
        'ffn1_norm': 1.0 + nrm(ks[2], (L, D), 0.02),
        'ffn1_w_gu': nrm(ks[3], (L, D, 2 * D_FF), D ** -0.5),
        'ffn1_w_down': nrm(ks[4], (L, D_FF, D), D_FF ** -0.5),
        'mix_norm': 1.0 + nrm(ks[5], (L, D), 0.02),
        'w_in': nrm(ks[6], (L, D, N_IN), D ** -0.5),
        'hg_lb_logits': 1.0 + nrm(ks[7], (2, L, HG_KWIDTH), 0.1),
        'hg_out_norm': 1.0 + nrm(ks[8], (L, HG_WIDTH), 0.02),
        'w_hg_out': nrm(ks[9], (L, HG_WIDTH, D), HG_WIDTH ** -0.5),
        'conv_w': nrm(ks[10], (L, CONV_K, CV_WIDTH), CONV_K ** -0.5),
        'conv_b': nrm(ks[11], (L, CV_WIDTH), 0.01),
        'w_cv_out': nrm(ks[12], (L, CV_WIDTH, D), CV_WIDTH ** -0.5),
        'na_rpb': nrm(ks[13], (L, NA_HEADS, 2 * NA_KH - 1, 2 * NA_KW - 1), 0.1),
        'w_na_out': nrm(ks[14], (L, NA_WIDTH, D), NA_WIDTH ** -0.5),
        'w_out': nrm(ks[15], (L, D, D), D ** -0.5),
        'ffn2_norm': 1.0 + nrm(ks[16], (L, D), 0.02),
        'ffn2_w_gu': nrm(ks[17], (L, D, 2 * D_FF), D ** -0.5),
        'ffn2_w_down': nrm(ks[18], (L, D_FF, D), D_FF ** -0.5),
        'final_norm': 1.0 + nrm(ks[19], (D,), 0.02),
    }


def reference(x_prompt, x_sample, ffn1_norm, ffn1_w_gu, ffn1_w_down, mix_norm, w_in, hg_lb_logits,
              hg_out_norm, w_hg_out, conv_w, conv_b, w_cv_out, na_rpb, w_na_out, w_out,
              ffn2_norm, ffn2_w_gu, ffn2_w_down, final_norm):
    y_prompt = run_trunk(x_prompt, ffn1_norm, ffn1_w_gu, ffn1_w_down, mix_norm, w_in, hg_lb_logits,
                         hg_out_norm, w_hg_out, conv_w, conv_b, w_cv_out, na_rpb, w_na_out, w_out,
                         ffn2_norm, ffn2_w_gu, ffn2_w_down, final_norm)
    y_sample = run_trunk(x_sample, ffn1_norm, ffn1_w_gu, ffn1_w_down, mix_norm, w_in, hg_lb_logits,
                         hg_out_norm, w_hg_out, conv_w, conv_b, w_cv_out, na_rpb, w_na_out, w_out,
                         ffn2_norm, ffn2_w_gu, ffn2_w_down, final_norm)
    return (y_prompt, y_sample)
```

```python
import numpy as np
import concourse.bass as bass
import concourse.mybir as mybir
from concourse.bass_utils import run_bass_kernel_spmd
from contextlib import ExitStack

F32 = mybir.dt.float32
BF16 = mybir.dt.bfloat16
AF = mybir.ActivationFunctionType
ALU = mybir.AluOpType

D = 1024
DFF = 2816
NIN = 8704
EPS = 1e-6
TS = 512


class Buf:
    __slots__ = ("w", "r")

    def __init__(self):
        self.w = {}
        self.r = {}


class Eng:
    def __init__(self, name, h, sem, is_pe=False):
        self.name = name
        self.h = h
        self.sem = sem
        self.cnt = 0
        self.seen = {}
        self.is_pe = is_pe


class FW:
    def __init__(self, nc, es, n_dsem=40):
        self.nc = nc
        mk = lambda n: es.enter_context(nc.semaphore(n))
        self.pe = Eng("pe", nc.tensor, mk("s_pe"), True)
        self.act = Eng("act", nc.scalar, mk("s_act"))
        self.dve = Eng("dve", nc.vector, mk("s_dve"))
        self.pool = Eng("pool", nc.gpsimd, mk("s_pool"))
        self.sp = Eng("sp", nc.sync, mk("s_sp"))
        self.engs = [self.pe, self.act, self.dve, self.pool, self.sp]
        self.dsems = [mk("s_d%d" % i) for i in range(n_dsem)]
        self.wsems = [mk("s_w%d" % i) for i in range(16)]
        self.wnext = 0
        self.dnext = 0
        self.dval = {}
        self.semvals = {}

    def new_dsem(self):
        s = self.dsems[self.dnext % len(self.dsems)]
        self.dnext += 1
        return s

    def new_wsem(self):
        s = self.wsems[self.wnext % len(self.wsems)]
        self.wnext += 1
        return s

    def _waits(self, E, reads, writes):
        waits = {}
        for b in reads:
            for s, v in b.w.items():
                if waits.get(s, 0) < v:
                    waits[s] = v
        for b in writes:
            for d in (b.w, b.r):
                for s, v in d.items():
                    if waits.get(s, 0) < v:
                        waits[s] = v
        for s, v in waits.items():
            if E.seen.get(s, 0) >= v:
                continue
            if E.is_pe and s is E.sem:
                continue
            E.seen[s] = v
            E.h.wait_ge(s, v)

    def op(self, E, fn, reads=(), writes=()):
        self._waits(E, reads, writes)
        ins = fn()
        E.cnt += 1
        ins.then_inc(E.sem, 1)
        self.semvals[E.sem] = E.cnt
        for b in reads:
            b.r[E.sem] = E.cnt
        for b in writes:
            b.w = {E.sem: E.cnt}
            b.r = {}

    def dma(self, Q, sem, out, in_, reads=(), writes=()):
        self.dma_group(Q, sem, [(out, in_)], reads, writes)

    def dma_group(self, Q, sem, pairs, reads=(), writes=(), slow=False):
        self._waits(Q, reads, writes)
        v = self.dval.get(sem, 0)
        for (o, i) in pairs:
            if slow:
                Q.h.dma_start(out=o, in_=i, allow_slow_non_contiguous=True).then_inc(sem, 16)
            else:
                Q.h.dma_start(out=o, in_=i).then_inc(sem, 16)
            v += 16
        self.dval[sem] = v
        self.semvals[sem] = v
        for b in reads:
            b.r[sem] = v
        for b in writes:
            b.w = {sem: v}
            b.r = {}

    def barrier(self):
        for E in self.engs:
            for s, v in self.semvals.items():
                if E.seen.get(s, 0) >= v:
                    continue
                E.seen[s] = v
                if E.is_pe and s is E.sem:
                    continue
                E.h.wait_ge(s, v)


_UID = [0]


def uniq(name):
    _UID[0] += 1
    return "%s_%d" % (name, _UID[0])


class Ring:
    def __init__(self, fw, es, name, n, shape, dt, psum=False, dsem=False):
        nc = fw.nc
        self.t = []
        self.b = []
        self.s = []
        for i in range(n):
            if psum:
                self.t.append(es.enter_context(nc.psum_tensor(uniq(name), shape, dt)))
            else:
                self.t.append(es.enter_context(nc.sbuf_tensor(uniq(name), shape, dt)))
            self.b.append(Buf())
            self.s.append(fw.new_dsem() if dsem else None)
        self.n = n
        self.i = -1

    def next(self):
        self.i += 1
        k = self.i % self.n
        return self.t[k], self.b[k], self.s[k]


def build(SEQS, DEPTH, debug=False):
    NT = sum(SEQS)
    NTI = NT // TS
    assert all(s % 1024 == 0 for s in SEQS)
    seq_start_tiles = set()
    seq_end_tiles = set()
    off = 0
    seq_info = []
    for s in SEQS:
        seq_start_tiles.add(off // TS)
        seq_end_tiles.add((off + s) // TS - 1)
        seq_info.append((off, s))
        off += s

    nc = bass.Bass("TRN2", target_bir_lowering=False)
    I = lambda n, s, d=F32: nc.dram_tensor(n, list(s), d, kind="ExternalInput").ap()
    L = DEPTH
    x_in = I("x", [NT, D])
    W_gu = [I("ffn1_w_gu", [L, D, 2 * DFF]), I("ffn2_w_gu", [L, D, 2 * DFF])]
    W_dn = [I("ffn1_w_down", [L, DFF, D]), I("ffn2_w_down", [L, DFF, D])]
    W_in = I("w_in", [L, D, NIN])
    W_mo = [I("w_hg_out", [L, 512, D]), I("w_cv_out", [L, 512, D]), I("w_na_out", [L, 512, D])]
    W_out = I("w_out", [L, D, D])
    gains_in = I("gains", [128, 3, L, 8])
    gfin_in = I("gfin", [128, D])
    lbl_in = I("lbl", [128, 2, L, 4])
    gn_in = I("gn", [128, L, 4])
    cw_in = I("cw", [128, L, 3, 4])
    cb_in = I("cbias", [128, L, 4])
    b2g_in = I("b2g", [L, 128, 8, 9, 128])
    neg_in = I("negm", [128, 5, 5, 128])
    ident_in = I("ident", [128, 128])
    maskf_in = I("maskf", [128, 128])
    maskb_in = I("maskb", [128, 128])
    rm_in = I("rmask", [128, TS])
    y_out = nc.dram_tensor("y", [NT, D], F32, kind="ExternalOutput").ap()

    skind = "ExternalOutput" if debug else "Internal"
    S = lambda n, s, d=BF16: nc.dram_tensor(n, list(s), d, kind=skind).ap()
    XS = S("xs", [NT, D], F32)
    HT = S("ht", [22, 128, NT])
    HQ = S("hq", [2, 2, 4, 128, NT])
    KB = S("kb", [2, 4, NT, 128])
    HV = S("hv", [NT, 512])
    DCH = S("dch", [2, 128, 4, NT // 64], F32)
    HGT = S("hgt", [4, 128, NT])
    ZT = S("zt", [4, 128, NT])
    CBT = S("cbt", [4, 128, NT])
    NQ = S("nq", [4, 128, NT])
    NK = S("nk", [4, 128, NT])
    NV = S("nv", [NT, 512])
    OF = S("of", [4, 128, NT], F32)
    OHG = S("ohg", [4, 128, NT])
    ONA = S("ona", [4, 128, NT])

    with ExitStack() as ges:
        fw = FW(nc, ges)
        PE, ACT, DVE, POOL, SP = fw.pe, fw.act, fw.dve, fw.pool, fw.sp
        V = nc.vector
        A = nc.scalar
        T = nc.tensor
        G = nc.gpsimd
        gsb = lambda n, s, d=F32: ges.enter_context(nc.sbuf_tensor(uniq(n), list(s), d))
        cst = Buf()
        ident_f = gsb("ident_f", [128, 128])
        idb = gsb("idb", [128, 128], BF16)
        ones_b = gsb("ones_b", [128, 128], BF16)
        zeros_b = gsb("zeros_b", [128, 128], BF16)
        maskf = gsb("maskf", [128, 128])
        maskb = gsb("maskb", [128, 128])
        rmk = gsb("rmk", [128, TS])
        gains = gsb("gains", [128, 3, L, 8])
        gn = gsb("gn", [128, L, 4])
        cw = gsb("cw", [128, L, 3, 4])
        cbias = gsb("cbias", [128, L, 4])
        lbl = gsb("lbl", [128, 2, L, 4])
        lbe = gsb("lbe", [128, 2, L, 4])
        lbs = gsb("lbs", [128, 2, 4])
        lb = gsb("lb", [128, 2, L, 4])
        oml = gsb("oml", [128, 2, L, 4])
        csem = fw.new_dsem()
        fw.dma_group(SP, csem, [(ident_f[:], ident_in), (maskf[:], maskf_in), (maskb[:], maskb_in), (rmk[:], rm_in),
                                (gains[:], gains_in), (gn[:], gn_in), (cw[:], cw_in), (cbias[:], cb_in), (lbl[:], lbl_in)],
                     writes=[cst])
        c2 = Buf()
        fw.op(DVE, lambda: V.tensor_copy(out=idb[:], in_=ident_f[:]), [cst], [c2])
        fw.op(DVE, lambda: V.memset(ones_b[:], 1.0), [], [c2])
        fw.op(DVE, lambda: V.memset(zeros_b[:], 0.0), [], [c2])
        fw.op(ACT, lambda: A.activation(out=lbe[:], in_=lbl[:], func=AF.Exp), [cst], [c2])
        fw.op(DVE, lambda: V.memset(lb[:], 0.0), [], [c2])
        fw.op(DVE, lambda: V.tensor_copy(out=lbs[:], in_=lbe[:, :, 0, :]), [c2], [c2])
        for l in range(1, L):
            fw.op(DVE, lambda l=l: V.tensor_tensor(out=lbs[:], in0=lbs[:], in1=lbe[:, :, l, :], op=ALU.add), [c2], [c2])
        fw.op(DVE, lambda: V.reciprocal(out=lbs[:], in_=lbs[:]), [c2], [c2])
        for l in range(1, L):
            fw.op(DVE, lambda l=l: V.tensor_tensor(out=lbe[:, :, l, :], in0=lbe[:, :, l, :], in1=lbs[:], op=ALU.mult), [c2], [c2])
            fw.op(DVE, lambda l=l: V.tensor_tensor(out=lb[:, :, l, :], in0=lb[:, :, l - 1, :], in1=lbe[:, :, l, :], op=ALU.add), [c2], [c2])
        fw.op(DVE, lambda: V.tensor_scalar(out=oml[:], in0=lb[:], scalar1=-1.0, scalar2=1.0, op0=ALU.mult, op1=ALU.add), [c2], [c2])
        fw.barrier()

        def load_weights(es, name, kch, ncols, src_fn, order, wt=None):
            if wt is None:
                wt = es.enter_context(nc.sbuf_tensor(uniq(name), [128, kch, ncols], BF16))
            bufs = {}
            groups = [order[:1], order[1:4], order[4:]]
            for g in groups:
                if not g:
                    continue
                sem = fw.new_wsem()
                bl = []
                pairs = []
                for blk in g:
                    c0 = blk * 512
                    c1 = min(ncols, c0 + 512)
                    b = Buf()
                    bufs[blk] = b
                    bl.append(b)
                    pairs.append((wt[:, :, c0:c1], src_fn(c0, c1).rearrange("(k p) n -> p k n", p=128)))
                fw.dma_group(POOL, sem, pairs, writes=bl)
            return wt, bufs

        def pro_load(env, t, xsrc):
            xt, bx, sx = env["xr"].next()
            fw.dma(SP, sx, xt[:], xsrc[t * TS:(t + 1) * TS, :].rearrange("(s p) d -> p s d", p=128), writes=[bx])
            return xt, bx, sx

        def prologue(env, loaded, gain_ap):
            xt, bx, sx = loaded
            ss, bss, _ = env["ssr"].next()
            fw.op(DVE, lambda: V.memset(ss[:], 0.0), [], [bss])
            xns = []
            for s in range(4):
                xn, bxn, _ = env["xnr"].next()
                xns.append((xn, bxn))
            for s in range(4):
                fw.op(ACT, lambda s=s: A.activation(out=xns[s][0][:], in_=xt[:, s, :], func=AF.Square, accum_out=ss[:, s:s + 1]),
                      [bx], [xns[s][1], bss])
            fw.op(ACT, lambda: A.activation(out=ss[:, 4:8], in_=ss[:, 0:4], func=AF.Ln, scale=1.0 / D, bias=EPS), [bss], [bss])
            fw.op(ACT, lambda: A.activation(out=ss[:, 8:12], in_=ss[:, 4:8], func=AF.Exp, scale=-0.5), [bss], [bss])
            xnT, _, _ = env["xntr"].next()
            bxt = env["xntb"][env["xntr"].i % 2]
            for s in range(4):
                xn, bxn = xns[s]
                fw.op(DVE, lambda s=s, xn=xn: V.tensor_scalar(out=xn[:], in0=xt[:, s, :], scalar1=ss[:, 8 + s:9 + s], scalar2=None,
                                                        op0=ALU.mult), [bx, bss], [bxn])
            for c in range(8):
                pt, bpt, _ = env["ptr"].next()

                def ftr(c=c, pt=pt):
                    for s in range(4):
                        ins = T.transpose(pt[:, s * 128:(s + 1) * 128], xns[s][0][:, c * 128:(c + 1) * 128], idb[:])
                    return ins
                fw.op(PE, ftr, [b for (_, b) in xns], [bpt])
                if c % 2 == 0:
                    fw.op(ACT, lambda c=c, pt=pt: A.activation(out=xnT[:, c, :], in_=pt[:], func=AF.Copy, scale=gain_ap[:, c:c + 1]),
                          [bpt], [bxt[0]])
                else:
                    fw.op(DVE, lambda c=c, pt=pt: V.tensor_scalar(out=xnT[:, c, :], in0=pt[:], scalar1=gain_ap[:, c:c + 1], scalar2=None,
                                                                 op0=ALU.mult), [bpt], [bxt[1]])
            return xt, bx, sx, xnT, bxt

        def pipelined_pro(tiles, env, xsrc, gain, body_fn):
            lds = {0: pro_load(env, tiles[0], xsrc)}
            if len(tiles) > 1:
                lds[1] = pro_load(env, tiles[1], xsrc)
            pros = {0: prologue(env, lds[0], gain)}
            for i, t in enumerate(tiles):
                def hook(i=i):
                    if i + 1 < len(tiles) and (i + 1) not in pros:
                        pros[i + 1] = prologue(env, lds[i + 1], gain)
                body_fn(t, pros[i], hook)
                hook()
                if i + 2 < len(tiles):
                    lds[i + 2] = pro_load(env, tiles[i + 2], xsrc)

        def pipelined(tiles, load_fn, body_fn):
            nxt = load_fn(tiles[0])
            for i, t in enumerate(tiles):
                cur = nxt
                if i + 1 < len(tiles):
                    nxt = load_fn(tiles[i + 1])
                body_fn(t, cur)

        def prologue_env(es, xring=2, xnring=4):
            env = {}
            env["xr"] = Ring(fw, es, "xt", xring, [128, 4, D], F32, dsem=True)
            env["ssr"] = Ring(fw, es, "ss", 2, [128, 12], F32)
            env["xnr"] = Ring(fw, es, "xn", xnring, [128, D], BF16)
            env["xntr"] = Ring(fw, es, "xnT", 2, [128, 8, TS], BF16)
            env["xntb"] = [[Buf(), Buf()], [Buf(), Buf()]]
            env["ptr"] = Ring(fw, es, "ptr", 2, [128, TS], BF16, psum=True)
            return env

        def ffn_gu(l, which, xsrc, outer):
            with ExitStack() as es:
                order = [0, 5, 6, 1, 7, 2, 8, 3, 9, 4, 10]
                wt, wb = load_weights(es, "wgu", 8, 2 * DFF, lambda c0, c1: W_gu[which][l, :, c0:c1], order)
                outer["wdn"] = load_weights(None, "wdn", 22, D, lambda c0, c1: W_dn[which][l, :, c0:c1], [0, 1], wt=outer["wdn_t"])
                env = prologue_env(es)
                pa = Ring(fw, es, "pa", 3, [128, TS], F32, psum=True)
                pb = Ring(fw, es, "pb", 3, [128, TS], F32, psum=True)
                sar = Ring(fw, es, "sa", 3, [128, TS], F32)
                hr = Ring(fw, es, "hst", 4, [128, TS], BF16, dsem=True)
                gain = gains[:, 0 if which == 0 else 2, l, :]
                def body(t, pro, hook):
                    xt, bx, sx, xnT, bxt = pro
                    for j in range(22):
                        if j == 17:
                            hook()
                        A_, bA, _ = pa.next()
                        B_, bB, _ = pb.next()
                        ca = j * 128
                        cb = DFF + j * 128

                        def fmm(P_, c0):
                            for k in range(8):
                                ins = T.matmul(P_[:], lhsT=wt[:, k, c0:c0 + 128], rhs=xnT[:, k, :], start=(k == 0), stop=(k == 7))
                            return ins
                        fw.op(PE, lambda: fmm(A_, ca), bxt + [wb[ca // 512]], [bA])
                        fw.op(PE, lambda: fmm(B_, cb), bxt + [wb[cb // 512]], [bB])
                        sa, bsa, _ = sar.next()
                        fw.op(ACT, lambda: A.activation(out=sa[:], in_=A_[:], func=AF.Silu), [bA], [bsa])
                        h, bh, sh = hr.next()
                        fw.op(DVE, lambda: V.tensor_tensor(out=h[:], in0=B_[:], in1=sa[:], op=ALU.mult), [bB, bsa], [bh])
                        fw.dma(SP, sh, HT[j, :, t * TS:(t + 1) * TS], h[:], reads=[bh])
                pipelined_pro(list(range(NTI)), env, xsrc, gain, body)
                fw.barrier()

        def ffn_down(l, which, xsrc, xdst, outer):
            with ExitStack() as es:
                wt, wb = outer["wdn"]
                xr = Ring(fw, es, "xt", 2, [128, 4, D], F32, dsem=True)
                hr = Ring(fw, es, "hT", 2, [128, 22, TS], BF16, dsem=True)
                po = Ring(fw, es, "po", 3, [128, TS], F32, psum=True)
                def load(t):
                    xt, bx, sx = xr.next()
                    fw.dma(SP, sx, xt[:], xsrc[t * TS:(t + 1) * TS, :].rearrange("(s p) d -> p s d", p=128), writes=[bx])
                    hT, bh, sh = hr.next()
                    fw.dma(SP, sh, hT[:], HT[:, :, t * TS:(t + 1) * TS].rearrange("j p n -> p j n"), writes=[bh])
                    return xt, bx, sx, hT, bh, sh

                def body(t, loaded):
                    xt, bx, sx, hT, bh, sh = loaded
                    for s in range(4):
                        for hf in range(2):
                            P_, bP, _ = po.next()

                            def fmm():
                                for j in range(22):
                                    ins = T.matmul(P_[:], lhsT=hT[:, j, s * 128:(s + 1) * 128], rhs=wt[:, j, hf * 512:(hf + 1) * 512],
                                                   start=(j == 0), stop=(j == 21))
                                return ins
                            fw.op(PE, fmm, [bh, wb[hf]], [bP])
                            fw.op(DVE, lambda: V.scalar_tensor_tensor(out=xt[:, s, hf * 512:(hf + 1) * 512], in0=P_[:], scalar=0.5,
                                                                      in1=xt[:, s, hf * 512:(hf + 1) * 512], op0=ALU.mult, op1=ALU.add),
                                  [bP, bx], [bx])
                    fw.dma(SP, sx, xdst[t * TS:(t + 1) * TS, :].rearrange("(s p) d -> p s d", p=128), xt[:], reads=[bx])
                pipelined(list(range(NTI)), load, body)
                fw.barrier()

        def mixer_a(l):
            with ExitStack() as es:
                order = list(range(11))
                wt, wb = load_weights(es, "wina", 8, 5632, lambda c0, c1: W_in[l, :, c0:c1], order)
                env = prologue_env(es)
                pp = Ring(fw, es, "pp", 6, [128, TS], F32, psum=True)
                tmp = Ring(fw, es, "tmp", 2, [128, TS], F32)
                st = Ring(fw, es, "st", 4, [128, TS], BF16, dsem=True)
                stq = Ring(fw, es, "stq", 2, [128, 4, TS], BF16, dsem=True)
                big = lambda n, d=F32: es.enter_context(nc.sbuf_tensor(uniq(n), [128, 4, TS], d))
                TA, TB, TC, TE = big("TA"), big("TB"), big("TC"), big("TE")
                QS, KBF = big("QS", BF16), big("KBF", BF16)
                bTA, bTB, bTC, bTE, bQS, bKBF = Buf(), Buf(), Buf(), Buf(), Buf(), Buf()
                fl = lambda X: X[:].rearrange("p h n -> p (h n)")
                db = es.enter_context(nc.sbuf_tensor(uniq("dbx"), [128, 2, 32], F32))
                bdb = Buf()
                dst = Ring(fw, es, "dst", 2, [128, 2, 4, 8], F32, dsem=True)
                gain = gains[:, 1, l, :]
                cnt = [0]

                def evac(eng_alt, out_ap, in_ap, rd, wr, func=None):
                    cnt[0] += 1
                    if func is not None or cnt[0] % 2 == 0:
                        fw.op(ACT, lambda: A.activation(out=out_ap, in_=in_ap, func=(func or AF.Copy)), rd, wr)
                    else:
                        fw.op(DVE, lambda: V.tensor_copy(out=out_ap, in_=in_ap), rd, wr)

                def fm_chunk(xnT, bxt, col):
                    P_, bP, _ = pp.next()

                    def f():
                        for k in range(8):
                            ins = T.matmul(P_[:], lhsT=wt[:, k, col:col + 128], rhs=xnT[:, k, :], start=(k == 0), stop=(k == 7))
                        return ins
                    fw.op(PE, f, bxt + [wb[col // 512]], [bP])
                    return P_, bP

                def tm_proj(xnT, bxt, col, dst_ap_fn):
                    for s in range(4):
                        P_, bP, _ = pp.next()

                        def f():
                            for k in range(8):
                                ins = T.matmul(P_[:], lhsT=xnT[:, k, s * 128:(s + 1) * 128], rhs=wt[:, k, col:col + 512],
                                               start=(k == 0), stop=(k == 7))
                            return ins
                        fw.op(PE, f, bxt + [wb[col // 512]], [bP])
                        o, bo, so = st.next()
                        evac(None, o[:], P_[:], [bP], [bo])
                        fw.dma(SP, so, dst_ap_fn(s), o[:], reads=[bo])

                def body(t, pro, hook):
                    tok = slice(t * TS, (t + 1) * TS)
                    xt, bx, sx, xnT, bxt = pro
                    jobs = []

                    def tm_job(col, dstT, s_):
                        def f():
                            P_, bP, _ = pp.next()

                            def fmm():
                                for k in range(8):
                                    ins = T.matmul(P_[:], lhsT=xnT[:, k, s_ * 128:(s_ + 1) * 128], rhs=wt[:, k, col:col + 512],
                                                   start=(k == 0), stop=(k == 7))
                                return ins
                            fw.op(PE, fmm, bxt + [wb[col // 512]], [bP])
                            o, bo, so = st.next()
                            evac(None, o[:], P_[:], [bP], [bo])
                            fw.dma(SP, so, dstT[t * TS + s_ * 128: t * TS + (s_ + 1) * 128, :], o[:], reads=[bo])
                        return f

                    def plain_job(base, dstT, func, c):
                        def f():
                            P_, bP = fm_chunk(xnT, bxt, base + c * 128)
                            o, bo, so = st.next()
                            evac(None, o[:], P_[:], [bP], [bo], func)
                            fw.dma(SP, so, dstT[c, :, tok], o[:], reads=[bo])
                        return f

                    def conv_job(c):
                        def f():
                            Pa, bPa = fm_chunk(xnT, bxt, 2560 + c * 128)
                            Pc, bPc = fm_chunk(xnT, bxt, 3584 + c * 128)
                            ta, bta, _ = tmp.next()
                            fw.op(ACT, lambda: A.activation(out=ta[:], in_=Pa[:], func=AF.Copy), [bPa], [bta])
                            o, bo, so = st.next()
                            fw.op(DVE, lambda: V.tensor_tensor(out=o[:], in0=Pc[:], in1=ta[:], op=ALU.mult), [bPc, bta], [bo])
                            fw.dma(SP, so, ZT[c, :, tok], o[:], reads=[bo])
                        return f
                    for s_ in range(4):
                        jobs.append(tm_job(512, HV, s_))
                    for s_ in range(4):
                        jobs.append(tm_job(5120, NV, s_))
                    for (base, dstT, func) in ((4096, NQ, None), (4608, NK, None), (3072, CBT, None), (2048, HGT, AF.Silu)):
                        for c in range(4):
                            jobs.append(plain_job(base, dstT, func, c))
                    for c in range(4):
                        jobs.append(conv_job(c))

                    def pump():
                        if jobs:
                            jobs.pop(0)()
                    dt_, bdt, sdt = dst.next()
                    for h in range(4):
                        Pq, bPq = fm_chunk(xnT, bxt, h * 128)
                        fw.op(ACT, lambda: A.activation(out=QS[:, h, :], in_=Pq[:], func=AF.Copy), [bPq], [bQS])
                    for dr in range(2):
                        for h in range(4):
                            Pz, bPz = fm_chunk(xnT, bxt, 1024 + dr * 512 + h * 128)
                            fw.op(ACT, lambda: A.activation(out=TA[:, h, :], in_=Pz[:], func=AF.Sigmoid), [bPz], [bTA])
                        for h in range(4):
                            fw.op(DVE, lambda h=h: V.tensor_scalar(out=TB[:, h, :], in0=TA[:, h, :], scalar1=oml[:, dr, l, h:h + 1],
                                                                  scalar2=lb[:, dr, l, h:h + 1], op0=ALU.mult, op1=ALU.add), [bTA], [bTB])
                        fw.op(DVE, lambda: V.tensor_scalar(out=fl(TB), in0=fl(TB), scalar1=1e-30, scalar2=None, op0=ALU.max), [bTB], [bTB])
                        pump()
                        fw.op(ACT, lambda: A.activation(out=fl(TA), in_=fl(TB), func=AF.Ln), [bTB], [bTA])
                        pump()
                        fw.op(DVE, lambda: V.tensor_scalar(out=fl(TB), in0=fl(TB), scalar1=-1.0, scalar2=1.0, op0=ALU.mult, op1=ALU.add),
                              [bTB, bTA], [bTB])
                        for h in range(4):
                            fw.op(DVE, lambda h=h: V.tensor_tensor_scan(out=TC[:, h, :], data0=rmk[:], data1=TA[:, h, :], initial=0.0,
                                                                       op0=ALU.mult, op1=ALU.add), [bTA], [bTC])
                            pump()
                        c3 = lambda X: fl(X).rearrange("p (g j) -> p g j", j=64)
                        if dr == 0:
                            bt, bb = TC, bTC
                            bend = c3(TC)[:, :, 63:64]
                        else:
                            fw.op(DVE, lambda: V.tensor_tensor(out=fl(TA), in0=fl(TA), in1=fl(TC), op=ALU.subtract), [bTA, bTC], [bTA])
                            fw.op(DVE, lambda: V.tensor_tensor(out=c3(TA), in0=c3(TA), in1=c3(TC)[:, :, 63:64].to_broadcast([128, 32, 64]),
                                                               op=ALU.add), [bTA, bTC], [bTA])
                            bt, bb = TA, bTA
                            bend = c3(TA)[:, :, 0:1]
                        TX, bTX = (TA, bTA) if dr == 0 else (TC, bTC)
                        fw.op(ACT, lambda: A.activation(out=fl(TE), in_=fl(bt), func=AF.Exp), [bb], [bTE])
                        fw.op(ACT, lambda: A.activation(out=fl(TX), in_=fl(bt), func=AF.Exp, scale=-1.0), [bb], [bTX])
                        fw.op(ACT, lambda: A.activation(out=db[:, dr, :], in_=bend.rearrange("p g o -> p (g o)"), func=AF.Exp), [bb], [bdb])
                        pump()
                        o, bo, so = stq.next()
                        fw.op(DVE, lambda: V.tensor_tensor(out=fl(o), in0=fl(QS), in1=fl(TE), op=ALU.mult), [bQS, bTE], [bo])
                        fw.dma(SP, so, HQ[dr, 0, :, :, tok].rearrange("h p n -> p h n"), o[:], reads=[bo])
                        pump()
                        fw.op(DVE, lambda: V.tensor_tensor(out=fl(TX), in0=fl(TB), in1=fl(TX), op=ALU.mult), [bTB, bTX], [bTX])
                        pump()
                        o2, bo2, so2 = stq.next()
                        fw.op(ACT, lambda: A.activation(out=fl(o2), in_=fl(TX), func=AF.Copy), [bTX], [bo2])
                        fw.dma(SP, so2, HQ[dr, 1, :, :, tok].rearrange("h p n -> p h n"), o2[:], reads=[bo2])
                        fw.op(DVE, lambda: V.tensor_tensor(out=c3(KBF), in0=c3(TX), in1=db[:, dr, :].rearrange("p (g o) -> p g o", o=1).to_broadcast([128, 32, 64]),
                                                           op=ALU.mult), [bTX, bdb], [bKBF])
                        pump()
                        if dr == 0:
                            fw.op(ACT, lambda: A.activation(out=dt_[:, dr, :, :].rearrange("p h c -> p (h c)"), in_=db[:, dr, :], func=AF.Copy), [bdb], [bdt])
                        else:
                            d4 = db[:, dr, :].rearrange("p (h c) -> p h c", h=4)
                            for cc in range(8):
                                fw.op(ACT, lambda cc=cc: A.activation(out=dt_[:, dr, :, 7 - cc:8 - cc], in_=d4[:, :, cc:cc + 1], func=AF.Copy), [bdb], [bdt])
                        pump()
                        for h in range(4):
                            pt, bpt, _ = env["ptr"].next()

                            def ftr():
                                for s in range(4):
                                    ins = T.transpose(pt[:, s * 128:(s + 1) * 128], KBF[:, h, s * 128:(s + 1) * 128], idb[:])
                                return ins
                            fw.op(PE, ftr, [bKBF], [bpt])
                            o3, bo3, so3 = st.next()
                            evac(None, o3[:], pt[:], [bpt], [bo3])
                            fw.dma(SP, so3, KB[dr, h, tok, :].rearrange("(s p) k -> p s k", p=128),
                                   o3[:].rearrange("p (s k) -> p s k", k=128), reads=[bo3])
                    while jobs:
                        pump()
                    hook()
                    fw.dma(SP, sdt, DCH[0, :, :, t * 8:(t + 1) * 8], dt_[:, 0, :, :], reads=[bdt])
                    (sq0, sqn) = [(a // TS, b // TS) for (a, b) in seq_info if a // TS <= t < (a + b) // TS][0]
                    pos0 = sq0 * 8 + (sqn - 1 - (t - sq0)) * 8
                    fw.dma(SP, sdt, DCH[1, :, :, pos0:pos0 + 8], dt_[:, 1, :, :], reads=[bdt])
                pipelined_pro(list(range(NTI)), env, XS, gain, body)
                fw.barrier()

        def mixer_h(l):
            TM = max(SEQS)
            NCM = TM // 64
            NSM = TM // 128
            with ExitStack() as es:
                qr = Ring(fw, es, "hq_", 2, [128, TM], BF16, dsem=True)
                kr = Ring(fw, es, "hk_", 2, [128, TM], BF16, dsem=True)
                kbr = Ring(fw, es, "kb_", 2, [128, 2, NSM, 128], BF16, dsem=True)
                for i_ in range(2):
                    fw.op(POOL, lambda i_=i_: G.memset(kbr.t[i_][:], 0.0), [], [kbr.b[i_]])
                vr = Ring(fw, es, "hv_", 2, [128, NSM, 128], BF16, dsem=True)
                dcr = Ring(fw, es, "dc_", 2, [128, NCM], F32, dsem=True)
                dxr = Ring(fw, es, "dx_", 2, [128, 16 * NCM], F32)
                hgr = Ring(fw, es, "hg_", 2, [128, 512], BF16, dsem=True)
                ohr = Ring(fw, es, "oh_", 2, [128, 512], BF16, dsem=True)
                Uall = es.enter_context(nc.sbuf_tensor(uniq("Uall"), [128, 128 * NCM], F32))
                Sall = es.enter_context(nc.sbuf_tensor(uniq("Sall"), [128, 128 * NCM], BF16))
                of = es.enter_context(nc.sbuf_tensor(uniq("ofa"), [128, TM], F32))
                atrr = Ring(fw, es, "atraw", 2, [128, 4, 128], BF16)
                atm = es.enter_context(nc.sbuf_tensor(uniq("atm"), [128, NSM, 128], BF16))
                mkb = es.enter_context(nc.sbuf_tensor(uniq("mkb"), [128, 2, 128], BF16))
                bmk = Buf()
                fw.op(DVE, lambda: V.tensor_copy(out=mkb[:, 0, :], in_=maskf[:]), [], [bmk])
                fw.op(DVE, lambda: V.tensor_copy(out=mkb[:, 1, :], in_=maskb[:]), [bmk], [bmk])
                sqr = Ring(fw, es, "sq", 2, [128, 512], BF16)
                rsr = Ring(fw, es, "rs", 2, [128, 512], F32)
                pU = Ring(fw, es, "pU", 3, [128, 4, 128], F32, psum=True)
                pA = Ring(fw, es, "pA", 2, [128, 4, 128], F32, psum=True)
                pO = Ring(fw, es, "pO", 2, [128, 4, 128], F32, psum=True)
                pN = Ring(fw, es, "pN", 1, [128, 512], F32, psum=True)
                bU = [Buf() for _ in range(NCM // 4)]
                bS = [Buf() for _ in range(8)]
                bAr = [Buf() for _ in range(NSM // 4)]
                bAm = [Buf() for _ in range(NSM // 4)]
                bof = [Buf() for _ in range(TM // 512)]
                ev = [0]

                def load_unit(T0, TL, h, dr):
                    NC = TL // 64
                    NS = TL // 128
                    qh, bq, sq_ = qr.next()
                    fw.dma(SP, sq_, qh[:, 0:TL], HQ[dr, 0, h, :, T0:T0 + TL], writes=[bq])
                    kh, bk, sk_ = kr.next()
                    fw.dma(SP, sk_, kh[:, 0:TL], HQ[dr, 1, h, :, T0:T0 + TL], writes=[bk])
                    kb, bkb, skb = kbr.next()
                    kbv = KB[dr, h, T0:T0 + TL, :].rearrange("(s p) k -> p s k", p=128)
                    fw.dma_group(SP, skb, [(kb[0:64, 0, 0:NS, :], kbv[0:64]), (kb[64:128, 1, 0:NS, :], kbv[64:128])], writes=[bkb])
                    dc, bdc, sdc = dcr.next()
                    fw.dma(SP, sdc, dc[:, 0:NC], DCH[dr, :, h, T0 // 64:T0 // 64 + NC], writes=[bdc])
                    return (qh, bq, kh, bk, kb, bkb, dc, bdc)

                units = [(T0, TL, h, dr) for (T0, TL) in seq_info for h in range(4) for dr in range(2)]
                nxt = load_unit(*units[0])
                for ui, (T0, TL, h, dr) in enumerate(units):
                    NC = TL // 64
                    NS = TL // 128
                    (qh, bq, kh, bk, kb, bkb, dc, bdc) = nxt
                    if dr == 0:
                        vt, bv, sv = vr.next()
                        fw.dma(SP, sv, vt[:, 0:NS, :], HV[T0:T0 + TL, h * 128:(h + 1) * 128].rearrange("(s p) v -> p s v", p=128), writes=[bv])
                    if ui + 1 < len(units):
                        nxt = load_unit(*units[ui + 1])
                    Uv = Uall[:, 0:128 * NC].rearrange("p (v c) -> p v c", c=NC)
                    Sv = Sall[:, 0:128 * NC].rearrange("p (v c) -> p v c", c=NC)
                    dx, bdx, _ = dxr.next()
                    fw.op(DVE, lambda: V.memset(dc[:, 0:1], 0.0), [bdc], [bdc])
                    fw.op(DVE, lambda: V.tensor_copy(out=dx[:, 0:16 * NC].rearrange("p (v c) -> p v c", c=NC),
                                                     in_=dc[:, 0:NC].rearrange("p (o c) -> p o c", o=1).to_broadcast([128, 16, NC])), [bdc], [bdx])
                    import os as _os
                    KH = int(_os.environ.get("KH", "9"))
                    for g in range(NC // 4 if KH >= 1 else 0):
                        PU, bPU, _ = pU.next()

                        def fU():
                            for j in range(4):
                                pos = 4 * g + j
                                c = pos if dr == 0 else NC - 1 - pos
                                s_, cc = c // 2, c % 2
                                ins = T.matmul(PU[:, j, :], lhsT=kb[:, cc, s_, :], rhs=vt[:, s_, :], start=True, stop=True)
                            return ins
                        KM = int(_os.environ.get("KM", "3"))
                        if KM & 1:
                            fw.op(PE, fU, [bkb, bv], [bPU])
                        ev[0] += 1
                        if not (KM & 2):
                            continue
                        dstU = Uv[:, :, 4 * g:4 * g + 4].rearrange("p v c -> p c v")
                        if int(_os.environ.get("KE", "2")) == 3:
                            dstU = Uall[:, 0:512].rearrange("p (c v) -> p c v", c=4)
                        KE = int(_os.environ.get("KE", "1"))
                        if (ev[0] % 2 == 0 and KE == 2) or KE == 1:
                            fw.op(ACT, lambda: A.activation(out=dstU, in_=PU[:], func=AF.Copy), [bPU], [bU[g]])
                        else:
                            fw.op(DVE, lambda: V.tensor_copy(out=dstU, in_=PU[:]), [bPU], [bU[g]])
                    import os as _os
                    KH = int(_os.environ.get("KH", "9"))
                    for vg in range(8 if KH >= 2 else 0):
                        fw.op(DVE, lambda vg=vg: V.tensor_tensor_scan(out=Sv[:, vg * 16:(vg + 1) * 16, :].rearrange("p v c -> p (v c)"),
                                                                     data0=dx[:, 0:16 * NC],
                                                                     data1=Uv[:, vg * 16:(vg + 1) * 16, :].rearrange("p v c -> p (v c)"),
                                                                     initial=0.0, op0=ALU.mult, op1=ALU.add),
                              bU[0:NC // 4] + [bdx], [bS[vg]])
                    for gp in range(NS // 4 if KH >= 3 else 0):
                        PA, bPA, _ = pA.next()

                        def fA():
                            for j in range(4):
                                s_ = gp * 4 + j
                                ins = T.matmul(PA[:, j, :], lhsT=kh[:, s_ * 128:(s_ + 1) * 128], rhs=qh[:, s_ * 128:(s_ + 1) * 128],
                                               start=True, stop=True)
                            return ins
                        fw.op(PE, fA, [bk, bq], [bPA])
                        atr_, bar_, _ = atrr.next()
                        fw.op(ACT, lambda: A.activation(out=atr_[:], in_=PA[:], func=AF.Copy), [bPA], [bar_])
                        fw.op(POOL, lambda: G.tensor_tensor(out=atm[:, gp * 4:gp * 4 + 4, :], in0=atr_[:],
                                                            in1=mkb[:, dr:dr + 1, :].to_broadcast([128, 4, 128]), op=ALU.mult),
                              [bar_, bmk], [bAm[gp]])
                    for gp in range(NS // 4 if KH >= 4 else 0):
                        PO, bPO, _ = pO.next()

                        def fO():
                            first = True
                            for j in range(4):
                                s_ = gp * 4 + j
                                for cc in range(2):
                                    c = 2 * s_ + cc
                                    pos = c if dr == 0 else NC - 1 - c
                                    T.matmul(PO[:, j, cc * 64:(cc + 1) * 64], lhsT=(Sv[:, :, pos - 1] if pos > 0 else zeros_b[:]),
                                             rhs=qh[:, s_ * 128 + cc * 64:s_ * 128 + (cc + 1) * 64], start=first, stop=False)
                                    first = False
                            for j in range(4):
                                s_ = gp * 4 + j
                                ins = T.matmul(PO[:, j, :], lhsT=vt[:, s_, :], rhs=atm[:, s_, :], start=first, stop=(j == 3))
                                first = False
                            return ins
                        fw.op(PE, fO, bS + [bq, bv, bAm[gp]], [bPO])
                        ofs = of[:, gp * 512:(gp + 1) * 512]
                        if dr == 0:
                            fw.op(ACT, lambda: A.activation(out=ofs, in_=PO[:].rearrange("p j t -> p (j t)"), func=AF.Copy), [bPO], [bof[gp]])
                        else:
                            fw.op(DVE, lambda: V.tensor_tensor(out=ofs, in0=PO[:].rearrange("p j t -> p (j t)"), in1=ofs, op=ALU.add),
                                  [bPO, bof[gp]], [bof[gp]])
                    if dr == 1 and KH >= 5:
                        for gp in range(NS // 4):
                            cs = slice(gp * 512, (gp + 1) * 512)
                            oh, boh, soh = ohr.next()
                            hg, bhg, shg = hgr.next()
                            fw.dma(SP, shg, hg[:], HGT[h, :, T0 + gp * 512:T0 + (gp + 1) * 512], writes=[bhg])
                            sq, bsq, _ = sqr.next()
                            fw.op(ACT, lambda: A.activation(out=sq[:], in_=of[:, cs], func=AF.Square), [bof[gp]], [bsq])
                            PN, bPN, _ = pN.next()
                            fw.op(PE, lambda: T.matmul(PN[:], lhsT=ones_b[:], rhs=sq[:], start=True, stop=True), [bsq], [bPN])
                            rs, brs, _ = rsr.next()
                            fw.op(ACT, lambda: A.activation(out=rs[:], in_=PN[:], func=AF.Ln, scale=1.0 / 128, bias=EPS), [bPN], [brs])
                            fw.op(ACT, lambda: A.activation(out=rs[:], in_=rs[:], func=AF.Exp, scale=-0.5), [brs], [brs])
                            fw.op(DVE, lambda: V.tensor_tensor(out=rs[:], in0=of[:, cs], in1=rs[:], op=ALU.mult), [bof[gp], brs], [brs])
                            fw.op(DVE, lambda: V.scalar_tensor_tensor(out=oh[:], in0=rs[:], scalar=gn[:, l, h:h + 1], in1=hg[:],
                                                                      op0=ALU.mult, op1=ALU.mult), [brs, bhg], [boh])
                            fw.dma(SP, soh, OHG[h, :, T0 + gp * 512:T0 + (gp + 1) * 512], oh[:], reads=[boh])
                fw.barrier()

        def mixer_n(l):
            TM = max(SEQS)
            with ExitStack() as es:
                qT = es.enter_context(nc.sbuf_tensor(uniq("naq"), [128, 4, TM], BF16))
                kT = es.enter_context(nc.sbuf_tensor(uniq("nak"), [128, 4, TM], BF16))
                vx = es.enter_context(nc.sbuf_tensor(uniq("nav"), [128, TM // 128, 8, 65], BF16))
                bmc = es.enter_context(nc.sbuf_tensor(uniq("bmc"), [128, 8, 5, 5, 128], BF16))
                bq, bk, bv, bbm, bb2 = Buf(), Buf(), Buf(), Buf(), Buf()
                sl = [fw.new_dsem() for _ in range(4)]
                with ExitStack() as es2:
                    b2 = es2.enter_context(nc.sbuf_tensor(uniq("b2s"), [128, 8, 9, 128], F32))
                    ng = es2.enter_context(nc.sbuf_tensor(uniq("ngs"), [128, 5, 5, 128], F32))
                    fw.dma_group(SP, sl[3], [(b2[:], b2g_in[l]), (ng[:], neg_in)], writes=[bb2])
                    for c in range(5):
                        fw.op(DVE, lambda c=c: V.tensor_tensor(out=bmc[:, :, c, :, :], in0=b2[:, :, 4 - c:9 - c, :],
                                                               in1=ng[:, c:c + 1, :, :].to_broadcast([128, 8, 5, 128]), op=ALU.add), [bb2], [bbm])
                    fw.barrier()
                fw.op(POOL, lambda: G.memset(vx[:, :, :, 64:65], 1.0), [], [bv])
                vtr = Ring(fw, es, "vtmp", 2, [128, 8, 512], BF16, dsem=True)
                pS = Ring(fw, es, "pS", 2, [128, 1024], F32, psum=True)
                pO = Ring(fw, es, "pO", 2, [128, 512], F32, psum=True)
                ptr = Ring(fw, es, "ptr", 1, [128, TS], BF16, psum=True)
                tr = Ring(fw, es, "nt", 3, [128, 640], F32)
                pr = Ring(fw, es, "np", 4, [128, 5, 128], BF16)
                rr = Ring(fw, es, "nr", 4, [128, 1], F32)
                otr = Ring(fw, es, "not", 2, [128, 512], BF16)
                osr = Ring(fw, es, "nos", 2, [128, 4, 128], BF16, dsem=True)
                for (T0, TL) in seq_info:
                    rows = TL // 64
                    fw.dma(SP, sl[0], qT[:, :, 0:TL], NQ[:, :, T0:T0 + TL].rearrange("c p n -> p c n"), writes=[bq])
                    fw.dma(SP, sl[1], kT[:, :, 0:TL], NK[:, :, T0:T0 + TL].rearrange("c p n -> p c n"), writes=[bk])
                    nb = TL // 128
                    for b0 in range(0, nb, 8):
                        vtm_, bvt, svt = vtr.next()
                        fw.dma(SP, svt, vtm_[:], NV[T0 + b0 * 128:T0 + (b0 + 8) * 128, :].rearrange("(b p) n -> p b n", p=128), writes=[bvt])
                        fw.op(POOL, lambda: G.tensor_copy(out=vx[:, b0:b0 + 8, :, 0:64], in_=vtm_[:].rearrange("p b (h d) -> p b h d", d=64)),
                              [bvt], [bv])
                    items = [(rp, h) for rp in range(rows // 2) for h in range(8)]

                    def geom(rp):
                        r = 2 * rp
                        re_ = min(max(r - 4, 0), rows - 10)
                        cls = {0: 0, -2: 1, -4: 2, -6: 3, -8: 4}[re_ - r]
                        return r, re_, cls

                    def stage_a(rp, h):
                        r, re_, cls = geom(rp)
                        c = h // 2
                        pp_ = slice((h % 2) * 64, (h % 2) * 64 + 64)
                        PS_, bPS, _ = pS.next()

                        def fS():
                            for j in range(5):
                                ins = T.matmul(PS_[:, j * 128:(j + 1) * 128], lhsT=kT[pp_, c, (re_ + 2 * j) * 64:(re_ + 2 * j) * 64 + 128],
                                               rhs=qT[pp_, c, r * 64:r * 64 + 128], start=True, stop=True)
                            return ins
                        fw.op(PE, fS, [bq, bk], [bPS])
                        tt, btt, _ = tr.next()
                        fw.op(DVE, lambda: V.scalar_tensor_tensor(out=tt[:], in0=PS_[:, 0:640], scalar=0.125,
                                                                  in1=bmc[:, h, cls, :, :].rearrange("p j q -> p (j q)"),
                                                                  op0=ALU.mult, op1=ALU.add), [bPS, bbm], [btt])
                        pt_, bpt_, _ = pr.next()
                        fw.op(ACT, lambda: A.activation(out=pt_[:].rearrange("p j q -> p (j q)"), in_=tt[:], func=AF.Exp), [btt], [bpt_])
                        return pt_, bpt_

                    cur_ot = [None]

                    def stage_b(rp, h, st_):
                        r, re_, cls = geom(rp)
                        pt_, bpt_ = st_
                        if h == 0:
                            cur_ot[0] = otr.next()
                        ot, bot, _ = cur_ot[0]
                        PO_, bPO, _ = pO.next()

                        def fO():
                            for j in range(5):
                                ins = T.matmul(PO_[:, 0:65], lhsT=pt_[:, j, :], rhs=vx[:, re_ // 2 + j, h, :], start=(j == 0), stop=(j == 4))
                            return ins
                        fw.op(PE, fO, [bpt_, bv], [bPO])
                        rc, brc, _ = rr.next()
                        fw.op(DVE, lambda: V.reciprocal(out=rc[:], in_=PO_[:, 64:65]), [bPO], [brc])
                        fw.op(ACT, lambda: A.activation(out=ot[:, h * 64:(h + 1) * 64], in_=PO_[:, 0:64], func=AF.Copy, scale=rc[:, 0:1]),
                              [bPO, brc], [bot])
                        if h == 7:
                            pt, bpt, _ = ptr.next()

                            def ftr():
                                for c in range(4):
                                    ins = T.transpose(pt[:, c * 128:(c + 1) * 128], ot[:, c * 128:(c + 1) * 128], idb[:])
                                return ins
                            fw.op(PE, ftr, [bot], [bpt])
                            os_, bos, sos = osr.next()
                            fw.op(DVE, lambda: V.tensor_copy(out=os_[:].rearrange("p c t -> p (c t)"), in_=pt[:]), [bpt], [bos])
                            fw.dma(SP, sos, ONA[:, :, T0 + r * 64:T0 + r * 64 + 128].rearrange("c p n -> p c n"), os_[:], reads=[bos])

                    pend = [stage_a(*items[0])]
                    if len(items) > 1:
                        pend.append(stage_a(*items[1]))
                    for i, (rp, h) in enumerate(items):
                        if i + 2 < len(items):
                            pend.append(stage_a(*items[i + 2]))
                        stage_b(rp, h, pend.pop(0))
                fw.barrier()

        def mixer_o(l):
            with ExitStack() as es:
                wg, wgb = load_weights(es, "wg", 8, 3072, lambda c0, c1: W_in[l, :, 5632 + c0:5632 + c1], list(range(6)))
                wm = []
                for m in range(3):
                    wm.append(load_weights(es, "wm%d" % m, 4, D, lambda c0, c1, m=m: W_mo[m][l, :, c0:c1], [0, 1]))
                wo, wob = load_weights(es, "wo", 8, D, lambda c0, c1: W_out[l, :, c0:c1], [0, 1])
                env = prologue_env(es)
                inr = [Ring(fw, es, "oh_", 2, [128, 4, TS], BF16, dsem=True), None, Ring(fw, es, "on_", 2, [128, 4, TS], BF16, dsem=True)]
                zr = Ring(fw, es, "z_", 1, [128, 4, TS + 2], BF16, dsem=True)
                cbr = Ring(fw, es, "cb_", 1, [128, 4, TS], BF16, dsem=True)
                ocr = Ring(fw, es, "ocv", 2, [128, 4, TS], BF16)
                pre = {}
                mTr = Ring(fw, es, "mT", 1, [128, 8, TS], BF16)
                tmp = Ring(fw, es, "tm", 5, [128, TS], F32)
                pY = Ring(fw, es, "pY", 6, [128, TS], F32, psum=True)
                pG = pY
                pX = pY
                gain = gains[:, 1, l, :]
                def pre_tile(t):
                    tok = slice(t * TS, (t + 1) * TS)
                    oh, boh, soh = inr[0].next()
                    fw.dma(SP, soh, oh[:], OHG[:, :, tok].rearrange("c p n -> p c n"), writes=[boh])
                    on, bon, son = inr[2].next()
                    fw.dma(SP, son, on[:], ONA[:, :, tok].rearrange("c p n -> p c n"), writes=[bon])
                    z, bz, sz = zr.next()
                    pairs = [(z[:, :, 1:TS + 1], ZT[:, :, tok].rearrange("c p n -> p c n"))]
                    lo = t not in seq_start_tiles
                    hi = t not in seq_end_tiles
                    if lo:
                        pairs.append((z[:, :, 0:1], ZT[:, :, t * TS - 1:t * TS].rearrange("c p n -> p c n")))
                    if hi:
                        pairs.append((z[:, :, TS + 1:TS + 2], ZT[:, :, (t + 1) * TS:(t + 1) * TS + 1].rearrange("c p n -> p c n")))
                    fw.dma_group(SP, sz, pairs, writes=[bz], slow=True)
                    if not lo:
                        fw.op(POOL, lambda: G.memset(z[:, :, 0:1], 0.0), [bz], [bz])
                    if not hi:
                        fw.op(POOL, lambda: G.memset(z[:, :, TS + 1:TS + 2], 0.0), [bz], [bz])
                    cb_, bcb, scb = cbr.next()
                    fw.dma(SP, scb, cb_[:], CBT[:, :, tok].rearrange("c p n -> p c n"), writes=[bcb])
                    oc_, boc, _ = ocr.next()
                    for c in range(4):
                        t1, b1, _ = tmp.next()
                        fw.op(DVE, lambda: V.tensor_scalar(out=t1[:], in0=z[:, c, 1:TS + 1], scalar1=cw[:, l, 1, c:c + 1], scalar2=cbias[:, l, c:c + 1],
                                                           op0=ALU.mult, op1=ALU.add), [bz], [b1])
                        fw.op(DVE, lambda: V.scalar_tensor_tensor(out=t1[:], in0=z[:, c, 0:TS], scalar=cw[:, l, 0, c:c + 1], in1=t1[:],
                                                                  op0=ALU.mult, op1=ALU.add), [bz, b1], [b1])
                        fw.op(DVE, lambda: V.scalar_tensor_tensor(out=t1[:], in0=z[:, c, 2:TS + 2], scalar=cw[:, l, 2, c:c + 1], in1=t1[:],
                                                                  op0=ALU.mult, op1=ALU.add), [bz, b1], [b1])
                        fw.op(DVE, lambda: V.tensor_tensor(out=oc_[:, c, :], in0=t1[:], in1=cb_[:, c, :], op=ALU.mult), [b1, bcb], [boc])
                    pre[t] = (oh, boh, oc_, boc, on, bon)

                def body(t, pro, hook):
                    tok = slice(t * TS, (t + 1) * TS)
                    xt, bx, sx, xnT, bxt = pro
                    if t not in pre:
                        pre_tile(t)
                    (oh, boh, oc_, boc, on, bon) = pre.pop(t)
                    srcs = [(oh, boh), (oc_, boc), (on, bon)]
                    mT, bmT, _ = mTr.next()
                    for oc in range(8):
                        if oc == 4 and t + 1 < NTI:
                            pre_tile(t + 1)
                        if oc == 6:
                            hook()
                        macc, bmacc, _ = tmp.next()
                        for m in range(3):
                            PY, bPY, _ = pY.next()
                            PG, bPG, _ = pG.next()
                            src, bsrc = srcs[m]
                            wmt, wmb = wm[m]

                            def fY():
                                for k in range(4):
                                    ins = T.matmul(PY[:], lhsT=wmt[:, k, oc * 128:(oc + 1) * 128], rhs=src[:, k, :], start=(k == 0), stop=(k == 3))
                                return ins
                            fw.op(PE, fY, [bsrc, wmb[oc // 4]], [bPY])
                            gc = m * 1024 + oc * 128

                            def fG():
                                for k in range(8):
                                    ins = T.matmul(PG[:], lhsT=wg[:, k, gc:gc + 128], rhs=xnT[:, k, :], start=(k == 0), stop=(k == 7))
                                return ins
                            fw.op(PE, fG, bxt + [wgb[gc // 512]], [bPG])
                            sg, bsg, _ = tmp.next()
                            fw.op(ACT, lambda: A.activation(out=sg[:], in_=PG[:], func=AF.Sigmoid), [bPG], [bsg])
                            if m == 0:
                                fw.op(DVE, lambda: V.tensor_tensor(out=macc[:], in0=PY[:], in1=sg[:], op=ALU.mult), [bPY, bsg], [bmacc])
                            else:
                                fw.op(DVE, lambda: V.tensor_tensor(out=sg[:], in0=PY[:], in1=sg[:], op=ALU.mult), [bPY, bsg], [bsg])
                                if m == 1:
                                    fw.op(POOL, lambda: G.tensor_tensor(out=macc[:], in0=macc[:], in1=sg[:], op=ALU.add), [bsg, bmacc], [bmacc])
                                else:
                                    fw.op(POOL, lambda: G.tensor_tensor(out=mT[:, oc, :], in0=macc[:], in1=sg[:], op=ALU.add), [bsg, bmacc], [bmT])
                    for s in range(4):
                        for hf in range(2):
                            PX, bPX, _ = pX.next()

                            def fX():
                                for k in range(8):
                                    ins = T.matmul(PX[:], lhsT=mT[:, k, s * 128:(s + 1) * 128], rhs=wo[:, k, hf * 512:(hf + 1) * 512],
                                                   start=(k == 0), stop=(k == 7))
                                return ins
                            fw.op(PE, fX, [bmT, wob[hf]], [bPX])
                            fw.op(DVE, lambda: V.tensor_tensor(out=xt[:, s, hf * 512:(hf + 1) * 512], in0=PX[:], in1=xt[:, s, hf * 512:(hf + 1) * 512],
                                                               op=ALU.add), [bPX, bx], [bx])
                    fw.dma(SP, sx, XS[tok, :].rearrange("(s p) d -> p s d", p=128), xt[:], reads=[bx])
                pipelined_pro(list(range(NTI)), env, XS, gain, body)
                fw.barrier()

        def final_norm():
            with ExitStack() as es:
                gf = es.enter_context(nc.sbuf_tensor(uniq("gf"), [128, D], F32))
                bgf = Buf()
                fw.dma(SP, fw.new_dsem(), gf[:], gfin_in, writes=[bgf])
                xr = Ring(fw, es, "xt", 2, [128, 4, D], F32, dsem=True)
                ssr = Ring(fw, es, "ss", 2, [128, 12], F32)
                jk = es.enter_context(nc.sbuf_tensor(uniq("junk"), [128, D], BF16))
                bjk = Buf()
                for t in range(NTI):
                    tok = slice(t * TS, (t + 1) * TS)
                    xt, bx, sx = xr.next()
                    fw.dma(SP, sx, xt[:], XS[tok, :].rearrange("(s p) d -> p s d", p=128), writes=[bx])
                    ss, bss, _ = ssr.next()
                    fw.op(DVE, lambda: V.memset(ss[:], 0.0), [], [bss])
                    for s in range(4):
                        fw.op(ACT, lambda s=s: A.activation(out=jk[:], in_=xt[:, s, :], func=AF.Square, accum_out=ss[:, s:s + 1]), [bx], [bjk, bss])
                    fw.op(ACT, lambda: A.activation(out=ss[:, 4:8], in_=ss[:, 0:4], func=AF.Ln, scale=1.0 / D, bias=EPS), [bss], [bss])
                    fw.op(ACT, lambda: A.activation(out=ss[:, 8:12], in_=ss[:, 4:8], func=AF.Exp, scale=-0.5), [bss], [bss])
                    for s in range(4):
                        fw.op(DVE, lambda s=s: V.scalar_tensor_tensor(out=xt[:, s, :], in0=xt[:, s, :], scalar=ss[:, 8 + s:9 + s], in1=gf[:],
                                                                     op0=ALU.mult, op1=ALU.mult), [bx, bss, bgf], [bx])
                    fw.dma(SP, sx, y_out[tok, :].rearrange("(s p) d -> p s d", p=128), xt[:], reads=[bx])
                fw.barrier()

        import os as _os
        kstop = int(_os.environ.get("KSTOP", "99"))
        phases = []
        for l in range(L):
            def ffn(l, which, xsrc):
                with ExitStack() as oes:
                    outer = {"es": oes, "wdn_t": oes.enter_context(nc.sbuf_tensor(uniq("wdn"), [128, 22, D], BF16))}
                    ffn_gu(l, which, xsrc, outer)
                    ffn_down(l, which, xsrc, XS, outer)
            phases.append(lambda l=l: ffn(l, 0, x_in if l == 0 else XS))
            phases.append(lambda l=l: mixer_a(l))
            phases.append(lambda l=l: mixer_h(l))
            phases.append(lambda l=l: mixer_n(l))
            phases.append(lambda l=l: mixer_o(l))
            phases.append(lambda l=l: ffn(l, 1, XS))
        phases.append(final_norm)
        for i, ph in enumerate(phases):
            if i < kstop:
                ph()
    return nc


def host_consts(L, na_rpb):
    c = {}
    c["ident"] = np.eye(128, dtype=np.float32)
    s = np.arange(128)[:, None]
    t = np.arange(128)[None, :]
    same = (s // 64) == (t // 64)
    c["maskf"] = (same & (s <= t)).astype(np.float32)
    c["maskb"] = (same & (s >= t)).astype(np.float32)
    rm = np.ones((128, TS), np.float32)
    rm[:, ::64] = 0.0
    c["rmask"] = rm
    kp = (np.arange(128) // 64)[:, None]
    kc = (np.arange(128) % 64)[:, None]
    qp = (np.arange(128) // 64)[None, :]
    qc = (np.arange(128) % 64)[None, :]
    dc = np.clip(kc - qc + 15, 0, 30)
    b2g = np.zeros((L, 128, 8, 9, 128), np.float32)
    for pl in range(9):
        base = 2 * pl - 1
        drr = np.clip(base + kp - qp, 0, 14)
        b2g[:, :, :, pl, :] = np.transpose(na_rpb[:, :, drr, dc], (0, 2, 1, 3))
    c["b2g"] = b2g
    cs = np.clip(qc - 8, 0, 48)
    col_ok = (kc >= cs) & (kc < cs + 16)
    neg = np.zeros((128, 5, 5, 128), np.float32)
    for cls in range(5):
        delta = -2 * cls
        for j in range(5):
            base = delta + 2 * j + 7
            dr = base + kp - qp
            if cls == 0:
                lo = 7 - qp
                hi = 14 - qp
            elif cls == 1:
                lo = 7 - (2 + qp)
                hi = 14 - (2 + qp)
            elif cls == 2:
                lo = 3 + 0 * qp
                hi = 10 + 0 * qp
            elif cls == 3:
                lo = (4 - qp) - 1
                hi = (4 - qp) + 6
            else:
                lo = (2 - qp) - 1
                hi = (2 - qp) + 6
            ok = col_ok & (dr >= lo) & (dr <= hi)
            neg[:, cls, j, :] = np.where(ok, 0.0, -1e30)
    c["negm"] = neg
    return c


def fm(a, L):
    return np.ascontiguousarray(a.reshape(L, -1, 128).transpose(2, 0, 1))


def make_inputs(inp, L):
    sh = {}
    for k in ("ffn1_w_gu", "ffn2_w_gu", "ffn1_w_down", "ffn2_w_down", "w_in", "w_hg_out", "w_cv_out", "w_na_out", "w_out"):
        sh[k] = np.ascontiguousarray(inp[k], dtype=np.float32)
    g = np.stack([fm(inp["ffn1_norm"], L), fm(inp["mix_norm"], L), fm(inp["ffn2_norm"], L)], axis=1)
    sh["gains"] = np.ascontiguousarray(g, dtype=np.float32)
    sh["gfin"] = np.ascontiguousarray(np.broadcast_to(inp["final_norm"][None, :], (128, D)), dtype=np.float32)
    lbl = inp["hg_lb_logits"].reshape(2, L, 4, 128).transpose(3, 0, 1, 2)
    sh["lbl"] = np.ascontiguousarray(lbl, dtype=np.float32)
    sh["gn"] = fm(inp["hg_out_norm"], L).astype(np.float32)
    cwt = inp["conv_w"].reshape(L, 3, 4, 128).transpose(3, 0, 1, 2)
    sh["cw"] = np.ascontiguousarray(cwt, dtype=np.float32)
    sh["cbias"] = fm(inp["conv_b"], L).astype(np.float32)
    sh.update(host_consts(L, np.asarray(inp["na_rpb"], dtype=np.float32)))
    return sh


_CACHE = {}


def kernel(**inputs):
    inp = {k: np.asarray(v) for k, v in inputs.items()}
    L = 4
    SEQS = [2048, 2048, 4096]
    key = ("full",)
    if key not in _CACHE:
        _CACHE[key] = build(SEQS, L)
    nc = _CACHE[key]
    sh = make_inputs(inp, L)
    xp = inp["x_prompt"]
    xs = inp["x_sample"]
    in_maps = []
    for c in range(8):
        xc = np.concatenate([xp[2 * c], xp[2 * c + 1], xs[c]], axis=0).astype(np.float32)
        m = dict(sh)
        m["x"] = np.ascontiguousarray(xc)
        in_maps.append(m)
    res = run_bass_kernel_spmd(nc, in_maps, core_ids=list(range(8)))
    yp = np.zeros((16, 2048, D), np.float32)
    ys = np.zeros((8, 4096, D), np.float32)
    for c in range(8):
        y = res.results[c]["y"]
        yp[2 * c] = y[0:2048]
        yp[2 * c + 1] = y[2048:4096]
        ys[c] = y[4096:8192]
    return (yp, ys)
```

```python
import numpy as np
import concourse.bass as bass
import concourse.mybir as mybir
from concourse.bass_utils import run_bass_kernel_spmd
from contextlib import ExitStack

F32 = mybir.dt.float32
BF16 = mybir.dt.bfloat16
AF = mybir.ActivationFunctionType
ALU = mybir.AluOpType

D = 1024
DFF = 2816
NIN = 8704
EPS = 1e-6
TS = 512


class Buf:
    __slots__ = ("w", "r")

    def __init__(self):
        self.w = {}
        self.r = {}


class Eng:
    def __init__(self, name, h, sem, is_pe=False):
        self.name = name
        self.h = h
        self.sem = sem
        self.cnt = 0
        self.seen = {}
        self.is_pe = is_pe


class FW:
    def __init__(self, nc, es, n_dsem=40):
        self.nc = nc
        mk = lambda n: es.enter_context(nc.semaphore(n))
        self.pe = Eng("pe", nc.tensor, mk("s_pe"), True)
        self.act = Eng("act", nc.scalar, mk("s_act"))
        self.dve = Eng("dve", nc.vector, mk("s_dve"))
        self.pool = Eng("pool", nc.gpsimd, mk("s_pool"))
        self.sp = Eng("sp", nc.sync, mk("s_sp"))
        self.engs = [self.pe, self.act, self.dve, self.pool, self.sp]
        self.dsems = [mk("s_d%d" % i) for i in range(n_dsem)]
        self.wsems = [mk("s_w%d" % i) for i in range(16)]
        self.wnext = 0
        self.dnext = 0
        self.dval = {}
        self.semvals = {}

    def new_dsem(self):
        s = self.dsems[self.dnext % len(self.dsems)]
        self.dnext += 1
        return s

    def new_wsem(self):
        s = self.wsems[self.wnext % len(self.wsems)]
        self.wnext += 1
        return s

    def _waits(self, E, reads, writes):
        waits = {}
        for b in reads:
            for s, v in b.w.items():
                if waits.get(s, 0) < v:
                    waits[s] = v
        for b in writes:
            for d in (b.w, b.r):
                for s, v in d.items():
                    if waits.get(s, 0) < v:
                        waits[s] = v
        for s, v in waits.items():
            if E.seen.get(s, 0) >= v:
                continue
            if E.is_pe and s is E.sem:
                continue
            E.seen[s] = v
            E.h.wait_ge(s, v)

    def op(self, E, fn, reads=(), writes=()):
        self._waits(E, reads, writes)
        ins = fn()
        E.cnt += 1
        ins.then_inc(E.sem, 1)
        self.semvals[E.sem] = E.cnt
        for b in reads:
            b.r[E.sem] = E.cnt
        for b in writes:
            b.w = {E.sem: E.cnt}
            b.r = {}

    def dma(self, Q, sem, out, in_, reads=(), writes=()):
        self.dma_group(Q, sem, [(out, in_)], reads, writes)

    def dma_group(self, Q, sem, pairs, reads=(), writes=(), slow=False):
        self._waits(Q, reads, writes)
        v = self.dval.get(sem, 0)
        for (o, i) in pairs:
            if slow:
                Q.h.dma_start(out=o, in_=i, allow_slow_non_contiguous=True).then_inc(sem, 16)
            else:
                Q.h.dma_start(out=o, in_=i).then_inc(sem, 16)
            v += 16
        self.dval[sem] = v
        self.semvals[sem] = v
        for b in reads:
            b.r[sem] = v
        for b in writes:
            b.w = {sem: v}
            b.r = {}

    def barrier(self):
        for E in self.engs:
            for s, v in self.semvals.items():
                if E.seen.get(s, 0) >= v:
                    continue
                E.seen[s] = v
                if E.is_pe and s is E.sem:
                    continue
                E.h.wait_ge(s, v)


_UID = [0]


def uniq(name):
    _UID[0] += 1
    return "%s_%d" % (name, _UID[0])


class Ring:
    def __init__(self, fw, es, name, n, shape, dt, psum=False, dsem=False):
        nc = fw.nc
        self.t = []
        self.b = []
        self.s = []
        for i in range(n):
            if psum:
                self.t.append(es.enter_context(nc.psum_tensor(uniq(name), shape, dt)))
            else:
                self.t.append(es.enter_context(nc.sbuf_tensor(uniq(name), shape, dt)))
            self.b.append(Buf())
            self.s.append(fw.new_dsem() if dsem else None)
        self.n = n
        self.i = -1

    def next(self):
        self.i += 1
        k = self.i % self.n
        return self.t[k], self.b[k], self.s[k]


def build(SEQS, DEPTH, debug=False):
    NT = sum(SEQS)
    NTI = NT // TS
    assert all(s % 1024 == 0 for s in SEQS)
    seq_start_tiles = set()
    seq_end_tiles = set()
    off = 0
    seq_info = []
    for s in SEQS:
        seq_start_tiles.add(off // TS)
        seq_end_tiles.add((off + s) // TS - 1)
        seq_info.append((off, s))
        off += s

    nc = bass.Bass("TRN2", target_bir_lowering=False)
    I = lambda n, s, d=F32: nc.dram_tensor(n, list(s), d, kind="ExternalInput").ap()
    L = DEPTH
    x_in = I("x", [NT, D])
    W_gu = [I("ffn1_w_gu", [L, D, 2 * DFF]), I("ffn2_w_gu", [L, D, 2 * DFF])]
    W_dn = [I("ffn1_w_down", [L, DFF, D]), I("ffn2_w_down", [L, DFF, D])]
    W_in = I("w_in", [L, D, NIN])
    W_mo = [I("w_hg_out", [L, 512, D]), I("w_cv_out", [L, 512, D]), I("w_na_out", [L, 512, D])]
    W_out = I("w_out", [L, D, D])
    gains_in = I("gains", [128, 3, L, 8])
    gfin_in = I("gfin", [128, D])
    lbl_in = I("lbl", [128, 2, L, 4])
    gn_in = I("gn", [128, L, 4])
    cw_in = I("cw", [128, L, 3, 4])
    cb_in = I("cbias", [128, L, 4])
    b2g_in = I("b2g", [L, 128, 8, 9, 128])
    neg_in = I("negm", [128, 5, 5, 128])
    ident_in = I("ident", [128, 128])
    maskf_in = I("maskf", [128, 128])
    maskb_in = I("maskb", [128, 128])
    rm_in = I("rmask", [128, TS])
    y_out = nc.dram_tensor("y", [NT, D], F32, kind="ExternalOutput").ap()

    skind = "ExternalOutput" if debug else "Internal"
    S = lambda n, s, d=BF16: nc.dram_tensor(n, list(s), d, kind=skind).ap()
    XS = S("xs", [NT, D], F32)
    HT = S("ht", [22, 128, NT])
    HQ = S("hq", [2, 2, 4, 128, NT])
    KB = S("kb", [2, 4, NT, 128])
    HV = S("hv", [NT, 512])
    DCH = S("dch", [2, 128, 4, NT // 64], F32)
    HGT = S("hgt", [4, 128, NT])
    ZT = S("zt", [4, 128, NT])
    CBT = S("cbt", [4, 128, NT])
    NQ = S("nq", [4, 128, NT])
    NK = S("nk", [4, 128, NT])
    NV = S("nv", [NT, 512])
    OF = S("of", [4, 128, NT], F32)
    OHG = S("ohg", [4, 128, NT])
    ONA = S("ona", [4, 128, NT])

    with ExitStack() as ges:
        fw = FW(nc, ges)
        PE, ACT, DVE, POOL, SP = fw.pe, fw.act, fw.dve, fw.pool, fw.sp
        V = nc.vector
        A = nc.scalar
        T = nc.tensor
        G = nc.gpsimd
        gsb = lambda n, s, d=F32: ges.enter_context(nc.sbuf_tensor(uniq(n), list(s), d))
        cst = Buf()
        ident_f = gsb("ident_f", [128, 128])
        idb = gsb("idb", [128, 128], BF16)
        ones_b = gsb("ones_b", [128, 128], BF16)
        zeros_b = gsb("zeros_b", [128, 128], BF16)
        maskf = gsb("maskf", [128, 128])
        maskb = gsb("maskb", [128, 128])
        rmk = gsb("rmk", [128, TS])
        gains = gsb("gains", [128, 3, L, 8])
        gn = gsb("gn", [128, L, 4])
        cw = gsb("cw", [128, L, 3, 4])
        cbias = gsb("cbias", [128, L, 4])
        lbl = gsb("lbl", [128, 2, L, 4])
        lbe = gsb("lbe", [128, 2, L, 4])
        lbs = gsb("lbs", [128, 2, 4])
        lb = gsb("lb", [128, 2, L, 4])
        oml = gsb("oml", [128, 2, L, 4])
        csem = fw.new_dsem()
        fw.dma_group(SP, csem, [(ident_f[:], ident_in), (maskf[:], maskf_in), (maskb[:], maskb_in), (rmk[:], rm_in),
                                (gains[:], gains_in), (gn[:], gn_in), (cw[:], cw_in), (cbias[:], cb_in), (lbl[:], lbl_in)],
                     writes=[cst])
        c2 = Buf()
        fw.op(DVE, lambda: V.tensor_copy(out=idb[:], in_=ident_f[:]), [cst], [c2])
        fw.op(DVE, lambda: V.memset(ones_b[:], 1.0), [], [c2])
        fw.op(DVE, lambda: V.memset(zeros_b[:], 0.0), [], [c2])
        fw.op(ACT, lambda: A.activation(out=lbe[:], in_=lbl[:], func=AF.Exp), [cst], [c2])
        fw.op(DVE, lambda: V.memset(lb[:], 0.0), [], [c2])
        fw.op(DVE, lambda: V.tensor_copy(out=lbs[:], in_=lbe[:, :, 0, :]), [c2], [c2])
        for l in range(1, L):
            fw.op(DVE, lambda l=l: V.tensor_tensor(out=lbs[:], in0=lbs[:], in1=lbe[:, :, l, :], op=ALU.add), [c2], [c2])
        fw.op(DVE, lambda: V.reciprocal(out=lbs[:], in_=lbs[:]), [c2], [c2])
        for l in range(1, L):
            fw.op(DVE, lambda l=l: V.tensor_tensor(out=lbe[:, :, l, :], in0=lbe[:, :, l, :], in1=lbs[:], op=ALU.mult), [c2], [c2])
            fw.op(DVE, lambda l=l: V.tensor_tensor(out=lb[:, :, l, :], in0=lb[:, :, l - 1, :], in1=lbe[:, :, l, :], op=ALU.add), [c2], [c2])
        fw.op(DVE, lambda: V.tensor_scalar(out=oml[:], in0=lb[:], scalar1=-1.0, scalar2=1.0, op0=ALU.mult, op1=ALU.add), [c2], [c2])
        fw.barrier()

        def load_weights(es, name, kch, ncols, src_fn, order, wt=None):
            if wt is None:
                wt = es.enter_context(nc.sbuf_tensor(uniq(name), [128, kch, ncols], BF16))
            bufs = {}
            groups = [order[:1], order[1:4], order[4:]]
            for g in groups:
                if not g:
                    continue
                sem = fw.new_wsem()
                bl = []
                pairs = []
                for blk in g:
                    c0 = blk * 512
                    c1 = min(ncols, c0 + 512)
                    b = Buf()
                    bufs[blk] = b
                    bl.append(b)
                    pairs.append((wt[:, :, c0:c1], src_fn(c0, c1).rearrange("(k p) n -> p k n", p=128)))
                fw.dma_group(POOL, sem, pairs, writes=bl)
            return wt, bufs

        def pro_load(env, t, xsrc):
            xt, bx, sx = env["xr"].next()
            fw.dma(SP, sx, xt[:], xsrc[t * TS:(t + 1) * TS, :].rearrange("(s p) d -> p s d", p=128), writes=[bx])
            return xt, bx, sx

        def prologue(env, loaded, gain_ap):
            xt, bx, sx = loaded
            ss, bss, _ = env["ssr"].next()
            fw.op(DVE, lambda: V.memset(ss[:], 0.0), [], [bss])
            xns = []
            for s in range(4):
                xn, bxn, _ = env["xnr"].next()
                xns.append((xn, bxn))
            for s in range(4):
                fw.op(ACT, lambda s=s: A.activation(out=xns[s][0][:], in_=xt[:, s, :], func=AF.Square, accum_out=ss[:, s:s + 1]),
                      [bx], [xns[s][1], bss])
            fw.op(ACT, lambda: A.activation(out=ss[:, 4:8], in_=ss[:, 0:4], func=AF.Ln, scale=1.0 / D, bias=EPS), [bss], [bss])
            fw.op(ACT, lambda: A.activation(out=ss[:, 8:12], in_=ss[:, 4:8], func=AF.Exp, scale=-0.5), [bss], [bss])
            xnT, _, _ = env["xntr"].next()
            bxt = env["xntb"][env["xntr"].i % 2]
            for s in range(4):
                xn, bxn = xns[s]
                fw.op(DVE, lambda s=s, xn=xn: V.tensor_scalar(out=xn[:], in0=xt[:, s, :], scalar1=ss[:, 8 + s:9 + s], scalar2=None,
                                                        op0=ALU.mult), [bx, bss], [bxn])
            for c in range(8):
                pt, bpt, _ = env["ptr"].next()

                def ftr(c=c, pt=pt):
                    for s in range(4):
                        ins = T.transpose(pt[:, s * 128:(s + 1) * 128], xns[s][0][:, c * 128:(c + 1) * 128], idb[:])
                    return ins
                fw.op(PE, ftr, [b for (_, b) in xns], [bpt])
                if c % 2 == 0:
                    fw.op(ACT, lambda c=c, pt=pt: A.activation(out=xnT[:, c, :], in_=pt[:], func=AF.Copy, scale=gain_ap[:, c:c + 1]),
                          [bpt], [bxt[0]])
                else:
                    fw.op(DVE, lambda c=c, pt=pt: V.tensor_scalar(out=xnT[:, c, :], in0=pt[:], scalar1=gain_ap[:, c:c + 1], scalar2=None,
                                                                 op0=ALU.mult), [bpt], [bxt[1]])
            return xt, bx, sx, xnT, bxt

        def pipelined_pro(tiles, env, xsrc, gain, body_fn):
            lds = {0: pro_load(env, tiles[0], xsrc)}
            if len(tiles) > 1:
                lds[1] = pro_load(env, tiles[1], xsrc)
            pros = {0: prologue(env, lds[0], gain)}
            for i, t in enumerate(tiles):
                def hook(i=i):
                    if i + 1 < len(tiles) and (i + 1) not in pros:
                        pros[i + 1] = prologue(env, lds[i + 1], gain)
                body_fn(t, pros[i], hook)
                hook()
                if i + 2 < len(tiles):
                    lds[i + 2] = pro_load(env, tiles[i + 2], xsrc)

        def pipelined(tiles, load_fn, body_fn):
            nxt = load_fn(tiles[0])
            for i, t in enumerate(tiles):
                cur = nxt
                if i + 1 < len(tiles):
                    nxt = load_fn(tiles[i + 1])
                body_fn(t, cur)

        def prologue_env(es, xring=2, xnring=4):
            env = {}
            env["xr"] = Ring(fw, es, "xt", xring, [128, 4, D], F32, dsem=True)
            env["ssr"] = Ring(fw, es, "ss", 2, [128, 12], F32)
            env["xnr"] = Ring(fw, es, "xn", xnring, [128, D], BF16)
            env["xntr"] = Ring(fw, es, "xnT", 2, [128, 8, TS], BF16)
            env["xntb"] = [[Buf(), Buf()], [Buf(), Buf()]]
            env["ptr"] = Ring(fw, es, "ptr", 2, [128, TS], BF16, psum=True)
            return env

        def ffn_gu(l, which, xsrc, outer):
            with ExitStack() as es:
                order = [0, 5, 6, 1, 7, 2, 8, 3, 9, 4, 10]
                wt, wb = load_weights(es, "wgu", 8, 2 * DFF, lambda c0, c1: W_gu[which][l, :, c0:c1], order)
                outer["wdn"] = load_weights(None, "wdn", 22, D, lambda c0, c1: W_dn[which][l, :, c0:c1], [0, 1], wt=outer["wdn_t"])
                env = prologue_env(es)
                pa = Ring(fw, es, "pa", 3, [128, TS], F32, psum=True)
                pb = Ring(fw, es, "pb", 3, [128, TS], F32, psum=True)
                sar = Ring(fw, es, "sa", 3, [128, TS], F32)
                hr = Ring(fw, es, "hst", 4, [128, TS], BF16, dsem=True)
                gain = gains[:, 0 if which == 0 else 2, l, :]
                def body(t, pro, hook):
                    xt, bx, sx, xnT, bxt = pro
                    for j in range(22):
                        if j == 17:
                            hook()
                        A_, bA, _ = pa.next()
                        B_, bB, _ = pb.next()
                        ca = j * 128
                        cb = DFF + j * 128

                        def fmm(P_, c0):
                            for k in range(8):
                                ins = T.matmul(P_[:], lhsT=wt[:, k, c0:c0 + 128], rhs=xnT[:, k, :], start=(k == 0), stop=(k == 7))
                            return ins
                        fw.op(PE, lambda: fmm(A_, ca), bxt + [wb[ca // 512]], [bA])
                        fw.op(PE, lambda: fmm(B_, cb), bxt + [wb[cb // 512]], [bB])
                        sa, bsa, _ = sar.next()
                        fw.op(ACT, lambda: A.activation(out=sa[:], in_=A_[:], func=AF.Silu), [bA], [bsa])
                        h, bh, sh = hr.next()
                        fw.op(DVE, lambda: V.tensor_tensor(out=h[:], in0=B_[:], in1=sa[:], op=ALU.mult), [bB, bsa], [bh])
                        fw.dma(SP, sh, HT[j, :, t * TS:(t + 1) * TS], h[:], reads=[bh])
                pipelined_pro(list(range(NTI)), env, xsrc, gain, body)
                fw.barrier()

        def ffn_down(l, which, xsrc, xdst, outer):
            with ExitStack() as es:
                wt, wb = outer["wdn"]
                xr = Ring(fw, es, "xt", 2, [128, 4, D], F32, dsem=True)
                hr = Ring(fw, es, "hT", 2, [128, 22, TS], BF16, dsem=True)
                po = Ring(fw, es, "po", 3, [128, TS], F32, psum=True)
                def load(t):
                    xt, bx, sx = xr.next()
                    fw.dma(SP, sx, xt[:], xsrc[t * TS:(t + 1) * TS, :].rearrange("(s p) d -> p s d", p=128), writes=[bx])
                    hT, bh, sh = hr.next()
                    fw.dma(SP, sh, hT[:], HT[:, :, t * TS:(t + 1) * TS].rearrange("j p n -> p j n"), writes=[bh])
                    return xt, bx, sx, hT, bh, sh

                def body(t, loaded):
                    xt, bx, sx, hT, bh, sh = loaded
                    for s in range(4):
                        for hf in range(2):
                            P_, bP, _ = po.next()

                            def fmm():
                                for j in range(22):
                                    ins = T.matmul(P_[:], lhsT=hT[:, j, s * 128:(s + 1) * 128], rhs=wt[:, j, hf * 512:(hf + 1) * 512],
                                                   start=(j == 0), stop=(j == 21))
                                return ins
                            fw.op(PE, fmm, [bh, wb[hf]], [bP])
                            fw.op(DVE, lambda: V.scalar_tensor_tensor(out=xt[:, s, hf * 512:(hf + 1) * 512], in0=P_[:], scalar=0.5,
                                                                      in1=xt[:, s, hf * 512:(hf + 1) * 512], op0=ALU.mult, op1=ALU.add),
                                  [bP, bx], [bx])
                    fw.dma(SP, sx, xdst[t * TS:(t + 1) * TS, :].rearrange("(s p) d -> p s d", p=128), xt[:], reads=[bx])
                pipelined(list(range(NTI)), load, body)
                fw.barrier()

        def mixer_a(l):
            with ExitStack() as es:
                order = list(range(11))
                wt, wb = load_weights(es, "wina", 8, 5632, lambda c0, c1: W_in[l, :, c0:c1], order)
                env = prologue_env(es)
                pp = Ring(fw, es, "pp", 6, [128, TS], F32, psum=True)
                tmp = Ring(fw, es, "tmp", 2, [128, TS], F32)
                st = Ring(fw, es, "st", 4, [128, TS], BF16, dsem=True)
                stq = Ring(fw, es, "stq", 2, [128, 4, TS], BF16, dsem=True)
                big = lambda n, d=F32: es.enter_context(nc.sbuf_tensor(uniq(n), [128, 4, TS], d))
                TA, TB, TC, TE = big("TA"), big("TB"), big("TC"), big("TE")
                QS, KBF = big("QS", BF16), big("KBF", BF16)
                bTA, bTB, bTC, bTE, bQS, bKBF = Buf(), Buf(), Buf(), Buf(), Buf(), Buf()
                fl = lambda X: X[:].rearrange("p h n -> p (h n)")
                db = es.enter_context(nc.sbuf_tensor(uniq("dbx"), [128, 2, 32], F32))
                bdb = Buf()
                dst = Ring(fw, es, "dst", 2, [128, 2, 4, 8], F32, dsem=True)
                gain = gains[:, 1, l, :]
                cnt = [0]

                def evac(eng_alt, out_ap, in_ap, rd, wr, func=None):
                    cnt[0] += 1
                    if func is not None or cnt[0] % 2 == 0:
                        fw.op(ACT, lambda: A.activation(out=out_ap, in_=in_ap, func=(func or AF.Copy)), rd, wr)
                    else:
                        fw.op(DVE, lambda: V.tensor_copy(out=out_ap, in_=in_ap), rd, wr)

                def fm_chunk(xnT, bxt, col):
                    P_, bP, _ = pp.next()

                    def f():
                        for k in range(8):
                            ins = T.matmul(P_[:], lhsT=wt[:, k, col:col + 128], rhs=xnT[:, k, :], start=(k == 0), stop=(k == 7))
                        return ins
                    fw.op(PE, f, bxt + [wb[col // 512]], [bP])
                    return P_, bP

                def tm_proj(xnT, bxt, col, dst_ap_fn):
                    for s in range(4):
                        P_, bP, _ = pp.next()

                        def f():
                            for k in range(8):
                                ins = T.matmul(P_[:], lhsT=xnT[:, k, s * 128:(s + 1) * 128], rhs=wt[:, k, col:col + 512],
                                               start=(k == 0), stop=(k == 7))
                            return ins
                        fw.op(PE, f, bxt + [wb[col // 512]], [bP])
                        o, bo, so = st.next()
                        evac(None, o[:], P_[:], [bP], [bo])
                        fw.dma(SP, so, dst_ap_fn(s), o[:], reads=[bo])

                def body(t, pro, hook):
                    tok = slice(t * TS, (t + 1) * TS)
                    xt, bx, sx, xnT, bxt = pro
                    jobs = []

                    def tm_job(col, dstT, s_):
                        def f():
                            P_, bP, _ = pp.next()

                            def fmm():
                                for k in range(8):
                                    ins = T.matmul(P_[:], lhsT=xnT[:, k, s_ * 128:(s_ + 1) * 128], rhs=wt[:, k, col:col + 512],
                                                   start=(k == 0), stop=(k == 7))
                                return ins
                            fw.op(PE, fmm, bxt + [wb[col // 512]], [bP])
                            o, bo, so = st.next()
                            evac(None, o[:], P_[:], [bP], [bo])
                            fw.dma(SP, so, dstT[t * TS + s_ * 128: t * TS + (s_ + 1) * 128, :], o[:], reads=[bo])
                        return f

                    def plain_job(base, dstT, func, c):
                        def f():
                            P_, bP = fm_chunk(xnT, bxt, base + c * 128)
                            o, bo, so = st.next()
                            evac(None, o[:], P_[:], [bP], [bo], func)
                            fw.dma(SP, so, dstT[c, :, tok], o[:], reads=[bo])
                        return f

                    def conv_job(c):
                        def f():
                            Pa, bPa = fm_chunk(xnT, bxt, 2560 + c * 128)
                            Pc, bPc = fm_chunk(xnT, bxt, 3584 + c * 128)
                            ta, bta, _ = tmp.next()
                            fw.op(ACT, lambda: A.activation(out=ta[:], in_=Pa[:], func=AF.Copy), [bPa], [bta])
                            o, bo, so = st.next()
                            fw.op(DVE, lambda: V.tensor_tensor(out=o[:], in0=Pc[:], in1=ta[:], op=ALU.mult), [bPc, bta], [bo])
                            fw.dma(SP, so, ZT[c, :, tok], o[:], reads=[bo])
                        return f
                    for s_ in range(4):
                        jobs.append(tm_job(512, HV, s_))
                    for s_ in range(4):
                        jobs.append(tm_job(5120, NV, s_))
                    for (base, dstT, func) in ((4096, NQ, None), (4608, NK, None), (3072, CBT, None), (2048, HGT, AF.Silu)):
                        for c in range(4):
                            jobs.append(plain_job(base, dstT, func, c))
                    for c in range(4):
                        jobs.append(conv_job(c))

                    def pump():
                        if jobs:
                            jobs.pop(0)()
                    dt_, bdt, sdt = dst.next()
                    for h in range(4):
                        Pq, bPq = fm_chunk(xnT, bxt, h * 128)
                        fw.op(ACT, lambda: A.activation(out=QS[:, h, :], in_=Pq[:], func=AF.Copy), [bPq], [bQS])
                    for dr in range(2):
                        if dr == 1:
                            hook()
                        for h in range(4):
                            Pz, bPz = fm_chunk(xnT, bxt, 1024 + dr * 512 + h * 128)
                            fw.op(ACT, lambda: A.activation(out=TA[:, h, :], in_=Pz[:], func=AF.Sigmoid), [bPz], [bTA])
                        for h in range(4):
                            fw.op(DVE, lambda h=h: V.tensor_scalar(out=TB[:, h, :], in0=TA[:, h, :], scalar1=oml[:, dr, l, h:h + 1],
                                                                  scalar2=lb[:, dr, l, h:h + 1], op0=ALU.mult, op1=ALU.add), [bTA], [bTB])
                        fw.op(DVE, lambda: V.tensor_scalar(out=fl(TB), in0=fl(TB), scalar1=1e-30, scalar2=None, op0=ALU.max), [bTB], [bTB])
                        pump()
                        fw.op(ACT, lambda: A.activation(out=fl(TA), in_=fl(TB), func=AF.Ln), [bTB], [bTA])
                        pump()
                        fw.op(DVE, lambda: V.tensor_scalar(out=fl(TB), in0=fl(TB), scalar1=-1.0, scalar2=1.0, op0=ALU.mult, op1=ALU.add),
                              [bTB, bTA], [bTB])
                        for h in range(4):
                            fw.op(DVE, lambda h=h: V.tensor_tensor_scan(out=TC[:, h, :], data0=rmk[:], data1=TA[:, h, :], initial=0.0,
                                                                       op0=ALU.mult, op1=ALU.add), [bTA], [bTC])
                            pump()
                        c3 = lambda X: fl(X).rearrange("p (g j) -> p g j", j=64)
                        if dr == 0:
                            bt, bb = TC, bTC
                            bend = c3(TC)[:, :, 63:64]
                        else:
                            fw.op(DVE, lambda: V.tensor_tensor(out=fl(TA), in0=fl(TA), in1=fl(TC), op=ALU.subtract), [bTA, bTC], [bTA])
                            fw.op(DVE, lambda: V.tensor_tensor(out=c3(TA), in0=c3(TA), in1=c3(TC)[:, :, 63:64].to_broadcast([128, 32, 64]),
                                                               op=ALU.add), [bTA, bTC], [bTA])
                            bt, bb = TA, bTA
                            bend = c3(TA)[:, :, 0:1]
                        TX, bTX = (TA, bTA) if dr == 0 else (TC, bTC)
                        fw.op(ACT, lambda: A.activation(out=fl(TE), in_=fl(bt), func=AF.Exp), [bb], [bTE])
                        fw.op(ACT, lambda: A.activation(out=fl(TX), in_=fl(bt), func=AF.Exp, scale=-1.0), [bb], [bTX])
                        fw.op(ACT, lambda: A.activation(out=db[:, dr, :], in_=bend.rearrange("p g o -> p (g o)"), func=AF.Exp), [bb], [bdb])
                        pump()
                        o, bo, so = stq.next()
                        fw.op(DVE, lambda: V.tensor_tensor(out=fl(o), in0=fl(QS), in1=fl(TE), op=ALU.mult), [bQS, bTE], [bo])
                        fw.dma(SP, so, HQ[dr, 0, :, :, tok].rearrange("h p n -> p h n"), o[:], reads=[bo])
                        pump()
                        fw.op(DVE, lambda: V.tensor_tensor(out=fl(TX), in0=fl(TB), in1=fl(TX), op=ALU.mult), [bTB, bTX], [bTX])
                        pump()
                        o2, bo2, so2 = stq.next()
                        fw.op(ACT, lambda: A.activation(out=fl(o2), in_=fl(TX), func=AF.Copy), [bTX], [bo2])
                        fw.dma(SP, so2, HQ[dr, 1, :, :, tok].rearrange("h p n -> p h n"), o2[:], reads=[bo2])
                        fw.op(DVE, lambda: V.tensor_tensor(out=c3(KBF), in0=c3(TX), in1=db[:, dr, :].rearrange("p (g o) -> p g o", o=1).to_broadcast([128, 32, 64]),
                                                           op=ALU.mult), [bTX, bdb], [bKBF])
                        pump()
                        if dr == 0:
                            fw.op(ACT, lambda: A.activation(out=dt_[:, dr, :, :].rearrange("p h c -> p (h c)"), in_=db[:, dr, :], func=AF.Copy), [bdb], [bdt])
                        else:
                            d4 = db[:, dr, :].rearrange("p (h c) -> p h c", h=4)
                            for cc in range(8):
                                fw.op(ACT, lambda cc=cc: A.activation(out=dt_[:, dr, :, 7 - cc:8 - cc], in_=d4[:, :, cc:cc + 1], func=AF.Copy), [bdb], [bdt])
                        pump()
                        for h in range(4):
                            pt, bpt, _ = env["ptr"].next()

                            def ftr():
                                for s in range(4):
                                    ins = T.transpose(pt[:, s * 128:(s + 1) * 128], KBF[:, h, s * 128:(s + 1) * 128], idb[:])
                                return ins
                            fw.op(PE, ftr, [bKBF], [bpt])
                            o3, bo3, so3 = st.next()
                            evac(None, o3[:], pt[:], [bpt], [bo3])
                            fw.dma(SP, so3, KB[dr, h, tok, :].rearrange("(s p) k -> p s k", p=128),
                                   o3[:].rearrange("p (s k) -> p s k", k=128), reads=[bo3])
                    while jobs:
                        pump()
                    hook()
                    fw.dma(SP, sdt, DCH[0, :, :, t * 8:(t + 1) * 8], dt_[:, 0, :, :], reads=[bdt])
                    (sq0, sqn) = [(a // TS, b // TS) for (a, b) in seq_info if a // TS <= t < (a + b) // TS][0]
                    pos0 = sq0 * 8 + (sqn - 1 - (t - sq0)) * 8
                    fw.dma(SP, sdt, DCH[1, :, :, pos0:pos0 + 8], dt_[:, 1, :, :], reads=[bdt])
                pipelined_pro(list(range(NTI)), env, XS, gain, body)
                fw.barrier()

        def mixer_h(l):
            TM = max(SEQS)
            NCM = TM // 64
            NSM = TM // 128
            with ExitStack() as es:
                qr = Ring(fw, es, "hq_", 2, [128, TM], BF16, dsem=True)
                kr = Ring(fw, es, "hk_", 2, [128, TM], BF16, dsem=True)
                kbr = Ring(fw, es, "kb_", 2, [128, 2, NSM, 128], BF16, dsem=True)
                for i_ in range(2):
                    fw.op(POOL, lambda i_=i_: G.memset(kbr.t[i_][:], 0.0), [], [kbr.b[i_]])
                vr = Ring(fw, es, "hv_", 2, [128, NSM, 128], BF16, dsem=True)
                dcr = Ring(fw, es, "dc_", 2, [128, NCM], F32, dsem=True)
                dxr = Ring(fw, es, "dx_", 2, [128, 16 * NCM], F32)
                hgr = Ring(fw, es, "hg_", 2, [128, 512], BF16, dsem=True)
                ohr = Ring(fw, es, "oh_", 2, [128, 512], BF16, dsem=True)
                Uall = es.enter_context(nc.sbuf_tensor(uniq("Uall"), [128, 128 * NCM], F32))
                Sall = es.enter_context(nc.sbuf_tensor(uniq("Sall"), [128, 128 * NCM], BF16))
                of = es.enter_context(nc.sbuf_tensor(uniq("ofa"), [128, TM], F32))
                atrr = Ring(fw, es, "atraw", 2, [128, 4, 128], BF16)
                atm = es.enter_context(nc.sbuf_tensor(uniq("atm"), [128, NSM, 128], BF16))
                mkb = es.enter_context(nc.sbuf_tensor(uniq("mkb"), [128, 2, 128], BF16))
                bmk = Buf()
                fw.op(DVE, lambda: V.tensor_copy(out=mkb[:, 0, :], in_=maskf[:]), [], [bmk])
                fw.op(DVE, lambda: V.tensor_copy(out=mkb[:, 1, :], in_=maskb[:]), [bmk], [bmk])
                sqr = Ring(fw, es, "sq", 2, [128, 512], BF16)
                rsr = Ring(fw, es, "rs", 2, [128, 512], F32)
                pU = Ring(fw, es, "pU", 3, [128, 4, 128], F32, psum=True)
                pA = Ring(fw, es, "pA", 2, [128, 4, 128], F32, psum=True)
                pO = Ring(fw, es, "pO", 2, [128, 4, 128], F32, psum=True)
                pN = Ring(fw, es, "pN", 1, [128, 512], F32, psum=True)
                bU = [Buf() for _ in range(NCM // 4)]
                bS = [Buf() for _ in range(8)]
                bAr = [Buf() for _ in range(NSM // 4)]
                bAm = [Buf() for _ in range(NSM // 4)]
                bof = [Buf() for _ in range(TM // 512)]
                ev = [0]

                def load_unit(T0, TL, h, dr):
                    NC = TL // 64
                    NS = TL // 128
                    qh, bq, sq_ = qr.next()
                    fw.dma(SP, sq_, qh[:, 0:TL], HQ[dr, 0, h, :, T0:T0 + TL], writes=[bq])
                    kh, bk, sk_ = kr.next()
                    fw.dma(SP, sk_, kh[:, 0:TL], HQ[dr, 1, h, :, T0:T0 + TL], writes=[bk])
                    kb, bkb, skb = kbr.next()
                    kbv = KB[dr, h, T0:T0 + TL, :].rearrange("(s p) k -> p s k", p=128)
                    fw.dma_group(SP, skb, [(kb[0:64, 0, 0:NS, :], kbv[0:64]), (kb[64:128, 1, 0:NS, :], kbv[64:128])], writes=[bkb])
                    dc, bdc, sdc = dcr.next()
                    fw.dma(SP, sdc, dc[:, 0:NC], DCH[dr, :, h, T0 // 64:T0 // 64 + NC], writes=[bdc])
                    return (qh, bq, kh, bk, kb, bkb, dc, bdc)

                units = [(T0, TL, h, dr) for (T0, TL) in seq_info for h in range(4) for dr in range(2)]
                nxt = load_unit(*units[0])
                for ui, (T0, TL, h, dr) in enumerate(units):
                    NC = TL // 64
                    NS = TL // 128
                    (qh, bq, kh, bk, kb, bkb, dc, bdc) = nxt
                    if dr == 0:
                        vt, bv, sv = vr.next()
                        fw.dma(SP, sv, vt[:, 0:NS, :], HV[T0:T0 + TL, h * 128:(h + 1) * 128].rearrange("(s p) v -> p s v", p=128), writes=[bv])
                    if ui + 1 < len(units):
                        nxt = load_unit(*units[ui + 1])
                    Uv = Uall[:, 0:128 * NC].rearrange("p (v c) -> p v c", c=NC)
                    Sv = Sall[:, 0:128 * NC].rearrange("p (v c) -> p v c", c=NC)
                    dx, bdx, _ = dxr.next()
                    fw.op(DVE, lambda: V.memset(dc[:, 0:1], 0.0), [bdc], [bdc])
                    fw.op(DVE, lambda: V.tensor_copy(out=dx[:, 0:16 * NC].rearrange("p (v c) -> p v c", c=NC),
                                                     in_=dc[:, 0:NC].rearrange("p (o c) -> p o c", o=1).to_broadcast([128, 16, NC])), [bdc], [bdx])
                    import os as _os
                    KH = int(_os.environ.get("KH", "9"))
                    for g in range(NC // 4 if KH >= 1 else 0):
                        PU, bPU, _ = pU.next()

                        def fU():
                            for j in range(4):
                                pos = 4 * g + j
                                c = pos if dr == 0 else NC - 1 - pos
                                s_, cc = c // 2, c % 2
                                ins = T.matmul(PU[:, j, :], lhsT=kb[:, cc, s_, :], rhs=vt[:, s_, :], start=True, stop=True)
                            return ins
                        KM = int(_os.environ.get("KM", "3"))
                        if KM & 1:
                            fw.op(PE, fU, [bkb, bv], [bPU])
                        ev[0] += 1
                        if not (KM & 2):
                            continue
                        dstU = Uv[:, :, 4 * g:4 * g + 4].rearrange("p v c -> p c v")
                        if int(_os.environ.get("KE", "2")) == 3:
                            dstU = Uall[:, 0:512].rearrange("p (c v) -> p c v", c=4)
                        KE = int(_os.environ.get("KE", "1"))
                        if (ev[0] % 2 == 0 and KE == 2) or KE == 1:
                            fw.op(ACT, lambda: A.activation(out=dstU, in_=PU[:], func=AF.Copy), [bPU], [bU[g]])
                        else:
                            fw.op(DVE, lambda: V.tensor_copy(out=dstU, in_=PU[:]), [bPU], [bU[g]])
                    import os as _os
                    KH = int(_os.environ.get("KH", "9"))
                    for vg in range(8 if KH >= 2 else 0):
                        fw.op(DVE, lambda vg=vg: V.tensor_tensor_scan(out=Sv[:, vg * 16:(vg + 1) * 16, :].rearrange("p v c -> p (v c)"),
                                                                     data0=dx[:, 0:16 * NC],
                                                                     data1=Uv[:, vg * 16:(vg + 1) * 16, :].rearrange("p v c -> p (v c)"),
                                                                     initial=0.0, op0=ALU.mult, op1=ALU.add),
                              bU[0:NC // 4] + [bdx], [bS[vg]])
                    for gp in range(NS // 4 if KH >= 3 else 0):
                        PA, bPA, _ = pA.next()

                        def fA():
                            for j in range(4):
                                s_ = gp * 4 + j
                                ins = T.matmul(PA[:, j, :], lhsT=kh[:, s_ * 128:(s_ + 1) * 128], rhs=qh[:, s_ * 128:(s_ + 1) * 128],
                                               start=True, stop=True)
                            return ins
                        fw.op(PE, fA, [bk, bq], [bPA])
                        atr_, bar_, _ = atrr.next()
                        fw.op(ACT, lambda: A.activation(out=atr_[:], in_=PA[:], func=AF.Copy), [bPA], [bar_])
                        fw.op(POOL, lambda: G.tensor_tensor(out=atm[:, gp * 4:gp * 4 + 4, :], in0=atr_[:],
                                                            in1=mkb[:, dr:dr + 1, :].to_broadcast([128, 4, 128]), op=ALU.mult),
                              [bar_, bmk], [bAm[gp]])
                    for gp in range(NS // 4 if KH >= 4 else 0):
                        PO, bPO, _ = pO.next()

                        def fO():
                            first = True
                            for j in range(4):
                                s_ = gp * 4 + j
                                for cc in range(2):
                                    c = 2 * s_ + cc
                                    pos = c if dr == 0 else NC - 1 - c
                                    T.matmul(PO[:, j, cc * 64:(cc + 1) * 64], lhsT=(Sv[:, :, pos - 1] if pos > 0 else zeros_b[:]),
                                             rhs=qh[:, s_ * 128 + cc * 64:s_ * 128 + (cc + 1) * 64], start=first, stop=False)
                                    first = False
                            for j in range(4):
                                s_ = gp * 4 + j
                                ins = T.matmul(PO[:, j, :], lhsT=vt[:, s_, :], rhs=atm[:, s_, :], start=first, stop=(j == 3))
                                first = False
                            return ins
                        fw.op(PE, fO, bS + [bq, bv, bAm[gp]], [bPO])
                        ofs = of[:, gp * 512:(gp + 1) * 512]
                        if dr == 0:
                            fw.op(ACT, lambda: A.activation(out=ofs, in_=PO[:].rearrange("p j t -> p (j t)"), func=AF.Copy), [bPO], [bof[gp]])
                        else:
                            fw.op(DVE, lambda: V.tensor_tensor(out=ofs, in0=PO[:].rearrange("p j t -> p (j t)"), in1=ofs, op=ALU.add),
                                  [bPO, bof[gp]], [bof[gp]])
                    if dr == 1 and KH >= 5:
                        for gp in range(NS // 4):
                            cs = slice(gp * 512, (gp + 1) * 512)
                            oh, boh, soh = ohr.next()
                            hg, bhg, shg = hgr.next()
                            fw.dma(SP, shg, hg[:], HGT[h, :, T0 + gp * 512:T0 + (gp + 1) * 512], writes=[bhg])
                            sq, bsq, _ = sqr.next()
                            fw.op(ACT, lambda: A.activation(out=sq[:], in_=of[:, cs], func=AF.Square), [bof[gp]], [bsq])
                            PN, bPN, _ = pN.next()
                            fw.op(PE, lambda: T.matmul(PN[:], lhsT=ones_b[:], rhs=sq[:], start=True, stop=True), [bsq], [bPN])
                            rs, brs, _ = rsr.next()
                            fw.op(ACT, lambda: A.activation(out=rs[:], in_=PN[:], func=AF.Ln, scale=1.0 / 128, bias=EPS), [bPN], [brs])
                            fw.op(ACT, lambda: A.activation(out=rs[:], in_=rs[:], func=AF.Exp, scale=-0.5), [brs], [brs])
                            fw.op(DVE, lambda: V.tensor_tensor(out=rs[:], in0=of[:, cs], in1=rs[:], op=ALU.mult), [bof[gp], brs], [brs])
                            fw.op(DVE, lambda: V.scalar_tensor_tensor(out=oh[:], in0=rs[:], scalar=gn[:, l, h:h + 1], in1=hg[:],
                                                                      op0=ALU.mult, op1=ALU.mult), [brs, bhg], [boh])
                            fw.dma(SP, soh, OHG[h, :, T0 + gp * 512:T0 + (gp + 1) * 512], oh[:], reads=[boh])
                fw.barrier()

        def mixer_n(l):
            TM = max(SEQS)
            with ExitStack() as es:
                qT = es.enter_context(nc.sbuf_tensor(uniq("naq"), [128, 4, TM], BF16))
                kT = es.enter_context(nc.sbuf_tensor(uniq("nak"), [128, 4, TM], BF16))
                vx = es.enter_context(nc.sbuf_tensor(uniq("nav"), [128, TM // 128, 8, 65], BF16))
                bmc = es.enter_context(nc.sbuf_tensor(uniq("bmc"), [128, 8, 5, 5, 128], BF16))
                bq, bk, bv, bbm, bb2 = Buf(), Buf(), Buf(), Buf(), Buf()
                sl = [fw.new_dsem() for _ in range(4)]
                with ExitStack() as es2:
                    b2 = es2.enter_context(nc.sbuf_tensor(uniq("b2s"), [128, 8, 9, 128], F32))
                    ng = es2.enter_context(nc.sbuf_tensor(uniq("ngs"), [128, 5, 5, 128], F32))
                    fw.dma_group(SP, sl[3], [(b2[:], b2g_in[l]), (ng[:], neg_in)], writes=[bb2])
                    for c in range(5):
                        fw.op(DVE, lambda c=c: V.tensor_tensor(out=bmc[:, :, c, :, :], in0=b2[:, :, 4 - c:9 - c, :],
                                                               in1=ng[:, c:c + 1, :, :].to_broadcast([128, 8, 5, 128]), op=ALU.add), [bb2], [bbm])
                    fw.barrier()
                fw.op(POOL, lambda: G.memset(vx[:, :, :, 64:65], 1.0), [], [bv])
                vtr = Ring(fw, es, "vtmp", 2, [128, 8, 512], BF16, dsem=True)
                pS = Ring(fw, es, "pS", 2, [128, 1024], F32, psum=True)
                pO = Ring(fw, es, "pO", 2, [128, 512], F32, psum=True)
                ptr = Ring(fw, es, "ptr", 1, [128, TS], BF16, psum=True)
                tr = Ring(fw, es, "nt", 3, [128, 640], F32)
                pr = Ring(fw, es, "np", 4, [128, 5, 128], BF16)
                rr = Ring(fw, es, "nr", 4, [128, 1], F32)
                otr = Ring(fw, es, "not", 2, [128, 512], BF16)
                osr = Ring(fw, es, "nos", 2, [128, 4, 128], BF16, dsem=True)
                for (T0, TL) in seq_info:
                    rows = TL // 64
                    fw.dma(SP, sl[0], qT[:, :, 0:TL], NQ[:, :, T0:T0 + TL].rearrange("c p n -> p c n"), writes=[bq])
                    fw.dma(SP, sl[1], kT[:, :, 0:TL], NK[:, :, T0:T0 + TL].rearrange("c p n -> p c n"), writes=[bk])
                    nb = TL // 128
                    for b0 in range(0, nb, 8):
                        vtm_, bvt, svt = vtr.next()
                        fw.dma(SP, svt, vtm_[:], NV[T0 + b0 * 128:T0 + (b0 + 8) * 128, :].rearrange("(b p) n -> p b n", p=128), writes=[bvt])
                        fw.op(POOL, lambda: G.tensor_copy(out=vx[:, b0:b0 + 8, :, 0:64], in_=vtm_[:].rearrange("p b (h d) -> p b h d", d=64)),
                              [bvt], [bv])
                    items = [(rp, h) for rp in range(rows // 2) for h in range(8)]

                    def geom(rp):
                        r = 2 * rp
                        re_ = min(max(r - 4, 0), rows - 10)
                        cls = {0: 0, -2: 1, -4: 2, -6: 3, -8: 4}[re_ - r]
                        return r, re_, cls

                    def stage_a(rp, h):
                        r, re_, cls = geom(rp)
                        c = h // 2
                        pp_ = slice((h % 2) * 64, (h % 2) * 64 + 64)
                        PS_, bPS, _ = pS.next()

                        def fS():
                            for j in range(5):
                                ins = T.matmul(PS_[:, j * 128:(j + 1) * 128], lhsT=kT[pp_, c, (re_ + 2 * j) * 64:(re_ + 2 * j) * 64 + 128],
                                               rhs=qT[pp_, c, r * 64:r * 64 + 128], start=True, stop=True)
                            return ins
                        fw.op(PE, fS, [bq, bk], [bPS])
                        tt, btt, _ = tr.next()
                        fw.op(DVE, lambda: V.scalar_tensor_tensor(out=tt[:], in0=PS_[:, 0:640], scalar=0.125,
                                                                  in1=bmc[:, h, cls, :, :].rearrange("p j q -> p (j q)"),
                                                                  op0=ALU.mult, op1=ALU.add), [bPS, bbm], [btt])
                        pt_, bpt_, _ = pr.next()
                        fw.op(ACT, lambda: A.activation(out=pt_[:].rearrange("p j q -> p (j q)"), in_=tt[:], func=AF.Exp), [btt], [bpt_])
                        return pt_, bpt_

                    cur_ot = [None]

                    def stage_b(rp, h, st_):
                        r, re_, cls = geom(rp)
                        pt_, bpt_ = st_
                        if h == 0:
                            cur_ot[0] = otr.next()
                        ot, bot, _ = cur_ot[0]
                        PO_, bPO, _ = pO.next()

                        def fO():
                            for j in range(5):
                                ins = T.matmul(PO_[:, 0:65], lhsT=pt_[:, j, :], rhs=vx[:, re_ // 2 + j, h, :], start=(j == 0), stop=(j == 4))
                            return ins
                        fw.op(PE, fO, [bpt_, bv], [bPO])
                        rc, brc, _ = rr.next()
                        fw.op(DVE, lambda: V.reciprocal(out=rc[:], in_=PO_[:, 64:65]), [bPO], [brc])
                        fw.op(ACT, lambda: A.activation(out=ot[:, h * 64:(h + 1) * 64], in_=PO_[:, 0:64], func=AF.Copy, scale=rc[:, 0:1]),
                              [bPO, brc], [bot])
                        if h == 7:
                            pt, bpt, _ = ptr.next()

                            def ftr():
                                for c in range(4):
                                    ins = T.transpose(pt[:, c * 128:(c + 1) * 128], ot[:, c * 128:(c + 1) * 128], idb[:])
                                return ins
                            fw.op(PE, ftr, [bot], [bpt])
                            os_, bos, sos = osr.next()
                            fw.op(DVE, lambda: V.tensor_copy(out=os_[:].rearrange("p c t -> p (c t)"), in_=pt[:]), [bpt], [bos])
                            fw.dma(SP, sos, ONA[:, :, T0 + r * 64:T0 + r * 64 + 128].rearrange("c p n -> p c n"), os_[:], reads=[bos])

                    pend = [stage_a(*items[0])]
                    if len(items) > 1:
                        pend.append(stage_a(*items[1]))
                    for i, (rp, h) in enumerate(items):
                        if i + 2 < len(items):
                            pend.append(stage_a(*items[i + 2]))
                        stage_b(rp, h, pend.pop(0))
                fw.barrier()

        def mixer_o(l):
            with ExitStack() as es:
                wg, wgb = load_weights(es, "wg", 8, 3072, lambda c0, c1: W_in[l, :, 5632 + c0:5632 + c1], list(range(6)))
                wm = []
                for m in range(3):
                    wm.append(load_weights(es, "wm%d" % m, 4, D, lambda c0, c1, m=m: W_mo[m][l, :, c0:c1], [0, 1]))
                wo, wob = load_weights(es, "wo", 8, D, lambda c0, c1: W_out[l, :, c0:c1], [0, 1])
                env = prologue_env(es)
                inr = [Ring(fw, es, "oh_", 2, [128, 4, TS], BF16, dsem=True), None, Ring(fw, es, "on_", 2, [128, 4, TS], BF16, dsem=True)]
                zr = Ring(fw, es, "z_", 1, [128, 4, TS + 2], BF16, dsem=True)
                cbr = Ring(fw, es, "cb_", 1, [128, 4, TS], BF16, dsem=True)
                ocr = Ring(fw, es, "ocv", 2, [128, 4, TS], BF16)
                pre = {}
                mTr = Ring(fw, es, "mT", 1, [128, 8, TS], BF16)
                tmp = Ring(fw, es, "tm", 5, [128, TS], F32)
                pY = Ring(fw, es, "pY", 6, [128, TS], F32, psum=True)
                pG = pY
                pX = pY
                gain = gains[:, 1, l, :]
                def pre_tile(t):
                    tok = slice(t * TS, (t + 1) * TS)
                    oh, boh, soh = inr[0].next()
                    fw.dma(SP, soh, oh[:], OHG[:, :, tok].rearrange("c p n -> p c n"), writes=[boh])
                    on, bon, son = inr[2].next()
                    fw.dma(SP, son, on[:], ONA[:, :, tok].rearrange("c p n -> p c n"), writes=[bon])
                    z, bz, sz = zr.next()
                    pairs = [(z[:, :, 1:TS + 1], ZT[:, :, tok].rearrange("c p n -> p c n"))]
                    lo = t not in seq_start_tiles
                    hi = t not in seq_end_tiles
                    if lo:
                        pairs.append((z[:, :, 0:1], ZT[:, :, t * TS - 1:t * TS].rearrange("c p n -> p c n")))
                    if hi:
                        pairs.append((z[:, :, TS + 1:TS + 2], ZT[:, :, (t + 1) * TS:(t + 1) * TS + 1].rearrange("c p n -> p c n")))
                    fw.dma_group(SP, sz, pairs, writes=[bz], slow=True)
                    if not lo:
                        fw.op(POOL, lambda: G.memset(z[:, :, 0:1], 0.0), [bz], [bz])
                    if not hi:
                        fw.op(POOL, lambda: G.memset(z[:, :, TS + 1:TS + 2], 0.0), [bz], [bz])
                    cb_, bcb, scb = cbr.next()
                    fw.dma(SP, scb, cb_[:], CBT[:, :, tok].rearrange("c p n -> p c n"), writes=[bcb])
                    oc_, boc, _ = ocr.next()
                    for c in range(4):
                        t1, b1, _ = tmp.next()
                        fw.op(DVE, lambda: V.tensor_scalar(out=t1[:], in0=z[:, c, 1:TS + 1], scalar1=cw[:, l, 1, c:c + 1], scalar2=cbias[:, l, c:c + 1],
                                                           op0=ALU.mult, op1=ALU.add), [bz], [b1])
                        fw.op(DVE, lambda: V.scalar_tensor_tensor(out=t1[:], in0=z[:, c, 0:TS], scalar=cw[:, l, 0, c:c + 1], in1=t1[:],
                                                                  op0=ALU.mult, op1=ALU.add), [bz, b1], [b1])
                        fw.op(DVE, lambda: V.scalar_tensor_tensor(out=t1[:], in0=z[:, c, 2:TS + 2], scalar=cw[:, l, 2, c:c + 1], in1=t1[:],
                                                                  op0=ALU.mult, op1=ALU.add), [bz, b1], [b1])
                        fw.op(DVE, lambda: V.tensor_tensor(out=oc_[:, c, :], in0=t1[:], in1=cb_[:, c, :], op=ALU.mult), [b1, bcb], [boc])
                    pre[t] = (oh, boh, oc_, boc, on, bon)

                def body(t, pro, hook):
                    tok = slice(t * TS, (t + 1) * TS)
                    xt, bx, sx, xnT, bxt = pro
                    if t not in pre:
                        pre_tile(t)
                    (oh, boh, oc_, boc, on, bon) = pre.pop(t)
                    srcs = [(oh, boh), (oc_, boc), (on, bon)]
                    mT, bmT, _ = mTr.next()
                    for oc in range(8):
                        if oc == 4 and t + 1 < NTI:
                            pre_tile(t + 1)
                        if oc == 6:
                            hook()
                        macc, bmacc, _ = tmp.next()
                        for m in range(3):
                            PY, bPY, _ = pY.next()
                            PG, bPG, _ = pG.next()
                            src, bsrc = srcs[m]
                            wmt, wmb = wm[m]

                            def fY():
                                for k in range(4):
                                    ins = T.matmul(PY[:], lhsT=wmt[:, k, oc * 128:(oc + 1) * 128], rhs=src[:, k, :], start=(k == 0), stop=(k == 3))
                                return ins
                            fw.op(PE, fY, [bsrc, wmb[oc // 4]], [bPY])
                            gc = m * 1024 + oc * 128

                            def fG():
                                for k in range(8):
                                    ins = T.matmul(PG[:], lhsT=wg[:, k, gc:gc + 128], rhs=xnT[:, k, :], start=(k == 0), stop=(k == 7))
                                return ins
                            fw.op(PE, fG, bxt + [wgb[gc // 512]], [bPG])
                            sg, bsg, _ = tmp.next()
                            fw.op(ACT, lambda: A.activation(out=sg[:], in_=PG[:], func=AF.Sigmoid), [bPG], [bsg])
                            if m == 0:
                                fw.op(DVE, lambda: V.tensor_tensor(out=macc[:], in0=PY[:], in1=sg[:], op=ALU.mult), [bPY, bsg], [bmacc])
                            else:
                                fw.op(DVE, lambda: V.tensor_tensor(out=sg[:], in0=PY[:], in1=sg[:], op=ALU.mult), [bPY, bsg], [bsg])
                                if m == 1:
                                    fw.op(POOL, lambda: G.tensor_tensor(out=macc[:], in0=macc[:], in1=sg[:], op=ALU.add), [bsg, bmacc], [bmacc])
                                else:
                                    fw.op(POOL, lambda: G.tensor_tensor(out=mT[:, oc, :], in0=macc[:], in1=sg[:], op=ALU.add), [bsg, bmacc], [bmT])
                    for s in range(4):
                        for hf in range(2):
                            PX, bPX, _ = pX.next()

                            def fX():
                                for k in range(8):
                                    ins = T.matmul(PX[:], lhsT=mT[:, k, s * 128:(s + 1) * 128], rhs=wo[:, k, hf * 512:(hf + 1) * 512],
                                                   start=(k == 0), stop=(k == 7))
                                return ins
                            fw.op(PE, fX, [bmT, wob[hf]], [bPX])
                            fw.op(DVE, lambda: V.tensor_tensor(out=xt[:, s, hf * 512:(hf + 1) * 512], in0=PX[:], in1=xt[:, s, hf * 512:(hf + 1) * 512],
                                                               op=ALU.add), [bPX, bx], [bx])
                    fw.dma(SP, sx, XS[tok, :].rearrange("(s p) d -> p s d", p=128), xt[:], reads=[bx])
                pipelined_pro(list(range(NTI)), env, XS, gain, body)
                fw.barrier()

        def final_norm():
            with ExitStack() as es:
                gf = es.enter_context(nc.sbuf_tensor(uniq("gf"), [128, D], F32))
                bgf = Buf()
                fw.dma(SP, fw.new_dsem(), gf[:], gfin_in, writes=[bgf])
                xr = Ring(fw, es, "xt", 2, [128, 4, D], F32, dsem=True)
                ssr = Ring(fw, es, "ss", 2, [128, 12], F32)
                jk = es.enter_context(nc.sbuf_tensor(uniq("junk"), [128, D], BF16))
                bjk = Buf()
                for t in range(NTI):
                    tok = slice(t * TS, (t + 1) * TS)
                    xt, bx, sx = xr.next()
                    fw.dma(SP, sx, xt[:], XS[tok, :].rearrange("(s p) d -> p s d", p=128), writes=[bx])
                    ss, bss, _ = ssr.next()
                    fw.op(DVE, lambda: V.memset(ss[:], 0.0), [], [bss])
                    for s in range(4):
                        fw.op(ACT, lambda s=s: A.activation(out=jk[:], in_=xt[:, s, :], func=AF.Square, accum_out=ss[:, s:s + 1]), [bx], [bjk, bss])
                    fw.op(ACT, lambda: A.activation(out=ss[:, 4:8], in_=ss[:, 0:4], func=AF.Ln, scale=1.0 / D, bias=EPS), [bss], [bss])
                    fw.op(ACT, lambda: A.activation(out=ss[:, 8:12], in_=ss[:, 4:8], func=AF.Exp, scale=-0.5), [bss], [bss])
                    for s in range(4):
                        fw.op(DVE, lambda s=s: V.scalar_tensor_tensor(out=xt[:, s, :], in0=xt[:, s, :], scalar=ss[:, 8 + s:9 + s], in1=gf[:],
                                                                     op0=ALU.mult, op1=ALU.mult), [bx, bss, bgf], [bx])
                    fw.dma(SP, sx, y_out[tok, :].rearrange("(s p) d -> p s d", p=128), xt[:], reads=[bx])
                fw.barrier()

        import os as _os
        kstop = int(_os.environ.get("KSTOP", "99"))
        phases = []
        for l in range(L):
            def ffn(l, which, xsrc):
                with ExitStack() as oes:
                    outer = {"es": oes, "wdn_t": oes.enter_context(nc.sbuf_tensor(uniq("wdn"), [128, 22, D], BF16))}
                    ffn_gu(l, which, xsrc, outer)
                    ffn_down(l, which, xsrc, XS, outer)
            phases.append(lambda l=l: ffn(l, 0, x_in if l == 0 else XS))
            phases.append(lambda l=l: mixer_a(l))
            phases.append(lambda l=l: mixer_h(l))
            phases.append(lambda l=l: mixer_n(l))
            phases.append(lambda l=l: mixer_o(l))
            phases.append(lambda l=l: ffn(l, 1, XS))
        phases.append(final_norm)
        for i, ph in enumerate(phases):
            if i < kstop:
                ph()
    return nc


def host_consts(L, na_rpb):
    c = {}
    c["ident"] = np.eye(128, dtype=np.float32)
    s = np.arange(128)[:, None]
    t = np.arange(128)[None, :]
    same = (s // 64) == (t // 64)
    c["maskf"] = (same & (s <= t)).astype(np.float32)
    c["maskb"] = (same & (s >= t)).astype(np.float32)
    rm = np.ones((128, TS), np.float32)
    rm[:, ::64] = 0.0
    c["rmask"] = rm
    kp = (np.arange(128) // 64)[:, None]
    kc = (np.arange(128) % 64)[:, None]
    qp = (np.arange(128) // 64)[None, :]
    qc = (np.arange(128) % 64)[None, :]
    dc = np.clip(kc - qc + 15, 0, 30)
    b2g = np.zeros((L, 128, 8, 9, 128), np.float32)
    for pl in range(9):
        base = 2 * pl - 1
        drr = np.clip(base + kp - qp, 0, 14)
        b2g[:, :, :, pl, :] = np.transpose(na_rpb[:, :, drr, dc], (0, 2, 1, 3))
    c["b2g"] = b2g
    cs = np.clip(qc - 8, 0, 48)
    col_ok = (kc >= cs) & (kc < cs + 16)
    neg = np.zeros((128, 5, 5, 128), np.float32)
    for cls in range(5):
        delta = -2 * cls
        for j in range(5):
            base = delta + 2 * j + 7
            dr = base + kp - qp
            if cls == 0:
                lo = 7 - qp
                hi = 14 - qp
            elif cls == 1:
                lo = 7 - (2 + qp)
                hi = 14 - (2 + qp)
            elif cls == 2:
                lo = 3 + 0 * qp
                hi = 10 + 0 * qp
            elif cls == 3:
                lo = (4 - qp) - 1
                hi = (4 - qp) + 6
            else:
                lo = (2 - qp) - 1
                hi = (2 - qp) + 6
            ok = col_ok & (dr >= lo) & (dr <= hi)
            neg[:, cls, j, :] = np.where(ok, 0.0, -1e30)
    c["negm"] = neg
    return c


def fm(a, L):
    return np.ascontiguousarray(a.reshape(L, -1, 128).transpose(2, 0, 1))


def make_inputs(inp, L):
    sh = {}
    for k in ("ffn1_w_gu", "ffn2_w_gu", "ffn1_w_down", "ffn2_w_down", "w_in", "w_hg_out", "w_cv_out", "w_na_out", "w_out"):
        sh[k] = np.ascontiguousarray(inp[k], dtype=np.float32)
    g = np.stack([fm(inp["ffn1_norm"], L), fm(inp["mix_norm"], L), fm(inp["ffn2_norm"], L)], axis=1)
    sh["gains"] = np.ascontiguousarray(g, dtype=np.float32)
    sh["gfin"] = np.ascontiguousarray(np.broadcast_to(inp["final_norm"][None, :], (128, D)), dtype=np.float32)
    lbl = inp["hg_lb_logits"].reshape(2, L, 4, 128).transpose(3, 0, 1, 2)
    sh["lbl"] = np.ascontiguousarray(lbl, dtype=np.float32)
    sh["gn"] = fm(inp["hg_out_norm"], L).astype(np.float32)
    cwt = inp["conv_w"].reshape(L, 3, 4, 128).transpose(3, 0, 1, 2)
    sh["cw"] = np.ascontiguousarray(cwt, dtype=np.float32)
    sh["cbias"] = fm(inp["conv_b"], L).astype(np.float32)
    sh.update(host_consts(L, np.asarray(inp["na_rpb"], dtype=np.float32)))
    return sh


_CACHE = {}


def kernel(**inputs):
    inp = {k: np.asarray(v) for k, v in inputs.items()}
    L = 4
    SEQS = [2048, 2048, 4096]
    key = ("full",)
    if key not in _CACHE:
        _CACHE[key] = build(SEQS, L)
    nc = _CACHE[key]
    sh = make_inputs(inp, L)
    xp = inp["x_prompt"]
    xs = inp["x_sample"]
    in_maps = []
    for c in range(8):
        xc = np.concatenate([xp[2 * c], xp[2 * c + 1], xs[c]], axis=0).astype(np.float32)
        m = dict(sh)
        m["x"] = np.ascontiguousarray(xc)
        in_maps.append(m)
    res = run_bass_kernel_spmd(nc, in_maps, core_ids=list(range(8)))
    yp = np.zeros((16, 2048, D), np.float32)
    ys = np.zeros((8, 4096, D), np.float32)
    for c in range(8):
        y = res.results[c]["y"]
        yp[2 * c] = y[0:2048]
        yp[2 * c + 1] = y[2048:4096]
        ys[c] = y[4096:8192]
    return (yp, ys)
```

```python
import numpy as np
import concourse.bass as bass
import concourse.mybir as mybir
from concourse.bass_utils import run_bass_kernel_spmd
from contextlib import ExitStack

F32 = mybir.dt.float32
BF16 = mybir.dt.bfloat16
AF = mybir.ActivationFunctionType
ALU = mybir.AluOpType

D = 1024
DFF = 2816
NIN = 8704
EPS = 1e-6
TS = 512


class Buf:
    __slots__ = ("w", "r")

    def __init__(self):
        self.w = {}
        self.r = {}


class Eng:
    def __init__(self, name, h, sem, is_pe=False):
        self.name = name
        self.h = h
        self.sem = sem
        self.cnt = 0
        self.seen = {}
        self.is_pe = is_pe


class FW:
    def __init__(self, nc, es, n_dsem=40):
        self.nc = nc
        mk = lambda n: es.enter_context(nc.semaphore(n))
        self.pe = Eng("pe", nc.tensor, mk("s_pe"), True)
        self.act = Eng("act", nc.scalar, mk("s_act"))
        self.dve = Eng("dve", nc.vector, mk("s_dve"))
        self.pool = Eng("pool", nc.gpsimd, mk("s_pool"))
        self.sp = Eng("sp", nc.sync, mk("s_sp"))
        self.engs = [self.pe, self.act, self.dve, self.pool, self.sp]
        self.dsems = [mk("s_d%d" % i) for i in range(n_dsem)]
        self.wsems = [mk("s_w%d" % i) for i in range(16)]
        self.wnext = 0
        self.dnext = 0
        self.dval = {}
        self.semvals = {}

    def new_dsem(self):
        s = self.dsems[self.dnext % len(self.dsems)]
        self.dnext += 1
        return s

    def new_wsem(self):
        s = self.wsems[self.wnext % len(self.wsems)]
        self.wnext += 1
        return s

    def _waits(self, E, reads, writes):
        waits = {}
        for b in reads:
            for s, v in b.w.items():
                if waits.get(s, 0) < v:
                    waits[s] = v
        for b in writes:
            for d in (b.w, b.r):
                for s, v in d.items():
                    if waits.get(s, 0) < v:
                        waits[s] = v
        for s, v in waits.items():
            if E.seen.get(s, 0) >= v:
                continue
            if E.is_pe and s is E.sem:
                continue
            E.seen[s] = v
            E.h.wait_ge(s, v)

    def op(self, E, fn, reads=(), writes=()):
        self._waits(E, reads, writes)
        ins = fn()
        E.cnt += 1
        ins.then_inc(E.sem, 1)
        self.semvals[E.sem] = E.cnt
        for b in reads:
            b.r[E.sem] = E.cnt
        for b in writes:
            b.w = {E.sem: E.cnt}
            b.r = {}

    def dma(self, Q, sem, out, in_, reads=(), writes=()):
        self.dma_group(Q, sem, [(out, in_)], reads, writes)

    def dma_group(self, Q, sem, pairs, reads=(), writes=(), slow=False):
        self._waits(Q, reads, writes)
        v = self.dval.get(sem, 0)
        for (o, i) in pairs:
            if slow:
                Q.h.dma_start(out=o, in_=i, allow_slow_non_contiguous=True).then_inc(sem, 16)
            else:
                Q.h.dma_start(out=o, in_=i).then_inc(sem, 16)
            v += 16
        self.dval[sem] = v
        self.semvals[sem] = v
        for b in reads:
            b.r[sem] = v
        for b in writes:
            b.w = {sem: v}
            b.r = {}

    def barrier(self):
        for E in self.engs:
            for s, v in self.semvals.items():
                if E.seen.get(s, 0) >= v:
                    continue
                E.seen[s] = v
                if E.is_pe and s is E.sem:
                    continue
                E.h.wait_ge(s, v)


_UID = [0]


def uniq(name):
    _UID[0] += 1
    return "%s_%d" % (name, _UID[0])


class Ring:
    def __init__(self, fw, es, name, n, shape, dt, psum=False, dsem=False):
        nc = fw.nc
        self.t = []
        self.b = []
        self.s = []
        for i in range(n):
            if psum:
                self.t.append(es.enter_context(nc.psum_tensor(uniq(name), shape, dt)))
            else:
                self.t.append(es.enter_context(nc.sbuf_tensor(uniq(name), shape, dt)))
            self.b.append(Buf())
            self.s.append(fw.new_dsem() if dsem else None)
        self.n = n
        self.i = -1

    def next(self):
        self.i += 1
        k = self.i % self.n
        return self.t[k], self.b[k], self.s[k]


def build(SEQS, DEPTH, debug=False):
    NT = sum(SEQS)
    NTI = NT // TS
    assert all(s % 1024 == 0 for s in SEQS)
    seq_start_tiles = set()
    seq_end_tiles = set()
    off = 0
    seq_info = []
    for s in SEQS:
        seq_start_tiles.add(off // TS)
        seq_end_tiles.add((off + s) // TS - 1)
        seq_info.append((off, s))
        off += s

    nc = bass.Bass("TRN2", target_bir_lowering=False)
    I = lambda n, s, d=F32: nc.dram_tensor(n, list(s), d, kind="ExternalInput").ap()
    L = DEPTH
    x_in = I("x", [NT, D])
    W_gu = [I("ffn1_w_gu", [L, D, 2 * DFF]), I("ffn2_w_gu", [L, D, 2 * DFF])]
    W_dn = [I("ffn1_w_down", [L, DFF, D]), I("ffn2_w_down", [L, DFF, D])]
    W_in = I("w_in", [L, D, NIN])
    W_mo = [I("w_hg_out", [L, 512, D]), I("w_cv_out", [L, 512, D]), I("w_na_out", [L, 512, D])]
    W_out = I("w_out", [L, D, D])
    gains_in = I("gains", [128, 3, L, 8])
    gfin_in = I("gfin", [128, D])
    lbl_in = I("lbl", [128, 2, L, 4])
    gn_in = I("gn", [128, L, 4])
    cw_in = I("cw", [128, L, 3, 4])
    cb_in = I("cbias", [128, L, 4])
    b2g_in = I("b2g", [L, 128, 8, 9, 128])
    neg_in = I("negm", [128, 5, 5, 128])
    ident_in = I("ident", [128, 128])
    maskf_in = I("maskf", [128, 128])
    maskb_in = I("maskb", [128, 128])
    rm_in = I("rmask", [128, TS])
    y_out = nc.dram_tensor("y", [NT, D], F32, kind="ExternalOutput").ap()

    skind = "ExternalOutput" if debug else "Internal"
    S = lambda n, s, d=BF16: nc.dram_tensor(n, list(s), d, kind=skind).ap()
    XS = S("xs", [NT, D], F32)
    HT = S("ht", [22, 128, NT])
    HQ = S("hq", [2, 2, 4, 128, NT])
    KB = S("kb", [2, 4, NT, 128])
    HV = S("hv", [NT, 512])
    DCH = S("dch", [2, 128, 4, NT // 64], F32)
    HGT = S("hgt", [4, 128, NT])
    ZT = S("zt", [4, 128, NT])
    CBT = S("cbt", [4, 128, NT])
    NQ = S("nq", [4, 128, NT])
    NK = S("nk", [4, 128, NT])
    NV = S("nv", [NT, 512])
    OF = S("of", [4, 128, NT], F32)
    OHG = S("ohg", [4, 128, NT])
    ONA = S("ona", [4, 128, NT])

    with ExitStack() as ges:
        fw = FW(nc, ges)
        PE, ACT, DVE, POOL, SP = fw.pe, fw.act, fw.dve, fw.pool, fw.sp
        V = nc.vector
        A = nc.scalar
        T = nc.tensor
        G = nc.gpsimd
        gsb = lambda n, s, d=F32: ges.enter_context(nc.sbuf_tensor(uniq(n), list(s), d))
        cst = Buf()
        ident_f = gsb("ident_f", [128, 128])
        idb = gsb("idb", [128, 128], BF16)
        ones_b = gsb("ones_b", [128, 128], BF16)
        zeros_b = gsb("zeros_b", [128, 128], BF16)
        maskf = gsb("maskf", [128, 128])
        maskb = gsb("maskb", [128, 128])
        rmk = gsb("rmk", [128, TS])
        gains = gsb("gains", [128, 3, L, 8])
        gn = gsb("gn", [128, L, 4])
        cw = gsb("cw", [128, L, 3, 4])
        cbias = gsb("cbias", [128, L, 4])
        lbl = gsb("lbl", [128, 2, L, 4])
        lbe = gsb("lbe", [128, 2, L, 4])
        lbs = gsb("lbs", [128, 2, 4])
        lb = gsb("lb", [128, 2, L, 4])
        oml = gsb("oml", [128, 2, L, 4])
        csem = fw.new_dsem()
        fw.dma_group(SP, csem, [(ident_f[:], ident_in), (maskf[:], maskf_in), (maskb[:], maskb_in), (rmk[:], rm_in),
                                (gains[:], gains_in), (gn[:], gn_in), (cw[:], cw_in), (cbias[:], cb_in), (lbl[:], lbl_in)],
                     writes=[cst])
        c2 = Buf()
        fw.op(DVE, lambda: V.tensor_copy(out=idb[:], in_=ident_f[:]), [cst], [c2])
        fw.op(DVE, lambda: V.memset(ones_b[:], 1.0), [], [c2])
        fw.op(DVE, lambda: V.memset(zeros_b[:], 0.0), [], [c2])
        fw.op(ACT, lambda: A.activation(out=lbe[:], in_=lbl[:], func=AF.Exp), [cst], [c2])
        fw.op(DVE, lambda: V.memset(lb[:], 0.0), [], [c2])
        fw.op(DVE, lambda: V.tensor_copy(out=lbs[:], in_=lbe[:, :, 0, :]), [c2], [c2])
        for l in range(1, L):
            fw.op(DVE, lambda l=l: V.tensor_tensor(out=lbs[:], in0=lbs[:], in1=lbe[:, :, l, :], op=ALU.add), [c2], [c2])
        fw.op(DVE, lambda: V.reciprocal(out=lbs[:], in_=lbs[:]), [c2], [c2])
        for l in range(1, L):
            fw.op(DVE, lambda l=l: V.tensor_tensor(out=lbe[:, :, l, :], in0=lbe[:, :, l, :], in1=lbs[:], op=ALU.mult), [c2], [c2])
            fw.op(DVE, lambda l=l: V.tensor_tensor(out=lb[:, :, l, :], in0=lb[:, :, l - 1, :], in1=lbe[:, :, l, :], op=ALU.add), [c2], [c2])
        fw.op(DVE, lambda: V.tensor_scalar(out=oml[:], in0=lb[:], scalar1=-1.0, scalar2=1.0, op0=ALU.mult, op1=ALU.add), [c2], [c2])
        fw.barrier()

        def load_weights(es, name, kch, ncols, src_fn, order, wt=None):
            if wt is None:
                wt = es.enter_context(nc.sbuf_tensor(uniq(name), [128, kch, ncols], BF16))
            bufs = {}
            groups = [order[:1], order[1:4], order[4:]]
            for g in groups:
                if not g:
                    continue
                sem = fw.new_wsem()
                bl = []
                pairs = []
                for blk in g:
                    c0 = blk * 512
                    c1 = min(ncols, c0 + 512)
                    b = Buf()
                    bufs[blk] = b
                    bl.append(b)
                    pairs.append((wt[:, :, c0:c1], src_fn(c0, c1).rearrange("(k p) n -> p k n", p=128)))
                fw.dma_group(POOL, sem, pairs, writes=bl)
            return wt, bufs

        def pro_load(env, t, xsrc):
            xt, bx, sx = env["xr"].next()
            fw.dma(SP, sx, xt[:], xsrc[t * TS:(t + 1) * TS, :].rearrange("(s p) d -> p s d", p=128), writes=[bx])
            return xt, bx, sx

        def prologue(env, loaded, gain_ap):
            xt, bx, sx = loaded
            ss, bss, _ = env["ssr"].next()
            fw.op(DVE, lambda: V.memset(ss[:], 0.0), [], [bss])
            xns = []
            for s in range(4):
                xn, bxn, _ = env["xnr"].next()
                xns.append((xn, bxn))
            for s in range(4):
                fw.op(ACT, lambda s=s: A.activation(out=xns[s][0][:], in_=xt[:, s, :], func=AF.Square, accum_out=ss[:, s:s + 1]),
                      [bx], [xns[s][1], bss])
            fw.op(ACT, lambda: A.activation(out=ss[:, 4:8], in_=ss[:, 0:4], func=AF.Ln, scale=1.0 / D, bias=EPS), [bss], [bss])
            fw.op(ACT, lambda: A.activation(out=ss[:, 8:12], in_=ss[:, 4:8], func=AF.Exp, scale=-0.5), [bss], [bss])
            xnT, _, _ = env["xntr"].next()
            bxt = env["xntb"][env["xntr"].i % 2]
            for s in range(4):
                xn, bxn = xns[s]
                fw.op(DVE, lambda s=s, xn=xn: V.tensor_scalar(out=xn[:], in0=xt[:, s, :], scalar1=ss[:, 8 + s:9 + s], scalar2=None,
                                                        op0=ALU.mult), [bx, bss], [bxn])
            for c in range(8):
                pt, bpt, _ = env["ptr"].next()

                def ftr(c=c, pt=pt):
                    for s in range(4):
                        ins = T.transpose(pt[:, s * 128:(s + 1) * 128], xns[s][0][:, c * 128:(c + 1) * 128], idb[:])
                    return ins
                fw.op(PE, ftr, [b for (_, b) in xns], [bpt])
                if c % 2 == 0:
                    fw.op(ACT, lambda c=c, pt=pt: A.activation(out=xnT[:, c, :], in_=pt[:], func=AF.Copy, scale=gain_ap[:, c:c + 1]),
                          [bpt], [bxt[0]])
                else:
                    fw.op(DVE, lambda c=c, pt=pt: V.tensor_scalar(out=xnT[:, c, :], in0=pt[:], scalar1=gain_ap[:, c:c + 1], scalar2=None,
                                                                 op0=ALU.mult), [bpt], [bxt[1]])
            return xt, bx, sx, xnT, bxt

        def pipelined_pro(tiles, env, xsrc, gain, body_fn):
            lds = {0: pro_load(env, tiles[0], xsrc)}
            if len(tiles) > 1:
                lds[1] = pro_load(env, tiles[1], xsrc)
            pros = {0: prologue(env, lds[0], gain)}
            for i, t in enumerate(tiles):
                def hook(i=i):
                    if i + 1 < len(tiles) and (i + 1) not in pros:
                        pros[i + 1] = prologue(env, lds[i + 1], gain)
                body_fn(t, pros[i], hook)
                hook()
                if i + 2 < len(tiles):
                    lds[i + 2] = pro_load(env, tiles[i + 2], xsrc)

        def pipelined(tiles, load_fn, body_fn):
            nxt = load_fn(tiles[0])
            for i, t in enumerate(tiles):
                cur = nxt
                if i + 1 < len(tiles):
                    nxt = load_fn(tiles[i + 1])
                body_fn(t, cur)

        def prologue_env(es, xring=2, xnring=4):
            env = {}
            env["xr"] = Ring(fw, es, "xt", xring, [128, 4, D], F32, dsem=True)
            env["ssr"] = Ring(fw, es, "ss", 2, [128, 12], F32)
            env["xnr"] = Ring(fw, es, "xn", xnring, [128, D], BF16)
            env["xntr"] = Ring(fw, es, "xnT", 2, [128, 8, TS], BF16)
            env["xntb"] = [[Buf(), Buf()], [Buf(), Buf()]]
            env["ptr"] = Ring(fw, es, "ptr", 2, [128, TS], BF16, psum=True)
            return env

        def ffn_gu(l, which, xsrc, outer):
            with ExitStack() as es:
                order = [0, 5, 6, 1, 7, 2, 8, 3, 9, 4, 10]
                wt, wb = load_weights(es, "wgu", 8, 2 * DFF, lambda c0, c1: W_gu[which][l, :, c0:c1], order)
                outer["wdn"] = load_weights(None, "wdn", 22, D, lambda c0, c1: W_dn[which][l, :, c0:c1], [0, 1], wt=outer["wdn_t"])
                env = prologue_env(es)
                pa = Ring(fw, es, "pa", 3, [128, TS], F32, psum=True)
                pb = Ring(fw, es, "pb", 3, [128, TS], F32, psum=True)
                sar = Ring(fw, es, "sa", 3, [128, TS], F32)
                hr = Ring(fw, es, "hst", 4, [128, TS], BF16, dsem=True)
                gain = gains[:, 0 if which == 0 else 2, l, :]
                def body(t, pro, hook):
                    xt, bx, sx, xnT, bxt = pro
                    for j in range(22):
                        if j == 17:
                            hook()
                        A_, bA, _ = pa.next()
                        B_, bB, _ = pb.next()
                        ca = j * 128
                        cb = DFF + j * 128

                        def fmm(P_, c0):
                            for k in range(8):
                                ins = T.matmul(P_[:], lhsT=wt[:, k, c0:c0 + 128], rhs=xnT[:, k, :], start=(k == 0), stop=(k == 7))
                            return ins
                        fw.op(PE, lambda: fmm(A_, ca), bxt + [wb[ca // 512]], [bA])
                        fw.op(PE, lambda: fmm(B_, cb), bxt + [wb[cb // 512]], [bB])
                        sa, bsa, _ = sar.next()
                        fw.op(ACT, lambda: A.activation(out=sa[:], in_=A_[:], func=AF.Silu), [bA], [bsa])
                        h, bh, sh = hr.next()
                        fw.op(DVE, lambda: V.tensor_tensor(out=h[:], in0=B_[:], in1=sa[:], op=ALU.mult), [bB, bsa], [bh])
                        fw.dma(SP, sh, HT[j, :, t * TS:(t + 1) * TS], h[:], reads=[bh])
                pipelined_pro(list(range(NTI)), env, xsrc, gain, body)
                fw.barrier()

        def ffn_down(l, which, xsrc, xdst, outer):
            with ExitStack() as es:
                wt, wb = outer["wdn"]
                xr = Ring(fw, es, "xt", 2, [128, 4, D], F32, dsem=True)
                hr = Ring(fw, es, "hT", 2, [128, 22, TS], BF16, dsem=True)
                po = Ring(fw, es, "po", 3, [128, TS], F32, psum=True)
                def load(t):
                    xt, bx, sx = xr.next()
                    fw.dma(SP, sx, xt[:], xsrc[t * TS:(t + 1) * TS, :].rearrange("(s p) d -> p s d", p=128), writes=[bx])
                    hT, bh, sh = hr.next()
                    fw.dma(SP, sh, hT[:], HT[:, :, t * TS:(t + 1) * TS].rearrange("j p n -> p j n"), writes=[bh])
                    return xt, bx, sx, hT, bh, sh

                def body(t, loaded):
                    xt, bx, sx, hT, bh, sh = loaded
                    for s in range(4):
                        for hf in range(2):
                            P_, bP, _ = po.next()

                            def fmm():
                                for j in range(22):
                                    ins = T.matmul(P_[:], lhsT=hT[:, j, s * 128:(s + 1) * 128], rhs=wt[:, j, hf * 512:(hf + 1) * 512],
                                                   start=(j == 0), stop=(j == 21))
                                return ins
                            fw.op(PE, fmm, [bh, wb[hf]], [bP])
                            fw.op(DVE, lambda: V.scalar_tensor_tensor(out=xt[:, s, hf * 512:(hf + 1) * 512], in0=P_[:], scalar=0.5,
                                                                      in1=xt[:, s, hf * 512:(hf + 1) * 512], op0=ALU.mult, op1=ALU.add),
                                  [bP, bx], [bx])
                    fw.dma(SP, sx, xdst[t * TS:(t + 1) * TS, :].rearrange("(s p) d -> p s d", p=128), xt[:], reads=[bx])
                pipelined(list(range(NTI)), load, body)
                fw.barrier()

        def mixer_a(l):
            with ExitStack() as es:
                order = list(range(11))
                wt, wb = load_weights(es, "wina", 8, 5632, lambda c0, c1: W_in[l, :, c0:c1], order)
                env = prologue_env(es)
                pp = Ring(fw, es, "pp", 6, [128, TS], F32, psum=True)
                tmp = Ring(fw, es, "tmp", 2, [128, TS], F32)
                st = Ring(fw, es, "st", 4, [128, TS], BF16, dsem=True)
                stq = Ring(fw, es, "stq", 2, [128, 4, TS], BF16, dsem=True)
                big = lambda n, d=F32: es.enter_context(nc.sbuf_tensor(uniq(n), [128, 4, TS], d))
                TA, TB, TC, TE = big("TA"), big("TB"), big("TC"), big("TE")
                QS, KBF = big("QS", BF16), big("KBF", BF16)
                bTA, bTB, bTC, bTE, bQS, bKBF = Buf(), Buf(), Buf(), Buf(), Buf(), Buf()
                fl = lambda X: X[:].rearrange("p h n -> p (h n)")
                db = es.enter_context(nc.sbuf_tensor(uniq("dbx"), [128, 2, 32], F32))
                bdb = Buf()
                dst = Ring(fw, es, "dst", 2, [128, 2, 4, 8], F32, dsem=True)
                gain = gains[:, 1, l, :]
                cnt = [0]

                def evac(eng_alt, out_ap, in_ap, rd, wr, func=None):
                    cnt[0] += 1
                    if func is not None or cnt[0] % 2 == 0:
                        fw.op(ACT, lambda: A.activation(out=out_ap, in_=in_ap, func=(func or AF.Copy)), rd, wr)
                    else:
                        fw.op(DVE, lambda: V.tensor_copy(out=out_ap, in_=in_ap), rd, wr)

                def fm_chunk(xnT, bxt, col):
                    P_, bP, _ = pp.next()

                    def f():
                        for k in range(8):
                            ins = T.matmul(P_[:], lhsT=wt[:, k, col:col + 128], rhs=xnT[:, k, :], start=(k == 0), stop=(k == 7))
                        return ins
                    fw.op(PE, f, bxt + [wb[col // 512]], [bP])
                    return P_, bP

                def tm_proj(xnT, bxt, col, dst_ap_fn):
                    for s in range(4):
                        P_, bP, _ = pp.next()

                        def f():
                            for k in range(8):
                                ins = T.matmul(P_[:], lhsT=xnT[:, k, s * 128:(s + 1) * 128], rhs=wt[:, k, col:col + 512],
                                               start=(k == 0), stop=(k == 7))
                            return ins
                        fw.op(PE, f, bxt + [wb[col // 512]], [bP])
                        o, bo, so = st.next()
                        evac(None, o[:], P_[:], [bP], [bo])
                        fw.dma(SP, so, dst_ap_fn(s), o[:], reads=[bo])

                def body(t, pro, hook):
                    tok = slice(t * TS, (t + 1) * TS)
                    xt, bx, sx, xnT, bxt = pro
                    jobs = []

                    def tm_job(col, dstT, s_):
                        def f():
                            P_, bP, _ = pp.next()

                            def fmm():
                                for k in range(8):
                                    ins = T.matmul(P_[:], lhsT=xnT[:, k, s_ * 128:(s_ + 1) * 128], rhs=wt[:, k, col:col + 512],
                                                   start=(k == 0), stop=(k == 7))
                                return ins
                            fw.op(PE, fmm, bxt + [wb[col // 512]], [bP])
                            o, bo, so = st.next()
                            evac(None, o[:], P_[:], [bP], [bo])
                            fw.dma(SP, so, dstT[t * TS + s_ * 128: t * TS + (s_ + 1) * 128, :], o[:], reads=[bo])
                        return f

                    def plain_job(base, dstT, func, c):
                        def f():
                            P_, bP = fm_chunk(xnT, bxt, base + c * 128)
                            o, bo, so = st.next()
                            evac(None, o[:], P_[:], [bP], [bo], func)
                            fw.dma(SP, so, dstT[c, :, tok], o[:], reads=[bo])
                        return f

                    def conv_job(c):
                        def f():
                            Pa, bPa = fm_chunk(xnT, bxt, 2560 + c * 128)
                            Pc, bPc = fm_chunk(xnT, bxt, 3584 + c * 128)
                            ta, bta, _ = tmp.next()
                            fw.op(ACT, lambda: A.activation(out=ta[:], in_=Pa[:], func=AF.Copy), [bPa], [bta])
                            o, bo, so = st.next()
                            fw.op(DVE, lambda: V.tensor_tensor(out=o[:], in0=Pc[:], in1=ta[:], op=ALU.mult), [bPc, bta], [bo])
                            fw.dma(SP, so, ZT[c, :, tok], o[:], reads=[bo])
                        return f
                    for s_ in range(4):
                        jobs.append(tm_job(512, HV, s_))
                    for s_ in range(4):
                        jobs.append(tm_job(5120, NV, s_))
                    for (base, dstT, func) in ((4096, NQ, None), (4608, NK, None), (3072, CBT, None), (2048, HGT, AF.Silu)):
                        for c in range(4):
                            jobs.append(plain_job(base, dstT, func, c))
                    for c in range(4):
                        jobs.append(conv_job(c))

                    def pump():
                        if jobs:
                            jobs.pop(0)()
                    dt_, bdt, sdt = dst.next()
                    for h in range(4):
                        Pq, bPq = fm_chunk(xnT, bxt, h * 128)
                        fw.op(ACT, lambda: A.activation(out=QS[:, h, :], in_=Pq[:], func=AF.Copy), [bPq], [bQS])
                    for dr in range(2):
                        if dr == 1:
                            hook()
                        for h in range(4):
                            Pz, bPz = fm_chunk(xnT, bxt, 1024 + dr * 512 + h * 128)
                            fw.op(ACT, lambda: A.activation(out=TA[:, h, :], in_=Pz[:], func=AF.Sigmoid), [bPz], [bTA])
                        for h in range(4):
                            fw.op(DVE, lambda h=h: V.tensor_scalar(out=TB[:, h, :], in0=TA[:, h, :], scalar1=oml[:, dr, l, h:h + 1],
                                                                  scalar2=lb[:, dr, l, h:h + 1], op0=ALU.mult, op1=ALU.add), [bTA], [bTB])
                        fw.op(DVE, lambda: V.tensor_scalar(out=fl(TB), in0=fl(TB), scalar1=1e-30, scalar2=None, op0=ALU.max), [bTB], [bTB])
                        pump()
                        fw.op(ACT, lambda: A.activation(out=fl(TA), in_=fl(TB), func=AF.Ln), [bTB], [bTA])
                        pump()
                        fw.op(DVE, lambda: V.tensor_scalar(out=fl(TB), in0=fl(TB), scalar1=-1.0, scalar2=1.0, op0=ALU.mult, op1=ALU.add),
                              [bTB, bTA], [bTB])
                        for h in range(4):
                            fw.op(DVE, lambda h=h: V.tensor_tensor_scan(out=TC[:, h, :], data0=rmk[:], data1=TA[:, h, :], initial=0.0,
                                                                       op0=ALU.mult, op1=ALU.add), [bTA], [bTC])
                            pump()
                        c3 = lambda X: fl(X).rearrange("p (g j) -> p g j", j=64)
                        if dr == 0:
                            bt, bb = TC, bTC
                            bend = c3(TC)[:, :, 63:64]
                        else:
                            fw.op(DVE, lambda: V.tensor_tensor(out=fl(TA), in0=fl(TA), in1=fl(TC), op=ALU.subtract), [bTA, bTC], [bTA])
                            fw.op(DVE, lambda: V.tensor_tensor(out=c3(TA), in0=c3(TA), in1=c3(TC)[:, :, 63:64].to_broadcast([128, 32, 64]),
                                                               op=ALU.add), [bTA, bTC], [bTA])
                            bt, bb = TA, bTA
                            bend = c3(TA)[:, :, 0:1]
                        TX, bTX = (TA, bTA) if dr == 0 else (TC, bTC)
                        fw.op(ACT, lambda: A.activation(out=fl(TE), in_=fl(bt), func=AF.Exp), [bb], [bTE])
                        fw.op(ACT, lambda: A.activation(out=fl(TX), in_=fl(bt), func=AF.Exp, scale=-1.0), [bb], [bTX])
                        fw.op(ACT, lambda: A.activation(out=db[:, dr, :], in_=bend.rearrange("p g o -> p (g o)"), func=AF.Exp), [bb], [bdb])
                        pump()
                        o, bo, so = stq.next()
                        fw.op(DVE, lambda: V.tensor_tensor(out=fl(o), in0=fl(QS), in1=fl(TE), op=ALU.mult), [bQS, bTE], [bo])
                        fw.dma(SP, so, HQ[dr, 0, :, :, tok].rearrange("h p n -> p h n"), o[:], reads=[bo])
                        pump()
                        fw.op(DVE, lambda: V.tensor_tensor(out=fl(TX), in0=fl(TB), in1=fl(TX), op=ALU.mult), [bTB, bTX], [bTX])
                        pump()
                        o2, bo2, so2 = stq.next()
                        fw.op(ACT, lambda: A.activation(out=fl(o2), in_=fl(TX), func=AF.Copy), [bTX], [bo2])
                        fw.dma(SP, so2, HQ[dr, 1, :, :, tok].rearrange("h p n -> p h n"), o2[:], reads=[bo2])
                        fw.op(DVE, lambda: V.tensor_tensor(out=c3(KBF), in0=c3(TX), in1=db[:, dr, :].rearrange("p (g o) -> p g o", o=1).to_broadcast([128, 32, 64]),
                                                           op=ALU.mult), [bTX, bdb], [bKBF])
                        pump()
                        if dr == 0:
                            fw.op(ACT, lambda: A.activation(out=dt_[:, dr, :, :].rearrange("p h c -> p (h c)"), in_=db[:, dr, :], func=AF.Copy), [bdb], [bdt])
                        else:
                            d4 = db[:, dr, :].rearrange("p (h c) -> p h c", h=4)
                            for cc in range(8):
                                fw.op(ACT, lambda cc=cc: A.activation(out=dt_[:, dr, :, 7 - cc:8 - cc], in_=d4[:, :, cc:cc + 1], func=AF.Copy), [bdb], [bdt])
                        pump()
                        for h in range(4):
                            pt, bpt, _ = env["ptr"].next()

                            def ftr():
                                for s in range(4):
                                    ins = T.transpose(pt[:, s * 128:(s + 1) * 128], KBF[:, h, s * 128:(s + 1) * 128], idb[:])
                                return ins
                            fw.op(PE, ftr, [bKBF], [bpt])
                            o3, bo3, so3 = st.next()
                            evac(None, o3[:], pt[:], [bpt], [bo3])
                            fw.dma(SP, so3, KB[dr, h, tok, :].rearrange("(s p) k -> p s k", p=128),
                                   o3[:].rearrange("p (s k) -> p s k", k=128), reads=[bo3])
                    while jobs:
                        pump()
                    hook()
                    fw.dma(SP, sdt, DCH[0, :, :, t * 8:(t + 1) * 8], dt_[:, 0, :, :], reads=[bdt])
                    (sq0, sqn) = [(a // TS, b // TS) for (a, b) in seq_info if a // TS <= t < (a + b) // TS][0]
                    pos0 = sq0 * 8 + (sqn - 1 - (t - sq0)) * 8
                    fw.dma(SP, sdt, DCH[1, :, :, pos0:pos0 + 8], dt_[:, 1, :, :], reads=[bdt])
                pipelined_pro(list(range(NTI)), env, XS, gain, body)
                fw.barrier()

        def mixer_h(l):
            TM = max(SEQS)
            NCM = TM // 64
            NSM = TM // 128
            with ExitStack() as es:
                qr = Ring(fw, es, "hq_", 2, [128, TM], BF16, dsem=True)
                kr = Ring(fw, es, "hk_", 2, [128, TM], BF16, dsem=True)
                kbr = Ring(fw, es, "kb_", 2, [128, 2, NSM, 128], BF16, dsem=True)
                for i_ in range(2):
                    fw.op(POOL, lambda i_=i_: G.memset(kbr.t[i_][:], 0.0), [], [kbr.b[i_]])
                vr = Ring(fw, es, "hv_", 2, [128, NSM, 128], BF16, dsem=True)
                dcr = Ring(fw, es, "dc_", 2, [128, NCM], F32, dsem=True)
                dxr = Ring(fw, es, "dx_", 2, [128, 16 * NCM], F32)
                hgr = Ring(fw, es, "hg_", 2, [128, 512], BF16, dsem=True)
                ohr = Ring(fw, es, "oh_", 2, [128, 512], BF16, dsem=True)
                Uall = es.enter_context(nc.sbuf_tensor(uniq("Uall"), [128, 128 * NCM], F32))
                Sall = es.enter_context(nc.sbuf_tensor(uniq("Sall"), [128, 128 * NCM], BF16))
                of = es.enter_context(nc.sbuf_tensor(uniq("ofa"), [128, TM], F32))
                atrr = Ring(fw, es, "atraw", 2, [128, 4, 128], BF16)
                atm = es.enter_context(nc.sbuf_tensor(uniq("atm"), [128, NSM, 128], BF16))
                mkb = es.enter_context(nc.sbuf_tensor(uniq("mkb"), [128, 2, 128], BF16))
                bmk = Buf()
                fw.op(DVE, lambda: V.tensor_copy(out=mkb[:, 0, :], in_=maskf[:]), [], [bmk])
                fw.op(DVE, lambda: V.tensor_copy(out=mkb[:, 1, :], in_=maskb[:]), [bmk], [bmk])
                sqr = Ring(fw, es, "sq", 2, [128, 512], BF16)
                rsr = Ring(fw, es, "rs", 2, [128, 512], F32)
                pU = Ring(fw, es, "pU", 3, [128, 4, 128], F32, psum=True)
                pA = Ring(fw, es, "pA", 2, [128, 4, 128], F32, psum=True)
                pO = Ring(fw, es, "pO", 2, [128, 4, 128], F32, psum=True)
                pN = Ring(fw, es, "pN", 1, [128, 512], F32, psum=True)
                bU = [Buf() for _ in range(NCM // 4)]
                bS = [Buf() for _ in range(8)]
                bAr = [Buf() for _ in range(NSM // 4)]
                bAm = [Buf() for _ in range(NSM // 4)]
                bof = [Buf() for _ in range(TM // 512)]
                ev = [0]

                def load_unit(T0, TL, h, dr):
                    NC = TL // 64
                    NS = TL // 128
                    qh, bq, sq_ = qr.next()
                    fw.dma(SP, sq_, qh[:, 0:TL], HQ[dr, 0, h, :, T0:T0 + TL], writes=[bq])
                    kh, bk, sk_ = kr.next()
                    fw.dma(SP, sk_, kh[:, 0:TL], HQ[dr, 1, h, :, T0:T0 + TL], writes=[bk])
                    kb, bkb, skb = kbr.next()
                    kbv = KB[dr, h, T0:T0 + TL, :].rearrange("(s p) k -> p s k", p=128)
                    fw.dma_group(SP, skb, [(kb[0:64, 0, 0:NS, :], kbv[0:64]), (kb[64:128, 1, 0:NS, :], kbv[64:128])], writes=[bkb])
                    dc, bdc, sdc = dcr.next()
                    fw.dma(SP, sdc, dc[:, 0:NC], DCH[dr, :, h, T0 // 64:T0 // 64 + NC], writes=[bdc])
                    return (qh, bq, kh, bk, kb, bkb, dc, bdc)

                units = [(T0, TL, h, dr) for (T0, TL) in seq_info for h in range(4) for dr in range(2)]
                nxt = load_unit(*units[0])
                for ui, (T0, TL, h, dr) in enumerate(units):
                    NC = TL // 64
                    NS = TL // 128
                    (qh, bq, kh, bk, kb, bkb, dc, bdc) = nxt
                    if dr == 0:
                        vt, bv, sv = vr.next()
                        fw.dma(SP, sv, vt[:, 0:NS, :], HV[T0:T0 + TL, h * 128:(h + 1) * 128].rearrange("(s p) v -> p s v", p=128), writes=[bv])
                    if ui + 1 < len(units):
                        nxt = load_unit(*units[ui + 1])
                    Uv = Uall[:, 0:128 * NC].rearrange("p (v c) -> p v c", c=NC)
                    Sv = Sall[:, 0:128 * NC].rearrange("p (v c) -> p v c", c=NC)
                    dx, bdx, _ = dxr.next()
                    fw.op(DVE, lambda: V.memset(dc[:, 0:1], 0.0), [bdc], [bdc])
                    fw.op(DVE, lambda: V.tensor_copy(out=dx[:, 0:16 * NC].rearrange("p (v c) -> p v c", c=NC),
                                                     in_=dc[:, 0:NC].rearrange("p (o c) -> p o c", o=1).to_broadcast([128, 16, NC])), [bdc], [bdx])
                    import os as _os
                    KH = int(_os.environ.get("KH", "9"))
                    for g in range(NC // 4 if KH >= 1 else 0):
                        PU, bPU, _ = pU.next()

                        def fU():
                            for j in range(4):
                                pos = 4 * g + j
                                c = pos if dr == 0 else NC - 1 - pos
                                s_, cc = c // 2, c % 2
                                ins = T.matmul(PU[:, j, :], lhsT=kb[:, cc, s_, :], rhs=vt[:, s_, :], start=True, stop=True)
                            return ins
                        KM = int(_os.environ.get("KM", "3"))
                        if KM & 1:
                            fw.op(PE, fU, [bkb, bv], [bPU])
                        ev[0] += 1
                        if not (KM & 2):
                            continue
                        dstU = Uv[:, :, 4 * g:4 * g + 4].rearrange("p v c -> p c v")
                        if int(_os.environ.get("KE", "2")) == 3:
                            dstU = Uall[:, 0:512].rearrange("p (c v) -> p c v", c=4)
                        KE = int(_os.environ.get("KE", "1"))
                        if (ev[0] % 2 == 0 and KE == 2) or KE == 1:
                            fw.op(ACT, lambda: A.activation(out=dstU, in_=PU[:], func=AF.Copy), [bPU], [bU[g]])
                        else:
                            fw.op(DVE, lambda: V.tensor_copy(out=dstU, in_=PU[:]), [bPU], [bU[g]])
                    import os as _os
                    KH = int(_os.environ.get("KH", "9"))
                    for vg in range(8 if KH >= 2 else 0):
                        fw.op(DVE, lambda vg=vg: V.tensor_tensor_scan(out=Sv[:, vg * 16:(vg + 1) * 16, :].rearrange("p v c -> p (v c)"),
                                                                     data0=dx[:, 0:16 * NC],
                                                                     data1=Uv[:, vg * 16:(vg + 1) * 16, :].rearrange("p v c -> p (v c)"),
                                                                     initial=0.0, op0=ALU.mult, op1=ALU.add),
                              bU[0:NC // 4] + [bdx], [bS[vg]])
                    for gp in range(NS // 4 if KH >= 3 else 0):
                        PA, bPA, _ = pA.next()

                        def fA():
                            for j in range(4):
                                s_ = gp * 4 + j
                                ins = T.matmul(PA[:, j, :], lhsT=kh[:, s_ * 128:(s_ + 1) * 128], rhs=qh[:, s_ * 128:(s_ + 1) * 128],
                                               start=True, stop=True)
                            return ins
                        fw.op(PE, fA, [bk, bq], [bPA])
                        atr_, bar_, _ = atrr.next()
                        fw.op(ACT, lambda: A.activation(out=atr_[:], in_=PA[:], func=AF.Copy), [bPA], [bar_])
                        fw.op(POOL, lambda: G.tensor_tensor(out=atm[:, gp * 4:gp * 4 + 4, :], in0=atr_[:],
                                                            in1=mkb[:, dr:dr + 1, :].to_broadcast([128, 4, 128]), op=ALU.mult),
                              [bar_, bmk], [bAm[gp]])
                    for gp in range(NS // 4 if KH >= 4 else 0):
                        PO, bPO, _ = pO.next()

                        def fO():
                            first = True
                            for j in range(4):
                                s_ = gp * 4 + j
                                for cc in range(2):
                                    c = 2 * s_ + cc
                                    pos = c if dr == 0 else NC - 1 - c
                                    T.matmul(PO[:, j, cc * 64:(cc + 1) * 64], lhsT=(Sv[:, :, pos - 1] if pos > 0 else zeros_b[:]),
                                             rhs=qh[:, s_ * 128 + cc * 64:s_ * 128 + (cc + 1) * 64], start=first, stop=False)
                                    first = False
                            for j in range(4):
                                s_ = gp * 4 + j
                                ins = T.matmul(PO[:, j, :], lhsT=vt[:, s_, :], rhs=atm[:, s_, :], start=first, stop=(j == 3))
                                first = False
                            return ins
                        fw.op(PE, fO, bS + [bq, bv, bAm[gp]], [bPO])
                        ofs = of[:, gp * 512:(gp + 1) * 512]
                        if dr == 0:
                            fw.op(ACT, lambda: A.activation(out=ofs, in_=PO[:].rearrange("p j t -> p (j t)"), func=AF.Copy), [bPO], [bof[gp]])
                        else:
                            fw.op(DVE, lambda: V.tensor_tensor(out=ofs, in0=PO[:].rearrange("p j t -> p (j t)"), in1=ofs, op=ALU.add),
                                  [bPO, bof[gp]], [bof[gp]])
                    if dr == 1 and KH >= 5:
                        for gp in range(NS // 4):
                            cs = slice(gp * 512, (gp + 1) * 512)
                            oh, boh, soh = ohr.next()
                            hg, bhg, shg = hgr.next()
                            fw.dma(SP, shg, hg[:], HGT[h, :, T0 + gp * 512:T0 + (gp + 1) * 512], writes=[bhg])
                            sq, bsq, _ = sqr.next()
                            fw.op(ACT, lambda: A.activation(out=sq[:], in_=of[:, cs], func=AF.Square), [bof[gp]], [bsq])
                            PN, bPN, _ = pN.next()
                            fw.op(PE, lambda: T.matmul(PN[:], lhsT=ones_b[:], rhs=sq[:], start=True, stop=True), [bsq], [bPN])
                            rs, brs, _ = rsr.next()
                            fw.op(ACT, lambda: A.activation(out=rs[:], in_=PN[:], func=AF.Ln, scale=1.0 / 128, bias=EPS), [bPN], [brs])
                            fw.op(ACT, lambda: A.activation(out=rs[:], in_=rs[:], func=AF.Exp, scale=-0.5), [brs], [brs])
                            fw.op(DVE, lambda: V.tensor_tensor(out=rs[:], in0=of[:, cs], in1=rs[:], op=ALU.mult), [bof[gp], brs], [brs])
                            fw.op(DVE, lambda: V.scalar_tensor_tensor(out=oh[:], in0=rs[:], scalar=gn[:, l, h:h + 1], in1=hg[:],
                                                                      op0=ALU.mult, op1=ALU.mult), [brs, bhg], [boh])
                            fw.dma(SP, soh, OHG[h, :, T0 + gp * 512:T0 + (gp + 1) * 512], oh[:], reads=[boh])
                fw.barrier()

        def mixer_n(l):
            TM = max(SEQS)
            with ExitStack() as es:
                qT = es.enter_context(nc.sbuf_tensor(uniq("naq"), [128, 4, TM], BF16))
                kT = es.enter_context(nc.sbuf_tensor(uniq("nak"), [128, 4, TM], BF16))
                vx = es.enter_context(nc.sbuf_tensor(uniq("nav"), [128, TM // 128, 8, 65], BF16))
                bmc = es.enter_context(nc.sbuf_tensor(uniq("bmc"), [128, 8, 5, 5, 128], BF16))
                bq, bk, bv, bbm, bb2 = Buf(), Buf(), Buf(), Buf(), Buf()
                sl = [fw.new_dsem() for _ in range(4)]
                with ExitStack() as es2:
                    b2 = es2.enter_context(nc.sbuf_tensor(uniq("b2s"), [128, 8, 9, 128], F32))
                    ng = es2.enter_context(nc.sbuf_tensor(uniq("ngs"), [128, 5, 5, 128], F32))
                    fw.dma_group(SP, sl[3], [(b2[:], b2g_in[l]), (ng[:], neg_in)], writes=[bb2])
                    for c in range(5):
                        fw.op(DVE, lambda c=c: V.tensor_tensor(out=bmc[:, :, c, :, :], in0=b2[:, :, 4 - c:9 - c, :],
                                                               in1=ng[:, c:c + 1, :, :].to_broadcast([128, 8, 5, 128]), op=ALU.add), [bb2], [bbm])
                    fw.barrier()
                fw.op(POOL, lambda: G.memset(vx[:, :, :, 64:65], 1.0), [], [bv])
                vtr = Ring(fw, es, "vtmp", 2, [128, 8, 512], BF16, dsem=True)
                pS = Ring(fw, es, "pS", 2, [128, 1024], F32, psum=True)
                pO = Ring(fw, es, "pO", 2, [128, 512], F32, psum=True)
                ptr = Ring(fw, es, "ptr", 1, [128, TS], BF16, psum=True)
                tr = Ring(fw, es, "nt", 3, [128, 640], F32)
                pr = Ring(fw, es, "np", 4, [128, 5, 128], BF16)
                rr = Ring(fw, es, "nr", 4, [128, 1], F32)
                otr = Ring(fw, es, "not", 2, [128, 512], BF16)
                osr = Ring(fw, es, "nos", 2, [128, 4, 128], BF16, dsem=True)
                for (T0, TL) in seq_info:
                    rows = TL // 64
                    fw.dma(SP, sl[0], qT[:, :, 0:TL], NQ[:, :, T0:T0 + TL].rearrange("c p n -> p c n"), writes=[bq])
                    fw.dma(SP, sl[1], kT[:, :, 0:TL], NK[:, :, T0:T0 + TL].rearrange("c p n -> p c n"), writes=[bk])
                    nb = TL // 128
                    for b0 in range(0, nb, 8):
                        vtm_, bvt, svt = vtr.next()
                        fw.dma(SP, svt, vtm_[:], NV[T0 + b0 * 128:T0 + (b0 + 8) * 128, :].rearrange("(b p) n -> p b n", p=128), writes=[bvt])
                        fw.op(POOL, lambda: G.tensor_copy(out=vx[:, b0:b0 + 8, :, 0:64], in_=vtm_[:].rearrange("p b (h d) -> p b h d", d=64)),
                              [bvt], [bv])
                    items = [(rp, h) for rp in range(rows // 2) for h in range(8)]

                    def geom(rp):
                        r = 2 * rp
                        re_ = min(max(r - 4, 0), rows - 10)
                        cls = {0: 0, -2: 1, -4: 2, -6: 3, -8: 4}[re_ - r]
                        return r, re_, cls

                    def stage_a(rp, h):
                        r, re_, cls = geom(rp)
                        c = h // 2
                        pp_ = slice((h % 2) * 64, (h % 2) * 64 + 64)
                        PS_, bPS, _ = pS.next()

                        def fS():
                            for j in range(5):
                                ins = T.matmul(PS_[:, j * 128:(j + 1) * 128], lhsT=kT[pp_, c, (re_ + 2 * j) * 64:(re_ + 2 * j) * 64 + 128],
                                               rhs=qT[pp_, c, r * 64:r * 64 + 128], start=True, stop=True)
                            return ins
                        fw.op(PE, fS, [bq, bk], [bPS])
                        tt, btt, _ = tr.next()
                        fw.op(DVE, lambda: V.scalar_tensor_tensor(out=tt[:], in0=PS_[:, 0:640], scalar=0.125,
                                                                  in1=bmc[:, h, cls, :, :].rearrange("p j q -> p (j q)"),
                                                                  op0=ALU.mult, op1=ALU.add), [bPS, bbm], [btt])
                        pt_, bpt_, _ = pr.next()
                        fw.op(ACT, lambda: A.activation(out=pt_[:].rearrange("p j q -> p (j q)"), in_=tt[:], func=AF.Exp), [btt], [bpt_])
                        return pt_, bpt_

                    cur_ot = [None]

                    def stage_b(rp, h, st_):
                        r, re_, cls = geom(rp)
                        pt_, bpt_ = st_
                        if h == 0:
                            cur_ot[0] = otr.next()
                        ot, bot, _ = cur_ot[0]
                        PO_, bPO, _ = pO.next()

                        def fO():
                            for j in range(5):
                                ins = T.matmul(PO_[:, 0:65], lhsT=pt_[:, j, :], rhs=vx[:, re_ // 2 + j, h, :], start=(j == 0), stop=(j == 4))
                            return ins
                        fw.op(PE, fO, [bpt_, bv], [bPO])
                        rc, brc, _ = rr.next()
                        fw.op(DVE, lambda: V.reciprocal(out=rc[:], in_=PO_[:, 64:65]), [bPO], [brc])
                        fw.op(ACT, lambda: A.activation(out=ot[:, h * 64:(h + 1) * 64], in_=PO_[:, 0:64], func=AF.Copy, scale=rc[:, 0:1]),
                              [bPO, brc], [bot])
                        if h == 7:
                            pt, bpt, _ = ptr.next()

                            def ftr():
                                for c in range(4):
                                    ins = T.transpose(pt[:, c * 128:(c + 1) * 128], ot[:, c * 128:(c + 1) * 128], idb[:])
                                return ins
                            fw.op(PE, ftr, [bot], [bpt])
                            os_, bos, sos = osr.next()
                            fw.op(DVE, lambda: V.tensor_copy(out=os_[:].rearrange("p c t -> p (c t)"), in_=pt[:]), [bpt], [bos])
                            fw.dma(SP, sos, ONA[:, :, T0 + r * 64:T0 + r * 64 + 128].rearrange("c p n -> p c n"), os_[:], reads=[bos])

                    pend = [stage_a(*items[0])]
                    if len(items) > 1:
                        pend.append(stage_a(*items[1]))
                    for i, (rp, h) in enumerate(items):
                        if i + 2 < len(items):
                            pend.append(stage_a(*items[i + 2]))
                        stage_b(rp, h, pend.pop(0))
                fw.barrier()

        def mixer_o(l):
            with ExitStack() as es:
                wm = []
                for m in range(3):
                    wm.append(load_weights(es, "wm%d" % m, 4, D, lambda c0, c1, m=m: W_mo[m][l, :, c0:c1], [0, 1]))
                wg, wgb = load_weights(es, "wg", 8, 3072, lambda c0, c1: W_in[l, :, 5632 + c0:5632 + c1], list(range(6)))
                wo, wob = load_weights(es, "wo", 8, D, lambda c0, c1: W_out[l, :, c0:c1], [0, 1])
                env = prologue_env(es)
                inr = [Ring(fw, es, "oh_", 2, [128, 4, TS], BF16, dsem=True), None, Ring(fw, es, "on_", 2, [128, 4, TS], BF16, dsem=True)]
                zr = Ring(fw, es, "z_", 1, [128, 4, TS + 2], BF16, dsem=True)
                cbr = Ring(fw, es, "cb_", 1, [128, 4, TS], BF16, dsem=True)
                ocr = Ring(fw, es, "ocv", 2, [128, 4, TS], BF16)
                pre = {}
                mTr = Ring(fw, es, "mT", 1, [128, 8, TS], BF16)
                tmp = Ring(fw, es, "tm", 5, [128, TS], F32)
                pY = Ring(fw, es, "pY", 6, [128, TS], F32, psum=True)
                pG = pY
                pX = pY
                gain = gains[:, 1, l, :]
                def pre_tile(t):
                    tok = slice(t * TS, (t + 1) * TS)
                    oh, boh, soh = inr[0].next()
                    fw.dma(SP, soh, oh[:], OHG[:, :, tok].rearrange("c p n -> p c n"), writes=[boh])
                    on, bon, son = inr[2].next()
                    fw.dma(SP, son, on[:], ONA[:, :, tok].rearrange("c p n -> p c n"), writes=[bon])
                    z, bz, sz = zr.next()
                    pairs = [(z[:, :, 1:TS + 1], ZT[:, :, tok].rearrange("c p n -> p c n"))]
                    lo = t not in seq_start_tiles
                    hi = t not in seq_end_tiles
                    if lo:
                        pairs.append((z[:, :, 0:1], ZT[:, :, t * TS - 1:t * TS].rearrange("c p n -> p c n")))
                    if hi:
                        pairs.append((z[:, :, TS + 1:TS + 2], ZT[:, :, (t + 1) * TS:(t + 1) * TS + 1].rearrange("c p n -> p c n")))
                    fw.dma_group(SP, sz, pairs, writes=[bz], slow=True)
                    if not lo:
                        fw.op(POOL, lambda: G.memset(z[:, :, 0:1], 0.0), [bz], [bz])
                    if not hi:
                        fw.op(POOL, lambda: G.memset(z[:, :, TS + 1:TS + 2], 0.0), [bz], [bz])
                    cb_, bcb, scb = cbr.next()
                    fw.dma(SP, scb, cb_[:], CBT[:, :, tok].rearrange("c p n -> p c n"), writes=[bcb])
                    oc_, boc, _ = ocr.next()
                    for c in range(4):
                        t1, b1, _ = tmp.next()
                        fw.op(DVE, lambda: V.tensor_scalar(out=t1[:], in0=z[:, c, 1:TS + 1], scalar1=cw[:, l, 1, c:c + 1], scalar2=cbias[:, l, c:c + 1],
                                                           op0=ALU.mult, op1=ALU.add), [bz], [b1])
                        fw.op(DVE, lambda: V.scalar_tensor_tensor(out=t1[:], in0=z[:, c, 0:TS], scalar=cw[:, l, 0, c:c + 1], in1=t1[:],
                                                                  op0=ALU.mult, op1=ALU.add), [bz, b1], [b1])
                        fw.op(DVE, lambda: V.scalar_tensor_tensor(out=t1[:], in0=z[:, c, 2:TS + 2], scalar=cw[:, l, 2, c:c + 1], in1=t1[:],
                                                                  op0=ALU.mult, op1=ALU.add), [bz, b1], [b1])
                        fw.op(DVE, lambda: V.tensor_tensor(out=oc_[:, c, :], in0=t1[:], in1=cb_[:, c, :], op=ALU.mult), [b1, bcb], [boc])
                    pre[t] = (oh, boh, oc_, boc, on, bon)

                def body(t, pro, hook):
                    tok = slice(t * TS, (t + 1) * TS)
                    xt, bx, sx, xnT, bxt = pro
                    if t not in pre:
                        pre_tile(t)
                    (oh, boh, oc_, boc, on, bon) = pre.pop(t)
                    srcs = [(oh, boh), (oc_, boc), (on, bon)]
                    mT, bmT, _ = mTr.next()
                    for oc in range(8):
                        if oc == 4 and t + 1 < NTI:
                            pre_tile(t + 1)
                        if oc == 6:
                            hook()
                        macc, bmacc, _ = tmp.next()
                        for m in range(3):
                            PY, bPY, _ = pY.next()
                            PG, bPG, _ = pG.next()
                            src, bsrc = srcs[m]
                            wmt, wmb = wm[m]

                            def fY():
                                for k in range(4):
                                    ins = T.matmul(PY[:], lhsT=wmt[:, k, oc * 128:(oc + 1) * 128], rhs=src[:, k, :], start=(k == 0), stop=(k == 3))
                                return ins
                            fw.op(PE, fY, [bsrc, wmb[oc // 4]], [bPY])
                            gc = m * 1024 + oc * 128

                            def fG():
                                for k in range(8):
                                    ins = T.matmul(PG[:], lhsT=wg[:, k, gc:gc + 128], rhs=xnT[:, k, :], start=(k == 0), stop=(k == 7))
                                return ins
                            fw.op(PE, fG, bxt + [wgb[gc // 512]], [bPG])
                            sg, bsg, _ = tmp.next()
                            fw.op(ACT, lambda: A.activation(out=sg[:], in_=PG[:], func=AF.Sigmoid), [bPG], [bsg])
                            if m == 0:
                                fw.op(DVE, lambda: V.tensor_tensor(out=macc[:], in0=PY[:], in1=sg[:], op=ALU.mult), [bPY, bsg], [bmacc])
                            else:
                                fw.op(DVE, lambda: V.tensor_tensor(out=sg[:], in0=PY[:], in1=sg[:], op=ALU.mult), [bPY, bsg], [bsg])
                                if m == 1:
                                    fw.op(POOL, lambda: G.tensor_tensor(out=macc[:], in0=macc[:], in1=sg[:], op=ALU.add), [bsg, bmacc], [bmacc])
                                else:
                                    fw.op(POOL, lambda: G.tensor_tensor(out=mT[:, oc, :], in0=macc[:], in1=sg[:], op=ALU.add), [bsg, bmacc], [bmT])
                    for s in range(4):
                        for hf in range(2):
                            PX, bPX, _ = pX.next()

                            def fX():
                                for k in range(8):
                                    ins = T.matmul(PX[:], lhsT=mT[:, k, s * 128:(s + 1) * 128], rhs=wo[:, k, hf * 512:(hf + 1) * 512],
                                                   start=(k == 0), stop=(k == 7))
                                return ins
                            fw.op(PE, fX, [bmT, wob[hf]], [bPX])
                            fw.op(DVE, lambda: V.tensor_tensor(out=xt[:, s, hf * 512:(hf + 1) * 512], in0=PX[:], in1=xt[:, s, hf * 512:(hf + 1) * 512],
                                                               op=ALU.add), [bPX, bx], [bx])
                    fw.dma(SP, sx, XS[tok, :].rearrange("(s p) d -> p s d", p=128), xt[:], reads=[bx])
                pipelined_pro(list(range(NTI)), env, XS, gain, body)
                fw.barrier()

        def final_norm():
            with ExitStack() as es:
                gf = es.enter_context(nc.sbuf_tensor(uniq("gf"), [128, D], F32))
                bgf = Buf()
                fw.dma(SP, fw.new_dsem(), gf[:], gfin_in, writes=[bgf])
                xr = Ring(fw, es, "xt", 2, [128, 4, D], F32, dsem=True)
                ssr = Ring(fw, es, "ss", 2, [128, 12], F32)
                jk = es.enter_context(nc.sbuf_tensor(uniq("junk"), [128, D], BF16))
                bjk = Buf()
                for t in range(NTI):
                    tok = slice(t * TS, (t + 1) * TS)
                    xt, bx, sx = xr.next()
                    fw.dma(SP, sx, xt[:], XS[tok, :].rearrange("(s p) d -> p s d", p=128), writes=[bx])
                    ss, bss, _ = ssr.next()
                    fw.op(DVE, lambda: V.memset(ss[:], 0.0), [], [bss])
                    for s in range(4):
                        fw.op(ACT, lambda s=s: A.activation(out=jk[:], in_=xt[:, s, :], func=AF.Square, accum_out=ss[:, s:s + 1]), [bx], [bjk, bss])
                    fw.op(ACT, lambda: A.activation(out=ss[:, 4:8], in_=ss[:, 0:4], func=AF.Ln, scale=1.0 / D, bias=EPS), [bss], [bss])
                    fw.op(ACT, lambda: A.activation(out=ss[:, 8:12], in_=ss[:, 4:8], func=AF.Exp, scale=-0.5), [bss], [bss])
                    for s in range(4):
                        fw.op(DVE, lambda s=s: V.scalar_tensor_tensor(out=xt[:, s, :], in0=xt[:, s, :], scalar=ss[:, 8 + s:9 + s], in1=gf[:],
                                                                     op0=ALU.mult, op1=ALU.mult), [bx, bss, bgf], [bx])
                    fw.dma(SP, sx, y_out[tok, :].rearrange("(s p) d -> p s d", p=128), xt[:], reads=[bx])
                fw.barrier()

        import os as _os
        kstop = int(_os.environ.get("KSTOP", "99"))
        phases = []
        for l in range(L):
            def ffn(l, which, xsrc):
                with ExitStack() as oes:
                    outer = {"es": oes, "wdn_t": oes.enter_context(nc.sbuf_tensor(uniq("wdn"), [128, 22, D], BF16))}
                    ffn_gu(l, which, xsrc, outer)
                    ffn_down(l, which, xsrc, XS, outer)
            phases.append(lambda l=l: ffn(l, 0, x_in if l == 0 else XS))
            phases.append(lambda l=l: mixer_a(l))
            phases.append(lambda l=l: mixer_h(l))
            phases.append(lambda l=l: mixer_n(l))
            phases.append(lambda l=l: mixer_o(l))
            phases.append(lambda l=l: ffn(l, 1, XS))
        phases.append(final_norm)
        for i, ph in enumerate(phases):
            if i < kstop:
                ph()
    return nc


def host_consts(L, na_rpb):
    c = {}
    c["ident"] = np.eye(128, dtype=np.float32)
    s = np.arange(128)[:, None]
    t = np.arange(128)[None, :]
    same = (s // 64) == (t // 64)
    c["maskf"] = (same & (s <= t)).astype(np.float32)
    c["maskb"] = (same & (s >= t)).astype(np.float32)
    rm = np.ones((128, TS), np.float32)
    rm[:, ::64] = 0.0
    c["rmask"] = rm
    kp = (np.arange(128) // 64)[:, None]
    kc = (np.arange(128) % 64)[:, None]
    qp = (np.arange(128) // 64)[None, :]
    qc = (np.arange(128) % 64)[None, :]
    dc = np.clip(kc - qc + 15, 0, 30)
    b2g = np.zeros((L, 128, 8, 9, 128), np.float32)
    for pl in range(9):
        base = 2 * pl - 1
        drr = np.clip(base + kp - qp, 0, 14)
        b2g[:, :, :, pl, :] = np.transpose(na_rpb[:, :, drr, dc], (0, 2, 1, 3))
    c["b2g"] = b2g
    cs = np.clip(qc - 8, 0, 48)
    col_ok = (kc >= cs) & (kc < cs + 16)
    neg = np.zeros((128, 5, 5, 128), np.float32)
    for cls in range(5):
        delta = -2 * cls
        for j in range(5):
            base = delta + 2 * j + 7
            dr = base + kp - qp
            if cls == 0:
                lo = 7 - qp
                hi = 14 - qp
            elif cls == 1:
                lo = 7 - (2 + qp)
                hi = 14 - (2 + qp)
            elif cls == 2:
                lo = 3 + 0 * qp
                hi = 10 + 0 * qp
            elif cls == 3:
                lo = (4 - qp) - 1
                hi = (4 - qp) + 6
            else:
                lo = (2 - qp) - 1
                hi = (2 - qp) + 6
            ok = col_ok & (dr >= lo) & (dr <= hi)
            neg[:, cls, j, :] = np.where(ok, 0.0, -1e30)
    c["negm"] = neg
    return c


def fm(a, L):
    return np.ascontiguousarray(a.reshape(L, -1, 128).transpose(2, 0, 1))


def make_inputs(inp, L):
    sh = {}
    for k in ("ffn1_w_gu", "ffn2_w_gu", "ffn1_w_down", "ffn2_w_down", "w_in", "w_hg_out", "w_cv_out", "w_na_out", "w_out"):
        sh[k] = np.ascontiguousarray(inp[k], dtype=np.float32)
    g = np.stack([fm(inp["ffn1_norm"], L), fm(inp["mix_norm"], L), fm(inp["ffn2_norm"], L)], axis=1)
    sh["gains"] = np.ascontiguousarray(g, dtype=np.float32)
    sh["gfin"] = np.ascontiguousarray(np.broadcast_to(inp["final_norm"][None, :], (128, D)), dtype=np.float32)
    lbl = inp["hg_lb_logits"].reshape(2, L, 4, 128).transpose(3, 0, 1, 2)
    sh["lbl"] = np.ascontiguousarray(lbl, dtype=np.float32)
    sh["gn"] = fm(inp["hg_out_norm"], L).astype(np.float32)
    cwt = inp["conv_w"].reshape(L, 3, 4, 128).transpose(3, 0, 1, 2)
    sh["cw"] = np.ascontiguousarray(cwt, dtype=np.float32)
    sh["cbias"] = fm(inp["conv_b"], L).astype(np.float32)
    sh.update(host_consts(L, np.asarray(inp["na_rpb"], dtype=np.float32)))
    return sh


_CACHE = {}


def kernel(**inputs):
    inp = {k: np.asarray(v) for k, v in inputs.items()}
    L = 4
    SEQS = [2048, 2048, 4096]
    key = ("full",)
    if key not in _CACHE:
        _CACHE[key] = build(SEQS, L)
    nc = _CACHE[key]
    sh = make_inputs(inp, L)
    xp = inp["x_prompt"]
    xs = inp["x_sample"]
    in_maps = []
    for c in range(8):
        xc = np.concatenate([xp[2 * c], xp[2 * c + 1], xs[c]], axis=0).astype(np.float32)
        m = dict(sh)
        m["x"] = np.ascontiguousarray(xc)
        in_maps.append(m)
    res = run_bass_kernel_spmd(nc, in_maps, core_ids=list(range(8)))
    yp = np.zeros((16, 2048, D), np.float32)
    ys = np.zeros((8, 4096, D), np.float32)
    for c in range(8):
        y = res.results[c]["y"]
        yp[2 * c] = y[0:2048]
        yp[2 * c + 1] = y[2048:4096]
        ys[c] = y[4096:8192]
    return (yp, ys)
```
